# Optimizing a Trainium2 kernel written in Bass

```python
import math
import jax, jax.numpy as jnp
from jax import lax
import numpy as np

D_MODEL = 2048
BATCH = 4
SEQ = 2048
DEPTH = 1
DEC_BATCH = 128
DEC_SEQ = 4
PAST_LEN = 16384
PAGE_SIZE = 128

SSM_EXPAND = 2
SSM_INNER = SSM_EXPAND * D_MODEL
SSM_HEADDIM = 64
SSM_HEADS = SSM_INNER // SSM_HEADDIM
SSM_GROUPS = 8
SSM_HPG = SSM_HEADS // SSM_GROUPS
SSM_STATE = 128
CONV_W = 4
CONV_DIM = SSM_INNER + 2 * SSM_GROUPS * SSM_STATE
RET_HEADS = 8
RET_QK = D_MODEL
RET_DK = RET_QK // RET_HEADS
RET_V = 2 * D_MODEL
RET_DV = RET_V // RET_HEADS
ROPE_BASE = 10000.0
CHUNK = 128
N_BRANCH = 2
EPS = 1e-6
IN_SIZES = (SSM_INNER, CONV_DIM, SSM_HEADS, RET_QK, RET_QK, RET_V, RET_V, N_BRANCH * D_MODEL)
IN_DIM = SSM_INNER + CONV_DIM + SSM_HEADS + 2 * RET_QK + 2 * RET_V + N_BRANCH * D_MODEL

kernel_name = "hybrid_ssd_retention_gated_merge_step"


def _rms(x):
    xf = x.astype(jnp.float32)
    return (xf * lax.rsqrt(jnp.mean(xf * xf, axis=-1, keepdims=True) + EPS)).astype(x.dtype)


def _rmsnorm(x, w):
    return _rms(x) * w.astype(x.dtype)


def _split_cols(t, sizes):
    offs = np.cumsum(np.array(sizes))[:-1].tolist()
    return jnp.split(t, offs, axis=-1)


def _rope(x, pos):
    half = x.shape[-1] // 2
    inv = 1.0 / (ROPE_BASE ** (jnp.arange(half, dtype=jnp.float32) / half))
    ang = pos.astype(jnp.float32)[:, None] * inv[None, :]
    cos = jnp.cos(ang)[None, :, None, :].astype(x.dtype)
    sin = jnp.sin(ang)[None, :, None, :].astype(x.dtype)
    x1, x2 = x[..., :half], x[..., half:]
    return jnp.concatenate([x1 * cos - x2 * sin, x2 * cos + x1 * sin], axis=-1)


def chunked_decay_scan(q, k, v, log_a, s0):
    b, L, G, N = q.shape
    R, P = v.shape[3], v.shape[4]
    Q = min(CHUNK, L)
    nc = -(-L // Q)
    pad = nc * Q - L
    cdt = v.dtype
    log_a = log_a.astype(jnp.float32)
    if pad:
        padf = lambda t: jnp.pad(t, [(0, 0), (0, pad)] + [(0, 0)] * (t.ndim - 2))
        q, k, v, log_a = padf(q), padf(k), padf(v), padf(log_a)
    q = q.reshape(b, nc, Q, G, N)
    k = k.reshape(b, nc, Q, G, N)
    v = v.reshape(b, nc, Q, G, R, P)
    log_a = log_a.reshape(b, nc, Q, G, R)
    a_cs = jnp.cumsum(log_a, axis=2)
    causal = jnp.tril(jnp.ones((Q, Q), dtype=bool))[None, None, :, :, None, None]
    seg = jnp.exp(jnp.where(causal, a_cs[:, :, :, None] - a_cs[:, :, None, :], -jnp.inf)).astype(cdt)
    qk = jnp.einsum('bclgn,bcsgn->bclsg', q, k)
    y_intra = jnp.einsum('bclsgr,bcsgrp->bclgrp', seg * qk[..., None], v)
    decay_end = jnp.exp(a_cs[:, :, -1:] - a_cs).astype(cdt)
    chunk_states = jnp.einsum('bclgn,bclgrp->bcgrpn', k, v * decay_end[..., None])
    chunk_decay = jnp.exp(a_cs[:, :, -1]).astype(cdt)

    def step(s, inp):
        dec, st = inp
        return dec[..., None, None] * s + st, s

    s_final, s_prev = lax.scan(step, s0.astype(cdt),
                               (jnp.moveaxis(chunk_decay, 1, 0), jnp.moveaxis(chunk_states, 1, 0)))
    y_inter = jnp.einsum('bclgn,cbgrpn->bclgrp', q, s_prev) * jnp.exp(a_cs).astype(cdt)[..., None]
    y = (y_intra + y_inter).reshape(b, nc * Q, G, R, P)[:, :L]
    return y, s_final


def mixer_layer(x, conv_buf, ssm_state, ret_state, pos0, w_pre, w_in, conv_w, conv_b, dt_bias,
                a_log, d_skip, ssm_norm_w, w_proj_ssm, w_proj_ret, w_out, w_post):
    b, L, _ = x.shape
    h = _rmsnorm(x, w_pre)
    proj = h @ w_in
    z, xbc, dt_raw, q, k, v, g, gates = _split_cols(proj, IN_SIZES)

    xbc_full = jnp.concatenate([conv_buf.astype(xbc.dtype), xbc], axis=1)
    new_conv = xbc_full[:, L:]
    conv = conv_b.astype(xbc.dtype)
    for tap in range(CONV_W):
        conv = conv + xbc_full[:, tap:tap + L] * conv_w[tap].astype(xbc.dtype)
    xbc_act = jax.nn.silu(conv)
    xs, bm, cm = _split_cols(xbc_act, (SSM_INNER, SSM_GROUPS * SSM_STATE, SSM_GROUPS * SSM_STATE))
    dt = jax.nn.softplus((dt_raw + dt_bias.astype(dt_raw.dtype)).astype(jnp.float32))
    a_neg = -jnp.exp(a_log.astype(jnp.float32))
    xs = xs.reshape(b, L, SSM_GROUPS, SSM_HPG, SSM_HEADDIM)
    dt_g = dt.reshape(b, L, SSM_GROUPS, SSM_HPG)
    y_ssm, ssm_new = chunked_decay_scan(
        cm.reshape(b, L, SSM_GROUPS, SSM_STATE),
        bm.reshape(b, L, SSM_GROUPS, SSM_STATE),
        xs * dt_g.astype(xs.dtype)[..., None],
        dt_g * a_neg.reshape(SSM_GROUPS, SSM_HPG),
        ssm_state.reshape(b, SSM_GROUPS, SSM_HPG, SSM_HEADDIM, SSM_STATE))
    y_ssm = y_ssm + xs * d_skip.astype(xs.dtype).reshape(SSM_GROUPS, SSM_HPG)[:, :, None]
    y_ssm = y_ssm.reshape(b, L, SSM_INNER) * jax.nn.silu(z)
    y_ssm = _rms(y_ssm.reshape(b, L, SSM_GROUPS, SSM_INNER // SSM_GROUPS)).reshape(b, L, SSM_INNER)
    p_ssm = (y_ssm * ssm_norm_w.astype(y_ssm.dtype)) @ w_proj_ssm

    pos = pos0 + jnp.arange(L)
    qh = _rope(q.reshape(b, L, RET_HEADS, RET_DK), pos)
    kh = _rope(k.reshape(b, L, RET_HEADS, RET_DK), pos) * (RET_DK ** -0.5)
    log_gamma = jnp.log1p(-jnp.exp2(-5.0 - jnp.arange(RET_HEADS, dtype=jnp.float32)))
    log_a_ret = jnp.broadcast_to(log_gamma[:, None], (b, L, RET_HEADS, 1))
    o, ret_new = chunked_decay_scan(
        qh, kh, v.reshape(b, L, RET_HEADS, 1, RET_DV), log_a_ret,
        ret_state.reshape(b, RET_HEADS, 1, RET_DV, RET_DK))
    o = _rms(o.reshape(b, L, RET_HEADS, RET_DV)).reshape(b, L, RET_V) * jax.nn.silu(g)
    p_ret = o @ w_proj_ret

    g_ssm, g_ret = _split_cols(gates, (D_MODEL, D_MODEL))
    merged = jax.nn.sigmoid(g_ssm) * p_ssm + jax.nn.sigmoid(g_ret) * p_ret
    out = merged @ w_out
    y = x + _rmsnorm(out, w_post)
    return (y, new_conv,
            ssm_new.reshape(b, SSM_HEADS, SSM_HEADDIM, SSM_STATE),
            ret_new.reshape(b, RET_HEADS, RET_DV, RET_DK))


def setup_inputs(seed: int = 0) -> dict:
    key = jax.random.key(seed)
    ks = jax.random.split(key, 17)
    f32 = jnp.float32
    nrm = lambda kk, shape, s: s * jax.random.normal(kk, shape, f32)
    x_prompt = nrm(ks[0], (BATCH, SEQ, D_MODEL), 1.0)
    x_sample = nrm(ks[1], (DEC_BATCH, DEC_SEQ, D_MODEL), 1.0)
    cache_conv = nrm(ks[2], (DEPTH, DEC_BATCH, CONV_W - 1, CONV_DIM), 1.0)
    state_ssm = nrm(ks[3], (DEPTH, DEC_BATCH, SSM_HEADS, SSM_HEADDIM, SSM_STATE), 0.3)
    state_ret = nrm(ks[4], (DEPTH, DEC_BATCH, RET_HEADS, RET_DV, RET_DK), 1.0)
    w_pre = 1.0 + nrm(ks[5], (DEPTH, D_MODEL), 0.05)
    w_in = nrm(ks[6], (DEPTH, D_MODEL, IN_DIM), D_MODEL ** -0.5)
    conv_w = nrm(ks[7], (DEPTH, CONV_W, CONV_DIM), CONV_W ** -0.5)
    conv_b = nrm(ks[8], (DEPTH, CONV_DIM), 0.02)
    dt0 = jnp.exp(jax.random.uniform(ks[9], (DEPTH, SSM_HEADS), f32, math.log(1e-3), math.log(1e-1)))
    dt_bias = dt0 + jnp.log(-jnp.expm1(-dt0))
    a_log = jnp.log(jax.random.uniform(ks[10], (DEPTH, SSM_HEADS), f32, 1.0, 16.0))
    d_skip = 1.0 + nrm(ks[11], (DEPTH, SSM_HEADS), 0.1)
    ssm_norm_w = 1.0 + nrm(ks[12], (DEPTH, SSM_INNER), 0.05)
    w_proj_ssm = nrm(ks[13], (DEPTH, SSM_INNER, D_MODEL), SSM_INNER ** -0.5)
    w_proj_ret = nrm(ks[14], (DEPTH, RET_V, D_MODEL), RET_V ** -0.5)
    w_out = nrm(ks[15], (DEPTH, D_MODEL, D_MODEL), D_MODEL ** -0.5)
    w_post = 1.0 + nrm(ks[16], (DEPTH, D_MODEL), 0.05)
    return {"x_prompt": x_prompt, "x_sample": x_sample, "cache_conv": cache_conv,
            "state_ssm": state_ssm, "state_ret": state_ret, "w_pre": w_pre, "w_in": w_in,
            "conv_w": conv_w, "conv_b": conv_b, "dt_bias": dt_bias, "a_log": a_log,
            "d_skip": d_skip, "ssm_norm_w": ssm_norm_w, "w_proj_ssm": w_proj_ssm,
            "w_proj_ret": w_proj_ret, "w_out": w_out, "w_post": w_post}


def reference(x_prompt, x_sample, cache_conv, state_ssm, state_ret, w_pre, w_in, conv_w, conv_b,
              dt_bias, a_log, d_skip, ssm_norm_w, w_proj_ssm, w_proj_ret, w_out, w_post):
    yp, ys = x_prompt, x_sample
    bp = x_prompt.shape[0]
    conv_p, ssm_p, ret_p, conv_s, ssm_s, ret_s = [], [], [], [], [], []
    for li in range(DEPTH):
        params = (w_pre[li], w_in[li], conv_w[li], conv_b[li], dt_bias[li], a_log[li], d_skip[li],
                  ssm_norm_w[li], w_proj_ssm[li], w_proj_ret[li], w_out[li], w_post[li])
        zc = jnp.zeros((bp, CONV_W - 1, CONV_DIM), yp.dtype)
        zs = jnp.zeros((bp, SSM_HEADS, SSM_HEADDIM, SSM_STATE), yp.dtype)
        zr = jnp.zeros((bp, RET_HEADS, RET_DV, RET_DK), yp.dtype)
        yp, cp, sp, rp = mixer_layer(yp, zc, zs, zr, 0, *params)
        ys, cs, ss, rs = mixer_layer(ys, cache_conv[li], state_ssm[li], state_ret[li], PAST_LEN, *params)
        conv_p.append(cp); ssm_p.append(sp); ret_p.append(rp)
        conv_s.append(cs); ssm_s.append(ss); ret_s.append(rs)
    return (yp, ys, jnp.stack(conv_p), jnp.stack(ssm_p), jnp.stack(ret_p),
            jnp.stack(conv_s), jnp.stack(ssm_s), jnp.stack(ret_s))
```

```python
import contextlib
import math
import numpy as np
import concourse.bass as bass
import concourse.mybir as mybir
from concourse.bass_utils import run_bass_kernel_spmd

F32 = mybir.dt.float32
BF16 = mybir.dt.bfloat16
ALU = mybir.AluOpType
AF = mybir.ActivationFunctionType
AX = mybir.AxisListType

D = 2048
NM = 1024
NP_ = 1024
NS = 64
NT = NM + NS
IN_DIM = 26688
OZ, OX, OB, OC, ODT, OQ, OK_, OV, OG, OGS, OGR = 0, 4096, 8192, 9216, 10240, 10304, 12352, 14400, 18496, 22592, 24640
EPS = 1e-6
SC = 256
WITH_SAMPLE = True


class Buf:
    __slots__ = ("name", "w", "r", "dsem", "dcnt")

    def __init__(self, name):
        self.name = name
        self.w = []
        self.r = []
        self.dsem = None
        self.dcnt = 0


class Plan:
    ENGS = ("pe", "act", "dve", "pool", "sp")
    SEM_LIMIT = 30000
    SAME_ENGINE_SYNC = {"pe": False, "act": True, "dve": True, "pool": True, "sp": True}

    def __init__(self, nc, stack):
        self.nc = nc
        self.stack = stack
        self.streams = {e: [] for e in self.ENGS}
        self.esem = {}
        self.ecnt = {e: 0 for e in self.ENGS}
        self.seen = {e: {} for e in self.ENGS}
        self.nsem = 0
        for e in self.ENGS:
            self.esem[e] = self.new_sem("e_" + e)
        self.bufs = []

    def new_sem(self, name):
        self.nsem += 1
        return self.stack.enter_context(self.nc.semaphore(f"{name}_{self.nsem}"))

    def buf(self, name):
        b = Buf(name)
        self.bufs.append(b)
        return b

    def _deps(self, eng, reads, writes, extra=()):
        need = {}

        def add(lst):
            for (s, v) in lst:
                k = id(s)
                if k not in need or need[k][1] < v:
                    need[k] = (s, v)
        for b in reads:
            add(b.w)
        for b in writes:
            add(b.w)
            add(b.r)
        add(extra)
        out = []
        seen = self.seen[eng]
        own = id(self.esem[eng])
        for k, (s, v) in need.items():
            if k == own and not self.SAME_ENGINE_SYNC[eng]:
                continue
            if seen.get(k, -1) >= v:
                continue
            seen[k] = v
            out.append((s, v))
        return out

    def op(self, eng, fn, reads=(), writes=()):
        waits = self._deps(eng, reads, writes)
        if self.ecnt[eng] >= self.SEM_LIMIT:
            self.esem[eng] = self.new_sem("e_" + eng)
            self.ecnt[eng] = 0
        self.ecnt[eng] += 1
        tok = (self.esem[eng], self.ecnt[eng])
        self.streams[eng].append((waits, fn, (self.esem[eng], 1)))
        for b in reads:
            b.r.append(tok)
        for b in writes:
            b.w = [tok]
            b.r = []
        return tok

    def dma(self, eng, fn, reads=(), writes=(), owner=None):
        waits = self._deps(eng, reads, writes)
        ow = owner
        if ow.dsem is None:
            ow.dsem = self.new_sem("d_" + ow.name)
        ow.dcnt += 16
        tok = (ow.dsem, ow.dcnt)
        self.streams[eng].append((waits, fn, (ow.dsem, 16)))
        for b in reads:
            b.r.append(tok)
        for b in writes:
            b.w = [tok]
            b.r = []
        return tok

    def barrier(self):
        toks = []
        for b in self.bufs:
            toks += b.w
            toks += b.r
        for e in self.ENGS:
            if self.ecnt[e] > 0:
                toks.append((self.esem[e], self.ecnt[e]))
        for e in self.ENGS:
            waits = self._deps(e, (), (), extra=toks)
            self.streams[e].append((waits, None, None))
        for b in self.bufs:
            b.w = []
            b.r = []

    def emit(self, block):
        def run(stream):
            def body(e):
                for waits, fn, inc in stream:
                    for (s, v) in waits:
                        e.wait_ge(s, v)
                    if fn is not None:
                        fn(e).then_inc(inc[0], inc[1])
            return body
        block.tensor(run(self.streams["pe"]))
        block.scalar(run(self.streams["act"]))
        block.vector(run(self.streams["dve"]))
        block.gpsimd(run(self.streams["pool"]))
        block.sync(run(self.streams["sp"]))


def _gammas():
    return (1.0 - np.exp2(-5.0 - np.arange(8, dtype=np.float64)))


CONST_LAYOUT = {}


def make_consts():
    cols = []
    off = [0]

    def put(name, arr):
        a = np.zeros((128, arr.shape[1]), np.float32)
        a[:arr.shape[0]] = arr
        CONST_LAYOUT[name] = (off[0], arr.shape[1])
        off[0] += arr.shape[1]
        cols.append(a)
    i = np.arange(128)
    put("ident", np.eye(128, dtype=np.float32))
    put("causal", (i[:, None] <= i[None, :]).astype(np.float32))
    put("strict", (i[:, None] > i[None, :]).astype(np.float32))
    put("incl", (i[:, None] <= i[None, :]).astype(np.float32))
    put("ones", np.ones((128, 128), np.float32))
    g = _gammas()
    dk = 256 ** -0.5
    retD = np.zeros((128, 8 * 128), np.float64)
    gpow = np.zeros((128, 8 * 128), np.float64)
    kds = np.zeros((128, 8), np.float64)
    for h in range(8):
        dlt = (i[None, :] - i[:, None])
        m = np.where(dlt >= 0, g[h] ** np.maximum(dlt, 0), 0.0) * dk
        retD[:, h * 128:(h + 1) * 128] = m
        gpow[:, h * 128:(h + 1) * 128] = (g[h] ** (i + 1))[None, :]
        kds[:, h] = g[h] ** (127 - i) * dk
    put("retD", retD.astype(np.float32))
    put("gpow", gpow.astype(np.float32))
    put("kds", kds.astype(np.float32))
    j = np.arange(64)
    tt, sq = j // 16, j % 16
    same = (sq[:, None] == sq[None, :])
    put("causal_s", (same & (tt[:, None] <= tt[None, :])).astype(np.float32))
    put("strict_s", (same & (tt[:, None] > tt[None, :])).astype(np.float32))
    put("incl_s", (same & (tt[:, None] <= tt[None, :])).astype(np.float32))
    put("seqmask", (sq[:, None] == np.arange(16)[None, :]).astype(np.float32))
    put("same_s", same.astype(np.float32))
    retDs = np.zeros((64, 8 * 64), np.float64)
    gpows = np.zeros((128, 8 * 64), np.float64)
    kdss = np.zeros((64, 8), np.float64)
    for h in range(8):
        dlt = tt[None, :] - tt[:, None]
        m = np.where(same & (dlt >= 0), g[h] ** np.maximum(dlt, 0), 0.0) * dk
        retDs[:, h * 64:(h + 1) * 64] = m
        gpows[:, h * 64:(h + 1) * 64] = (g[h] ** (tt + 1))[None, :]
        kdss[:, h] = g[h] ** (3 - tt) * dk
    put("retD_s", retDs.astype(np.float32))
    put("gpow_s", gpows.astype(np.float32))
    put("kds_s", kdss.astype(np.float32))
    return np.ascontiguousarray(np.concatenate(cols, axis=1))


def rope_tables(pos):
    half = 128
    inv = (1.0 / (10000.0 ** (np.arange(half, dtype=np.float32) / np.float32(half)))).astype(np.float32)
    ang = pos.astype(np.float32)[None, :] * inv[:, None]
    return np.cos(ang).astype(np.float32), np.sin(ang).astype(np.float32)


def build_program():
    nc = bass.Bass("TRN2", target_bir_lowering=False)
    NCOL = sum(v[1] for v in CONST_LAYOUT.values())

    def din(name, shape, dt=F32):
        return nc.dram_tensor(name, list(shape), dt, kind="ExternalInput").ap()

    def dout(name, shape, dt=F32):
        return nc.dram_tensor(name, list(shape), dt, kind="ExternalOutput").ap()

    xm = din("xm", [NM, D]); xp = din("xp", [NP_, D]); xs_ = din("xs", [NS, D])
    flag_d = din("flag", [128, 1]); consts_d = din("consts", [128, NCOL])
    cosM_d = din("cosM", [128, NM]); sinM_d = din("sinM", [128, NM])
    cosP_d = din("cosP", [128, NP_]); sinP_d = din("sinP", [128, NP_])
    cosS_d = din("cosS", [128, NS]); sinS_d = din("sinS", [128, NS])
    w_in = din("w_in", [D, IN_DIM]); wssm = din("wssm", [4096, D]); wret = din("wret", [4096, D])
    wout = din("wout", [D, D])
    wpre_d = din("wpre_b", [128, D]); wpost_d = din("wpost_b", [128, D]); normw_d = din("normw_b", [128, 4096])
    convw_d = din("convw", [128, 48, 4]); convb_d = din("convb", [128, 48])
    dtb_d = din("dtb_b", [128, 64]); alog_d = din("alog_b", [128, 64]); dsk_d = din("dsk_b", [128, 64])
    cconv_d = din("cconv", [48, 6144]); sssm_d = din("sssm", [16, 4096, 128]); sret_d = din("sret", [16, 4096, 256])

    ym = dout("ym", [NM, D]); ys = dout("ys", [NS, D])
    convp = dout("convp", [3, 6144]); ssmp = dout("ssmp", [4096, 128]); retp = dout("retp", [4096, 256])
    convs = dout("convs", [16, 3, 6144]); ssms = dout("ssms", [16, 4096, 128]); rets = dout("rets", [16, 4096, 256])

    YTd = nc.dram_tensor("YTd", [32, 128, NT], BF16).ap()
    OTd = nc.dram_tensor("OTd", [32, 128, NT], BF16).ap()
    STs_d = nc.dram_tensor("STs_d", [8, 128, 512], F32).ap()
    STr_d = nc.dram_tensor("STr_d", [8, 128, 1024], F32).ap()

    with contextlib.ExitStack() as st0:
        P = Plan(nc, st0)

        uniq = [0]

        def sbuf(stack, name, shape, dt=F32):
            uniq[0] += 1
            nm = f"{name}_{uniq[0]}"
            t = stack.enter_context(nc.sbuf_tensor(nm, list(shape), dt))
            return t, P.buf(nm)

        def ZERO(ap, b):
            P.op("dve", lambda e: e.memset(ap, 0.0), [], [b])

        ps = []
        for i in range(6):
            t = st0.enter_context(nc.psum_tensor(f"ps{i}", [128, 512], F32))
            ps.append((t, P.buf(f"ps{i}")))
        psT, bpsT = st0.enter_context(nc.psum_tensor("psT", [128, 512], BF16)), P.buf("psT")
        psM, bpsM = st0.enter_context(nc.psum_tensor("psM", [128, 256], F32)), P.buf("psM")

        def MM(out, lhsT, rhs, start, stop, R, W):
            P.op("pe", lambda e: e.matmul(out, lhsT=lhsT, rhs=rhs, start=start, stop=stop), R, W)

        def TR(out, in_, ident, R, W):
            P.op("pe", lambda e: e.transpose(out=out, in_=in_, identity=ident), R, W)

        def ACT(out, in_, func, R, W, **kw):
            P.op("act", lambda e: e.activation(out=out, in_=in_, func=func, **kw), R, W)

        def TT(out, a, b, op, R, W):
            P.op("dve", lambda e: e.tensor_tensor(out=out, in0=a, in1=b, op=op), R, W)

        def TS(out, a, s1, s2, op0, op1, R, W):
            if s2 is None:
                P.op("dve", lambda e: e.tensor_scalar(out=out, in0=a, scalar1=s1, scalar2=None, op0=op0), R, W)
            else:
                P.op("dve", lambda e: e.tensor_scalar(out=out, in0=a, scalar1=s1, scalar2=s2, op0=op0, op1=op1), R, W)

        def STT(out, a, s, b, op0, op1, R, W):
            P.op("dve", lambda e: e.scalar_tensor_tensor(out=out, in0=a, scalar=s, in1=b, op0=op0, op1=op1), R, W)

        def CP(out, in_, R, W):
            P.op("dve", lambda e: e.tensor_copy(out=out, in_=in_), R, W)

        def LD(out, in_, W, owner, R=(), eng="sp"):
            P.dma(eng, lambda e: e.dma_start(out=out, in_=in_), R, W, owner)

        def LDC(out, in_, W, owner, R=()):
            P.dma("pool", lambda e: e.dma_start(out=out, in_=in_), R, W, owner)

        def LDNC(out, in_, W, owner, R=()):
            P.dma("sp", lambda e: e.dma_start(out=out, in_=in_, allow_slow_non_contiguous=True), R, W, owner)

        def rstd_from_ss(rs, brs, ss, n):
            TS(rs, ss, 1.0 / n, EPS, ALU.mult, ALU.add, [brs] if ss.tensor is rs.tensor else [brs], [brs])
            ACT(rs, rs, AF.Sqrt, [brs], [brs])
            P.op("dve", lambda e: e.reciprocal(out=rs, in_=rs), [brs], [brs])

        C, bC = sbuf(st0, "consts", [128, NCOL])
        LD(C[:], consts_d[:, :], [bC], bC)

        def cst(name, rows=128):
            o, n = CONST_LAYOUT[name]
            return C[0:rows, o:o + n]
        identb, bidb = sbuf(st0, "identb", [128, 128], BF16)
        CP(identb[:], cst("ident"), [bC], [bidb])
        flag, bflag = sbuf(st0, "flag", [128, 1])
        LD(flag[:], flag_d[:, :], [bflag], bflag)
        hTm, bhTm = sbuf(st0, "hTm", [128, 16, NM], BF16)
        hTs, bhTs = sbuf(st0, "hTs", [128, 16, NS], BF16)
        convtail, bct = sbuf(st0, "convtail", [128, 48, 3])
        convout, bco = sbuf(st0, "convout", [128, 48, 3])
        small, bsmall = sbuf(st0, "small", [128, 3, 64])
        LD(small[:, 0, :], dtb_d[:, :], [bsmall], bsmall)
        LD(small[:, 1, :], alog_d[:, :], [bsmall], bsmall)
        LD(small[:, 2, :], dsk_d[:, :], [bsmall], bsmall)
        ACT(small[:, 1, :], small[:, 1, :], AF.Exp, [bsmall], [bsmall])
        TS(small[:, 1, :], small[:, 1, :], -1.0, None, ALU.mult, None, [bsmall], [bsmall])
        convw, bcw = sbuf(st0, "convw", [128, 48, 4])
        convb, bcb = sbuf(st0, "convb", [128, 48])
        LD(convw[:], convw_d[:, :, :], [bcw], bcw)
        LD(convb[:], convb_d[:, :], [bcb], bcb)

        def phase0(stack, xsrc, ntok, hT, bhT):
            xt, bxt = sbuf(stack, "p0_xt", [128, D])
            hb, bhb = sbuf(stack, "p0_hb", [128, D], BF16)
            wpre, bwpre = sbuf(stack, "p0_wpre", [128, D])
            st_, bst = sbuf(stack, "p0_st", [128, 2])
            LD(wpre[:], wpre_d[:, :], [bwpre], bwpre)
            for t0 in range(0, ntok, 128):
                rows = min(128, ntok - t0)
                LD(xt[0:rows, :], xsrc[t0:t0 + rows, :], [bxt], bxt)
                ZERO(st_[:, 0:1], bst)
                ACT(hb[0:rows, :], xt[0:rows, :], AF.Square, [bxt], [bhb, bst], accum_out=st_[0:rows, 0:1])
                TS(st_[0:rows, 1:2], st_[0:rows, 0:1], 1.0 / D, EPS, ALU.mult, ALU.add, [bst], [bst])
                ACT(st_[0:rows, 1:2], st_[0:rows, 1:2], AF.Sqrt, [bst], [bst])
                P.op("dve", lambda e, rows=rows: e.reciprocal(out=st_[0:rows, 1:2], in_=st_[0:rows, 1:2]), [bst], [bst])
                STT(hb[0:rows, :], xt[0:rows, :], st_[0:rows, 1:2], wpre[0:rows, :], ALU.mult, ALU.mult,
                    [bxt, bst, bwpre, bhb], [bhb])
                for k0 in range(0, 16, 4):
                    for kk in range(4):
                        kc = k0 + kk
                        TR(psT[:, kk * 128:kk * 128 + rows], hb[0:rows, kc * 128:(kc + 1) * 128],
                           identb[0:rows, 0:rows], [bhb, bidb], [bpsT])
                    src = psT[:, 0:512].rearrange("p (a b) -> p a b", b=128)[:, :, 0:rows]
                    ACT(hT[:, k0:k0 + 4, t0:t0 + rows], src, AF.Copy, [bpsT], [bhT])

        def phase1(stack, hT, bhT, ntok, cos_d, sin_d, mode):
            main = (mode == "main")
            WA, bWA = sbuf(stack, "WA", [128, 16, 1536], BF16)
            if main:
                WB, bWB = sbuf(stack, "WB", [128, 16, 1280], BF16)
                normw, bnw = sbuf(stack, "normw", [128, 512])
                zs, bzs = sbuf(stack, "zs", [128, 512])
                qkm, bqkm = sbuf(stack, "qkm", [128, 128])
                Ld, bLd = sbuf(stack, "Ld", [128, 8, 128])
                segT, bsegT = sbuf(stack, "segT", [128, 512])
                MT, bMT = sbuf(stack, "MT", [128, 8, 128], BF16)
                xdt, bxdt = sbuf(stack, "xdt", [128, 512], BF16)
                xsk, bxsk = sbuf(stack, "xsk", [128, 512])
                ybar, bybar = sbuf(stack, "ybar", [128, 512], BF16)
                stage, bstage = sbuf(stack, "stage", [128, 4, 128], BF16)
                MTr, bMTr = sbuf(stack, "MTr", [128, 128], BF16)
                qs, bqs = sbuf(stack, "qs", [128, 2, 128], BF16)
            if main and WITH_SAMPLE:
                csS, bcsS = sbuf(stack, "csS", [128, 2, 64])
                LD(csS[:, 0, :], cosS_d[:, :], [bcsS], bcsS)
                LD(csS[:, 1, :], sinS_d[:, :], [bcsS], bcsS)
                cc, bcc = sbuf(stack, "cc", [128, 128])
                S0buf = [sbuf(stack, f"S0n{i}", [128, 4, 128]) for i in range(1)]
                R0buf = [sbuf(stack, f"R0n{i}", [128, 4, 256]) for i in range(1)]
                cdcol, bcdcol = sbuf(stack, "cdcol", [128, 4, 16])
                Bm, bBm = sbuf(stack, "Bm", [128, 128], BF16)
                kdm, bkdm = sbuf(stack, "kdm", [128, 256], BF16)
            Wdt, bWdt = sbuf(stack, "Wdt", [128, 16, 64], BF16)
            LDC(Wdt[:], w_in[:, ODT:ODT + 64].rearrange("(kc p) n -> p kc n", p=128), [bWdt], bWdt)
            raw, braw = sbuf(stack, "raw", [128, 6, SC + 3])
            acc, bacc = sbuf(stack, "acc", [128, 512])
            cT, bcT = sbuf(stack, "cT", [128, 6, SC], BF16)
            qkT, bqkT = sbuf(stack, "qkT", [128, 4, SC], BF16)
            cosT, bcos = sbuf(stack, "cosT", [128, SC])
            sinT, bsin = sbuf(stack, "sinT", [128, SC])
            r1, br1 = sbuf(stack, "r1", [128, 512])
            r2, br2 = sbuf(stack, "r2", [128, 512])
            y1, by1, y2, by2 = r1, br1, r2, br2
            sm, bsm = sbuf(stack, "sm", [128, 12, 8])
            xdd, bxdd = sbuf(stack, "xdd", [128, 512], BF16)
            Btok, bBtok = sbuf(stack, "Btok", [128, 128], BF16)
            ST, bST = sbuf(stack, "ST", [128, 512])
            STb, bSTb = sbuf(stack, "STb", [128, 512], BF16)
            tmpS, btmpS = acc, bacc
            RST, bRST = sbuf(stack, "RST", [128, 2, 512])
            RSTb, bRSTb = sbuf(stack, "RSTb", [128, 2, 512], BF16)
            vb, bvb = sbuf(stack, "vb", [128, 512], BF16)
            kd, bkd = sbuf(stack, "kd", [128, 256], BF16)
            rs, brs = sbuf(stack, "rs", [128, 2])
            mmi = [0]

            def mmbank():
                mmi[0] ^= 1
                return ps[mmi[0]]

            def wcols(dst, c0, lo, n):
                return (dst[:, :, lo:lo + n], w_in[:, c0:c0 + n].rearrange("(kc p) n -> p kc n", p=128))

            def sample_group(g, cblk):
                identf = cst("ident")
                raws, braws = raw[:, :, 0:112].rearrange("p b (t s) -> p b t s", s=16), braw
                cTs, bcTs = cT[:, :, 0:64], bcT
                qkTs, bqkTs = qkT[:, :, 0:64], bqkT
                S0T, bS0T = STb, bSTb
                R0T, bR0T = RSTb, bRSTb
                yinT, byinT = acc[:, 0:256].rearrange("p (j t) -> p j t", t=64), bacc
                qs_s, bqs_s = qs[:, :, 0:64], bqs
                stage_s, bstage_s = stage[:, :, 0:64], bstage
                seqm = cst("seqmask", 64)
                for (lo, n, c0) in ((0, 512, g * 512), (512, 128, 4096 + g * 128), (640, 128, 5120 + g * 128)):
                    pt, bpt = mmbank()
                    for kc in range(16):
                        MM(pt[0:64, 0:n], hTs[:, kc, 0:64], WA[:, kc, lo:lo + n], kc == 0, kc == 15, [bhTs, bWA], [bpt])
                    ACT(y1[0:64, 0:n], pt[0:64, 0:n], AF.Copy, [bpt], [by1])
                    for t in (1, 2, 3):
                        LD(convs[:, t - 1, c0:c0 + n], y1[16 * t:16 * t + 16, 0:n], [], by1, R=[by1])
                p2, bp2 = ps[2]
                for bi in range(6):
                    cb = cblk[bi]
                    LD(cc[0:48, :], cconv_d[:, cb * 128:(cb + 1) * 128], [bcc], bcc)
                    TR(p2[:, 0:48], cc[0:48, :], identf[0:48, 0:48], [bcc, bC], [bp2])
                    ACT(raws[:, bi, 0:3, :], p2[:, 0:48].rearrange("p (t s) -> p t s", s=16), AF.Copy, [bp2], [braws])
                    pt, bpt = mmbank()
                    for kc in range(16):
                        MM(pt[:, 0:64], WA[:, kc, bi * 128:(bi + 1) * 128], hTs[:, kc, 0:64], kc == 0, kc == 15,
                           [bWA, bhTs], [bpt])
                    ACT(raws[:, bi, 3:7, :], pt[:, 0:64].rearrange("p (t s) -> p t s", s=16), AF.Copy, [bpt], [braws])
                for bi in range(6):
                    cb = cblk[bi]
                    rf = raws[:, bi, :, :].rearrange("p t s -> p (t s)")
                    TS(acc[:, 0:64], rf[:, 48:112], convw[:, cb, 3:4], convb[:, cb:cb + 1], ALU.mult, ALU.add,
                       [braws, bcw, bcb], [bacc])
                    for tap in (2, 1, 0):
                        STT(acc[:, 0:64], rf[:, tap * 16:tap * 16 + 64], convw[:, cb, tap:tap + 1], acc[:, 0:64],
                            ALU.mult, ALU.add, [braws, bcw, bacc], [bacc])
                    ACT(cTs[:, bi, :], acc[:, 0:64], AF.Silu, [bacc], [bcTs])
                for which in (0, 1):
                    W_, bW_, lo0 = (WB, bWB, 512) if which == 0 else (WA, bWA, 768)
                    p0, bp0 = ps[0]
                    p1, bp1 = ps[1]
                    for hf, (pt, bpt) in enumerate(((p0, bp0), (p1, bp1))):
                        for kc in range(16):
                            MM(pt[:, 0:64], W_[:, kc, lo0 + hf * 128:lo0 + (hf + 1) * 128], hTs[:, kc, 0:64], kc == 0,
                               kc == 15, [bW_, bhTs], [bpt])
                    cS_, sS_ = csS[:, 0, :], csS[:, 1, :]
                    TT(r1[:, 0:64], p0[:, 0:64], cS_, ALU.mult, [bp0, bcsS], [br1])
                    TT(r2[:, 0:64], p1[:, 0:64], sS_, ALU.mult, [bp1, bcsS], [br2])
                    TT(qkTs[:, which * 2, :], r1[:, 0:64], r2[:, 0:64], ALU.subtract, [br1, br2], [bqkTs])
                    TT(r1[:, 0:64], p1[:, 0:64], cS_, ALU.mult, [bp1, bcsS], [br1])
                    TT(r2[:, 0:64], p0[:, 0:64], sS_, ALU.mult, [bp0, bcsS], [br2])
                    TT(qkTs[:, which * 2 + 1, :], r1[:, 0:64], r2[:, 0:64], ALU.add, [br1, br2], [bqkTs])
                S = slice(0, 64)
                for kc in range(16):
                    MM(psM[S, 0:8], hTs[:, kc, 0:64], Wdt[:, kc, g * 8:(g + 1) * 8], kc == 0, kc == 15, [bhTs, bWdt], [bpsM])
                TT(sm[S, 0, :], psM[S, 0:8], small[S, 0, g * 8:(g + 1) * 8], ALU.add, [bpsM, bsmall], [bsm])
                ACT(sm[S, 0, :], sm[S, 0, :], AF.Exp, [bsm], [bsm])
                ACT(sm[S, 1, :], sm[S, 0, :], AF.Ln, [bsm], [bsm], bias=1.0)
                TT(sm[S, 2, :], sm[S, 1, :], small[S, 1, g * 8:(g + 1) * 8], ALU.mult, [bsm, bsmall], [bsm])
                MM(psM[S, 8:16], cst("incl_s", 64), sm[S, 2, :], True, True, [bC, bsm], [bpsM])
                MM(psM[S, 16:24], cst("same_s", 64), sm[S, 2, :], True, True, [bC, bsm], [bpsM])
                ACT(sm[S, 3, :], psM[S, 8:16], AF.Copy, [bpsM], [bsm])
                TT(sm[S, 4, :], psM[S, 16:24], sm[S, 3, :], ALU.subtract, [bpsM, bsm], [bsm])
                ACT(sm[S, 5, :], sm[S, 4, :], AF.Exp, [bsm], [bsm])
                ACT(sm[S, 6, :], sm[S, 3, :], AF.Exp, [bsm], [bsm])
                TT(sm[S, 8, :], sm[S, 1, :], sm[S, 5, :], ALU.mult, [bsm], [bsm])
                CP(segT[S, :].rearrange("p (r q) -> p r q", q=64), sm[S, 2, :].unsqueeze(2).broadcast_to([64, 8, 64]),
                   [bsm], [bsegT])
                for j in range(4):
                    MM(psM[:, 32 + j * 16:32 + (j + 1) * 16], segT[S, j * 128:(j + 1) * 128], seqm, True, True,
                       [bsegT, bC], [bpsM])
                ACT(cdcol[:], psM[:, 32:96].rearrange("p (j s) -> p j s", s=16), AF.Exp, [bpsM], [bcdcol])

                def b8s(row):
                    return sm[S, row, :].unsqueeze(2).broadcast_to([64, 8, 64])
                for j in range(4):
                    TR(psT[S, j * 128:(j + 1) * 128], cTs[:, j, :], identb[:], [bcTs, bidb], [bpsT])
                xs3 = psT[S, 0:512].rearrange("p (r q) -> p r q", q=64)
                TT(xdd[S, :].rearrange("p (r q) -> p r q", q=64), xs3, b8s(8), ALU.mult, [bpsT, bsm], [bxdd])
                TT(xdt[S, :].rearrange("p (r q) -> p r q", q=64), xs3, b8s(1), ALU.mult, [bpsT, bsm], [bxdt])
                dskb = small[S, 2, g * 8:(g + 1) * 8].unsqueeze(2).broadcast_to([64, 8, 64])
                TT(xsk[S, :].rearrange("p (r q) -> p r q", q=64), xs3, dskb, ALU.mult, [bpsT, bsmall], [bxsk])
                TR(psT[S, 0:128], cTs[:, 4, :], identb[:], [bcTs, bidb], [bpsT])
                ACT(Btok[S, :], psT[S, 0:128], AF.Copy, [bpsT], [bBtok])
                MM(psM[S, 128:192], cTs[:, 4, :], cTs[:, 5, :], True, True, [bcTs], [bpsM])
                TT(qkm[S, 0:64], psM[S, 128:192], cst("causal_s", 64), ALU.mult, [bpsM, bC], [bqkm])
                TT(Ld[S, :, 0:64], cst("strict_s", 64).unsqueeze(1).broadcast_to([64, 8, 64]),
                   sm[S, 2, :].unsqueeze(2).broadcast_to([64, 8, 64]), ALU.mult, [bC, bsm], [bLd])
                for r in range(8):
                    MM(p2[S, r * 64:(r + 1) * 64], Ld[S, r, 0:64], cst("incl_s", 64), True, True, [bLd, bC], [bp2])
                ACT(segT[S, :], p2[S, :], AF.Exp, [bp2], [bsegT])
                TT(MT[S, :, 0:64], segT[S, :].rearrange("p (r l) -> p r l", l=64),
                   qkm[S, 0:64].unsqueeze(1).broadcast_to([64, 8, 64]), ALU.mult, [bsegT, bqkm], [bMT])
                p3, bp3 = ps[3]
                p4, bp4 = ps[4]
                p5, bp5 = ps[5]
                for r in range(8):
                    MM(p3[S, r * 64:(r + 1) * 64], MT[S, r, 0:64], xdt[S, r * 64:(r + 1) * 64], True, True,
                       [bMT, bxdt], [bp3])
                for s in range(16):
                    S0n, bS0n = S0buf[0]
                    LD(S0n[:], sssm_d[s, g * 512:(g + 1) * 512, :].rearrange("(j p) n -> p j n", p=128), [bS0n], bS0n)
                    for j in range(4):
                        TR(p4[:, j * 128:(j + 1) * 128], S0n[:, j, :], identf, [bS0n, bC], [bp4])
                    ACT(S0T[:], p4[:, :], AF.Copy, [bp4], [bS0T])
                    for j in range(4):
                        MM(p5[:, j * 64 + s:(j + 1) * 64:16], S0T[:, j * 128:(j + 1) * 128], cTs[:, 5, s:64:16], True, True,
                           [bS0T, bcTs], [bp5])
                    TS(Bm[S, :], Btok[S, :], seqm[:, s:s + 1], None, ALU.mult, None, [bBtok, bC], [bBm])
                    pX, bpX = mmbank()
                    for j in range(4):
                        MM(pX[:, j * 128:(j + 1) * 128], xdd[S, j * 128:(j + 1) * 128], Bm[S, :], True, True,
                           [bxdd, bBm], [bpX])
                    TT(S0n[:], S0n[:], cdcol[:, :, s].unsqueeze(2).broadcast_to([128, 4, 128]), ALU.mult,
                       [bS0n, bcdcol], [bS0n])
                    TT(S0n[:], S0n[:], pX[:, :].rearrange("p (j n) -> p j n", n=128), ALU.add, [bS0n, bpX], [bS0n])
                    LD(ssms[s, g * 512:(g + 1) * 512, :].rearrange("(j p) n -> p j n", p=128), S0n[:], [], bS0n, R=[bS0n])
                ACT(yinT, p5[:, 0:256].rearrange("p (j t) -> p j t", t=64), AF.Copy, [bp5], [byinT])
                for j in range(4):
                    TR(p4[S, j * 128:(j + 1) * 128], yinT[:, j, :], identf, [byinT, bC], [bp4])
                pz, bpz = mmbank()
                for kc in range(16):
                    MM(pz[S, :], hTs[:, kc, 0:64], WB[:, kc, 0:512], kc == 0, kc == 15, [bhTs, bWB], [bpz])
                ACT(zs[S, :], pz[S, :], AF.Silu, [bpz], [bzs])
                TT(y1[S, :].rearrange("p (r q) -> p r q", q=64), p4[S, :].rearrange("p (r q) -> p r q", q=64), b8s(6),
                   ALU.mult, [bp4, bsm], [by1])
                TT(y2[S, :], y1[S, :], p3[S, :], ALU.add, [by1, bp3], [by2])
                TT(y1[S, :], y2[S, :], xsk[S, :], ALU.add, [by2, bxsk], [by1])
                TT(y2[S, :], y1[S, :], zs[S, :], ALU.mult, [by1, bzs], [by2])
                ZERO(rs[:, 0:1], brs)
                ACT(y1[S, :], y2[S, :], AF.Square, [by2], [by1, brs], accum_out=rs[S, 0:1])
                TS(rs[S, 1:2], rs[S, 0:1], 1.0 / 512, EPS, ALU.mult, ALU.add, [brs], [brs])
                ACT(rs[S, 1:2], rs[S, 1:2], AF.Sqrt, [brs], [brs])
                P.op("dve", lambda e: e.reciprocal(out=rs[S, 1:2], in_=rs[S, 1:2]), [brs], [brs])
                STT(ybar[S, :], y2[S, :], rs[S, 1:2], normw[S, :], ALU.mult, ALU.mult, [by2, brs, bnw], [bybar])
                for j in range(4):
                    TR(psT[:, j * 64:(j + 1) * 64], ybar[S, j * 128:(j + 1) * 128], identb[0:64, 0:64], [bybar, bidb], [bpsT])
                ACT(stage_s, psT[:, 0:256].rearrange("p (a b) -> p a b", b=64), AF.Copy, [bpsT], [bstage_s])
                LD(YTd[g * 4:g * 4 + 4, :, NM:NM + 64].rearrange("j p t -> p j t"), stage_s, [], bstage_s, R=[bstage_s])
                pv, bpv = mmbank()
                for kc in range(16):
                    MM(pv[S, :], hTs[:, kc, 0:64], WA[:, kc, 1024:1536], kc == 0, kc == 15, [bhTs, bWA], [bpv])
                ACT(vb[S, :], pv[S, :], AF.Copy, [bpv], [bvb])
                for hf in range(2):
                    TR(psT[S, hf * 128:(hf + 1) * 128], qkTs[:, 2 + hf, :], identb[:], [bqkTs, bidb], [bpsT])
                oK, _ = CONST_LAYOUT["kds_s"]
                TS(kd[S, :], psT[S, 0:256], C[S, oK + g:oK + g + 1], None, ALU.mult, None, [bpsT, bC], [bkd])
                for hf in range(2):
                    MM(psM[S, 128:192], qkTs[:, 2 + hf, :], qkTs[:, hf, :], hf == 0, hf == 1, [bqkTs], [bpsM])
                oD, _ = CONST_LAYOUT["retD_s"]
                TT(MTr[S, 0:64], psM[S, 128:192], C[S, oD + g * 64:oD + (g + 1) * 64], ALU.mult, [bpsM, bC], [bMTr])
                oG, _ = CONST_LAYOUT["gpow_s"]
                TT(qs_s, qkTs[:, 0:2, :], C[:, oG + g * 64:oG + (g + 1) * 64].unsqueeze(1).broadcast_to([128, 2, 64]),
                   ALU.mult, [bqkTs, bC], [bqs_s])
                MM(p3[S, :], MTr[S, 0:64], vb[S, :], True, True, [bMTr, bvb], [bp3])
                pg, bpg = mmbank()
                for kc in range(16):
                    MM(pg[S, :], hTs[:, kc, 0:64], WB[:, kc, 768:1280], kc == 0, kc == 15, [bhTs, bWB], [bpg])
                ACT(zs[S, :], pg[S, :], AF.Silu, [bpg], [bzs])
                g4 = float(_gammas()[g] ** 4)
                for s in range(16):
                    R0n, bR0n = R0buf[0]
                    LD(R0n[:], sret_d[s, g * 512:(g + 1) * 512, :].rearrange("(j p) k -> p j k", p=128), [bR0n], bR0n)
                    for hf in range(2):
                        for j in range(4):
                            TR(p4[:, j * 128:(j + 1) * 128], R0n[:, j, hf * 128:(hf + 1) * 128], identf, [bR0n, bC], [bp4])
                        ACT(R0T[:, hf, :], p4[:, :], AF.Copy, [bp4], [bR0T])
                    for j in range(4):
                        for hf in range(2):
                            MM(p5[:, j * 64 + s:(j + 1) * 64:16], R0T[:, hf, j * 128:(j + 1) * 128], qs_s[:, hf, s:64:16],
                               hf == 0, hf == 1, [bR0T, bqs_s], [bp5])
                    TS(kdm[S, :], kd[S, :], seqm[:, s:s + 1], None, ALU.mult, None, [bkd, bC], [bkdm])
                    for jj in range(2):
                        pX, bpX = mmbank()
                        for j2 in range(2):
                            j = jj * 2 + j2
                            MM(pX[:, j2 * 256:(j2 + 1) * 256], vb[S, j * 128:(j + 1) * 128], kdm[S, :], True, True,
                               [bvb, bkdm], [bpX])
                        STT(R0n[:, jj * 2:jj * 2 + 2, :], R0n[:, jj * 2:jj * 2 + 2, :], g4,
                            pX[:, :].rearrange("p (j k) -> p j k", k=256), ALU.mult, ALU.add, [bR0n, bpX], [bR0n])
                    LD(rets[s, g * 512:(g + 1) * 512, :].rearrange("(j p) k -> p j k", p=128), R0n[:], [], bR0n, R=[bR0n])
                ACT(yinT, p5[:, 0:256].rearrange("p (j t) -> p j t", t=64), AF.Copy, [bp5], [byinT])
                for j in range(4):
                    TR(p4[S, j * 128:(j + 1) * 128], yinT[:, j, :], identf, [byinT, bC], [bp4])
                ACT(y1[S, :], p4[S, :], AF.Copy, [bp4], [by1])
                TT(y2[S, :], y1[S, :], p3[S, :], ALU.add, [by1, bp3], [by2])
                ZERO(rs[:, 0:1], brs)
                ACT(y1[S, :], y2[S, :], AF.Square, [by2], [by1, brs], accum_out=rs[S, 0:1])
                TS(rs[S, 1:2], rs[S, 0:1], 1.0 / 512, EPS, ALU.mult, ALU.add, [brs], [brs])
                ACT(rs[S, 1:2], rs[S, 1:2], AF.Sqrt, [brs], [brs])
                P.op("dve", lambda e: e.reciprocal(out=rs[S, 1:2], in_=rs[S, 1:2]), [brs], [brs])
                STT(ybar[S, :], y2[S, :], rs[S, 1:2], zs[S, :], ALU.mult, ALU.mult, [by2, brs, bzs], [bybar])
                for j in range(4):
                    TR(psT[:, j * 64:(j + 1) * 64], ybar[S, j * 128:(j + 1) * 128], identb[0:64, 0:64], [bybar, bidb], [bpsT])
                ACT(stage_s, psT[:, 0:256].rearrange("p (a b) -> p a b", b=64), AF.Copy, [bpsT], [bstage_s])
                LD(OTd[g * 4:g * 4 + 4, :, NM:NM + 64].rearrange("j p t -> p j t"), stage_s, [], bstage_s, R=[bstage_s])

            for g in range(8):
                for (lo, c0, n) in ((0, OX + g * 512, 512), (512, OB + g * 128, 128), (640, OC + g * 128, 128),
                                    (768, OK_ + g * 256, 256), (1024, OV + g * 512, 512)):
                    o, i_ = wcols(WA, c0, lo, n)
                    LDC(o, i_, [bWA], bWA)
                if main:
                    for (lo, c0, n) in ((0, OZ + g * 512, 512), (512, OQ + g * 256, 256), (768, OG + g * 512, 512)):
                        o, i_ = wcols(WB, c0, lo, n)
                        LDC(o, i_, [bWB], bWB)
                    LD(normw[:], normw_d[:, g * 512:(g + 1) * 512], [bnw], bnw)
                if main:
                    LD(ST[:], STs_d[g], [bST], bST)
                    LD(RST[:], STr_d[g].rearrange("p (a b) -> p a b", b=512), [bRST], bRST)
                    CP(raw[:, :, 0:3], convtail[:, g * 6:(g + 1) * 6, :], [bct], [braw])
                else:
                    P.op("dve", lambda e: e.memset(ST[:], 0.0), [], [bST])
                    P.op("dve", lambda e: e.memset(RST[:], 0.0), [], [bRST])
                    P.op("dve", lambda e: e.memset(raw[:, :, 0:3], 0.0), [], [braw])
                ACT(STb[:], ST[:], AF.Copy, [bST], [bSTb])
                ACT(RSTb[:], RST[:], AF.Copy, [bRST], [bRSTb])
                cblk = [g * 4 + 0, g * 4 + 1, g * 4 + 2, g * 4 + 3, 32 + g, 40 + g]
                nblk = 6

                for sc in range(ntok // SC):
                    T0 = sc * SC
                    LD(cosT[:], cos_d[:, T0:T0 + SC], [bcos], bcos)
                    LD(sinT[:], sin_d[:, T0:T0 + SC], [bsin], bsin)
                    for bi in range(nblk):
                        W_, bW_, lo = WA, bWA, bi * 128
                        pt, bpt = mmbank()
                        for kc in range(16):
                            MM(pt[:, 0:SC], W_[:, kc, lo:lo + 128], hT[:, kc, T0:T0 + SC], kc == 0, kc == 15,
                               [bW_, bhT], [bpt])
                        ACT(raw[:, bi, 3:SC + 3], pt[:, 0:SC], AF.Copy, [bpt], [braw])
                    for bi in range(nblk):
                        cb = cblk[bi]
                        TS(acc[:, 0:SC], raw[:, bi, 3:SC + 3], convw[:, cb, 3:4], convb[:, cb:cb + 1], ALU.mult, ALU.add,
                           [braw, bcw, bcb], [bacc])
                        for tap in (2, 1, 0):
                            STT(acc[:, 0:SC], raw[:, bi, tap:tap + SC], convw[:, cb, tap:tap + 1], acc[:, 0:SC], ALU.mult,
                                ALU.add, [braw, bcw, bacc], [bacc])
                        ACT(cT[:, bi, :], acc[:, 0:SC], AF.Silu, [bacc], [bcT])
                    CP(raw[:, :, 0:3], raw[:, :, SC:SC + 3], [braw], [braw])
                    for which in ((0, 1) if main else (1,)):
                        W_, bW_ = (WB, bWB) if which == 0 else (WA, bWA)
                        p0, bp0 = ps[0]
                        p1, bp1 = ps[1]
                        for hf, (pt, bpt) in enumerate(((p0, bp0), (p1, bp1))):
                            lo = (512 if which == 0 else 768) + hf * 128
                            for kc in range(16):
                                MM(pt[:, 0:SC], W_[:, kc, lo:lo + 128], hT[:, kc, T0:T0 + SC], kc == 0, kc == 15,
                                   [bW_, bhT], [bpt])
                        TT(r1[:, 0:SC], p0[:, 0:SC], cosT[:], ALU.mult, [bp0, bcos], [br1])
                        TT(r2[:, 0:SC], p1[:, 0:SC], sinT[:], ALU.mult, [bp1, bsin], [br2])
                        TT(qkT[:, which * 2, :], r1[:, 0:SC], r2[:, 0:SC], ALU.subtract, [br1, br2], [bqkT])
                        TT(r1[:, 0:SC], p1[:, 0:SC], cosT[:], ALU.mult, [bp1, bcos], [br1])
                        TT(r2[:, 0:SC], p0[:, 0:SC], sinT[:], ALU.mult, [bp0, bsin], [br2])
                        TT(qkT[:, which * 2 + 1, :], r1[:, 0:SC], r2[:, 0:SC], ALU.add, [br1, br2], [bqkT])

                    for c in range(SC // 128):
                        lc = c * 128
                        tk = T0 + lc
                        for kc in range(16):
                            MM(psM[:, 0:8], hT[:, kc, tk:tk + 128], Wdt[:, kc, g * 8:(g + 1) * 8], kc == 0, kc == 15,
                               [bhT, bWdt], [bpsM])
                        TT(sm[:, 0, :], psM[:, 0:8], small[:, 0, g * 8:(g + 1) * 8], ALU.add, [bpsM, bsmall], [bsm])
                        ACT(sm[:, 0, :], sm[:, 0, :], AF.Exp, [bsm], [bsm])
                        ACT(sm[:, 1, :], sm[:, 0, :], AF.Ln, [bsm], [bsm], bias=1.0)
                        TT(sm[:, 2, :], sm[:, 1, :], small[:, 1, g * 8:(g + 1) * 8], ALU.mult, [bsm, bsmall], [bsm])
                        MM(psM[:, 8:16], cst("incl"), sm[:, 2, :], True, True, [bC, bsm], [bpsM])
                        MM(psM[:, 16:24], cst("ones"), sm[:, 2, :], True, True, [bC, bsm], [bpsM])
                        ACT(sm[:, 3, :], psM[:, 8:16], AF.Copy, [bpsM], [bsm])
                        TT(sm[:, 4, :], psM[:, 16:24], sm[:, 3, :], ALU.subtract, [bpsM, bsm], [bsm])
                        ACT(sm[:, 5, :], sm[:, 4, :], AF.Exp, [bsm], [bsm])
                        ACT(sm[:, 6, :], sm[:, 3, :], AF.Exp, [bsm], [bsm])
                        ACT(sm[:, 7, :], psM[:, 16:24], AF.Exp, [bpsM], [bsm])
                        TT(sm[:, 8, :], sm[:, 1, :], sm[:, 5, :], ALU.mult, [bsm], [bsm])
                        for j in range(4):
                            TR(psT[:, j * 128:(j + 1) * 128], cT[:, j, lc:lc + 128], identb[:], [bcT, bidb], [bpsT])
                        xs3 = psT[:, 0:512].rearrange("p (r q) -> p r q", q=64)

                        def b8(row):
                            return sm[:, row, :].unsqueeze(2).broadcast_to([128, 8, 64])
                        TT(xdd[:].rearrange("p (r q) -> p r q", q=64), xs3, b8(8), ALU.mult, [bpsT, bsm], [bxdd])
                        if main:
                            TT(xdt[:].rearrange("p (r q) -> p r q", q=64), xs3, b8(1), ALU.mult, [bpsT, bsm], [bxdt])
                            dskb = small[:, 2, g * 8:(g + 1) * 8].unsqueeze(2).broadcast_to([128, 8, 64])
                            TT(xsk[:].rearrange("p (r q) -> p r q", q=64), xs3, dskb, ALU.mult, [bpsT, bsmall], [bxsk])
                        TR(psT[:, 0:128], cT[:, 4, lc:lc + 128], identb[:], [bcT, bidb], [bpsT])
                        ACT(Btok[:], psT[:, 0:128], AF.Copy, [bpsT], [bBtok])
                        if main:
                            MM(psM[:, 128:256], cT[:, 4, lc:lc + 128], cT[:, 5, lc:lc + 128], True, True, [bcT], [bpsM])
                            TT(qkm[:], psM[:, 128:256], cst("causal"), ALU.mult, [bpsM, bC], [bqkm])
                            TT(Ld[:], cst("strict").unsqueeze(1).broadcast_to([128, 8, 128]),
                               sm[:, 2, :].unsqueeze(2).broadcast_to([128, 8, 128]), ALU.mult, [bC, bsm], [bLd])
                            p2, bp2 = ps[2]
                            for hh in range(2):
                                for r in range(4):
                                    MM(p2[:, r * 128:(r + 1) * 128], Ld[:, hh * 4 + r, :], cst("incl"), True, True,
                                       [bLd, bC], [bp2])
                                ACT(segT[:], p2[:, :], AF.Exp, [bp2], [bsegT])
                                TT(MT[:, hh * 4:hh * 4 + 4, :], segT[:].rearrange("p (r l) -> p r l", l=128),
                                   qkm[:].unsqueeze(1).broadcast_to([128, 4, 128]), ALU.mult, [bsegT, bqkm], [bMT])
                            p3, bp3 = ps[3]
                            for r in range(8):
                                MM(p3[:, r * 64:(r + 1) * 64], MT[:, r, :], xdt[:, r * 64:(r + 1) * 64], True, True,
                                   [bMT, bxdt], [bp3])
                            p4, bp4 = ps[4]
                            MM(p4[:, :], cT[:, 5, lc:lc + 128], STb[:], True, True, [bcT, bSTb], [bp4])
                            pz, bpz = mmbank()
                            for kc in range(16):
                                MM(pz[:, :], hT[:, kc, tk:tk + 128], WB[:, kc, 0:512], kc == 0, kc == 15, [bhT, bWB], [bpz])
                            ACT(zs[:], pz[:, :], AF.Silu, [bpz], [bzs])
                            TT(y1[:].rearrange("p (r q) -> p r q", q=64), p4[:, :].rearrange("p (r q) -> p r q", q=64),
                               b8(6), ALU.mult, [bp4, bsm], [by1])
                            TT(y2[:], y1[:], p3[:, :], ALU.add, [by1, bp3], [by2])
                            TT(y1[:], y2[:], xsk[:], ALU.add, [by2, bxsk], [by1])
                            TT(y2[:], y1[:], zs[:], ALU.mult, [by1, bzs], [by2])
                            ZERO(rs[:, 0:1], brs)
                            ACT(y1[:], y2[:], AF.Square, [by2], [by1, brs], accum_out=rs[:, 0:1])
                            TS(rs[:, 1:2], rs[:, 0:1], 1.0 / 512, EPS, ALU.mult, ALU.add, [brs], [brs])
                            ACT(rs[:, 1:2], rs[:, 1:2], AF.Sqrt, [brs], [brs])
                            P.op("dve", lambda e: e.reciprocal(out=rs[:, 1:2], in_=rs[:, 1:2]), [brs], [brs])
                            STT(ybar[:], y2[:], rs[:, 1:2], normw[:], ALU.mult, ALU.mult, [by2, brs, bnw], [bybar])
                            for j in range(4):
                                TR(psT[:, j * 128:(j + 1) * 128], ybar[:, j * 128:(j + 1) * 128], identb[:],
                                   [bybar, bidb], [bpsT])
                            ACT(stage[:], psT[:, 0:512].rearrange("p (a b) -> p a b", b=128), AF.Copy, [bpsT], [bstage])
                            LD(YTd[g * 4:g * 4 + 4, :, tk:tk + 128].rearrange("j p t -> p j t"), stage[:], [], bstage,
                               R=[bstage])
                        p5, bp5 = ps[5]
                        MM(p5[:, :], Btok[:], xdd[:], True, True, [bBtok, bxdd], [bp5])
                        TT(tmpS[:].rearrange("p (r q) -> p r q", q=64), ST[:].rearrange("p (r q) -> p r q", q=64),
                           b8(7), ALU.mult, [bST, bsm], [btmpS])
                        TT(ST[:], tmpS[:], p5[:, :], ALU.add, [btmpS, bp5], [bST])
                        ACT(STb[:], ST[:], AF.Copy, [bST], [bSTb])
                        pv, bpv = mmbank()
                        for kc in range(16):
                            MM(pv[:, :], hT[:, kc, tk:tk + 128], WA[:, kc, 1024:1536], kc == 0, kc == 15, [bhT, bWA], [bpv])
                        ACT(vb[:], pv[:, :], AF.Copy, [bpv], [bvb])
                        for hf in range(2):
                            TR(psT[:, hf * 128:(hf + 1) * 128], qkT[:, 2 + hf, lc:lc + 128], identb[:], [bqkT, bidb], [bpsT])
                        o_, n_ = CONST_LAYOUT["kds"]
                        TS(kd[:], psT[:, 0:256], C[:, o_ + g:o_ + g + 1], None, ALU.mult, None, [bpsT, bC], [bkd])
                        if main:
                            for hf in range(2):
                                MM(psM[:, 128:256], qkT[:, 2 + hf, lc:lc + 128], qkT[:, hf, lc:lc + 128], hf == 0, hf == 1,
                                   [bqkT], [bpsM])
                            oD, _ = CONST_LAYOUT["retD"]
                            TT(MTr[:], psM[:, 128:256], C[:, oD + g * 128:oD + (g + 1) * 128], ALU.mult, [bpsM, bC], [bMTr])
                            oG, _ = CONST_LAYOUT["gpow"]
                            TT(qs[:], qkT[:, 0:2, lc:lc + 128],
                               C[:, oG + g * 128:oG + (g + 1) * 128].unsqueeze(1).broadcast_to([128, 2, 128]),
                               ALU.mult, [bqkT, bC], [bqs])
                            p3, bp3 = ps[3]
                            MM(p3[:, :], MTr[:], vb[:], True, False, [bMTr, bvb], [bp3])
                            for hf in range(2):
                                MM(p3[:, :], qs[:, hf, :], RSTb[:, hf, :], False, hf == 1, [bqs, bRSTb], [bp3])
                            pg, bpg = mmbank()
                            for kc in range(16):
                                MM(pg[:, :], hT[:, kc, tk:tk + 128], WB[:, kc, 768:1280], kc == 0, kc == 15, [bhT, bWB], [bpg])
                            ACT(zs[:], pg[:, :], AF.Silu, [bpg], [bzs])
                            ZERO(rs[:, 0:1], brs)
                            ACT(y1[:], p3[:, :], AF.Square, [bp3], [by1, brs], accum_out=rs[:, 0:1])
                            TS(rs[:, 1:2], rs[:, 0:1], 1.0 / 512, EPS, ALU.mult, ALU.add, [brs], [brs])
                            ACT(rs[:, 1:2], rs[:, 1:2], AF.Sqrt, [brs], [brs])
                            P.op("dve", lambda e: e.reciprocal(out=rs[:, 1:2], in_=rs[:, 1:2]), [brs], [brs])
                            STT(ybar[:], p3[:, :], rs[:, 1:2], zs[:], ALU.mult, ALU.mult, [bp3, brs, bzs], [bybar])
                            for j in range(4):
                                TR(psT[:, j * 128:(j + 1) * 128], ybar[:, j * 128:(j + 1) * 128], identb[:],
                                   [bybar, bidb], [bpsT])
                            ACT(stage[:], psT[:, 0:512].rearrange("p (a b) -> p a b", b=128), AF.Copy, [bpsT], [bstage])
                            LD(OTd[g * 4:g * 4 + 4, :, tk:tk + 128].rearrange("j p t -> p j t"), stage[:], [], bstage,
                               R=[bstage])
                        gQ = float(_gammas()[g] ** 128)
                        for hf in range(2):
                            pq, bpq = ps[5]
                            MM(pq[:, :], kd[:, hf * 128:(hf + 1) * 128], vb[:], True, True, [bkd, bvb], [bpq])
                            STT(RST[:, hf, :], RST[:, hf, :], gQ, pq[:, :], ALU.mult, ALU.add, [bRST, bpq], [bRST])
                        ACT(RSTb[:], RST[:], AF.Copy, [bRST], [bRSTb])
                if main:
                    CP(convout[:, g * 4:g * 4 + 4, :], raw[:, 0:4, 0:3], [braw], [bco])
                    CP(convout[:, 32 + g, :], raw[:, 4, 0:3], [braw], [bco])
                    CP(convout[:, 40 + g, :], raw[:, 5, 0:3], [braw], [bco])
                if main and WITH_SAMPLE:
                    sample_group(g, cblk)
                if not main:
                    TS(ST[:], ST[:], flag[:, 0:1], None, ALU.mult, None, [bST, bflag], [bST])
                    TS(RST[:], RST[:], flag[:, 0:1], None, ALU.mult, None, [bRST, bflag], [bRST])
                    LD(STs_d[g], ST[:], [], bST, R=[bST])
                    LD(STr_d[g].rearrange("p (a b) -> p a b", b=512), RST[:], [], bRST, R=[bRST])
                    TS(convtail[:, g * 6:(g + 1) * 6, :], raw[:, :, 0:3], flag[:, 0:1], None, ALU.mult, None,
                       [braw, bflag], [bct])
                else:
                    identf = cst("ident")
                    stf, bstf = Ld[:].rearrange("p r l -> p (r l)").rearrange("p (a b) -> p a b", b=256), bLd
                    p2, bp2 = ps[2]
                    for j in range(4):
                        TR(p2[:, j * 128:(j + 1) * 128], ST[:, j * 128:(j + 1) * 128], identf, [bST, bC], [bp2])
                    ACT(stf[:, :, 0:128], p2[:, :].rearrange("p (a b) -> p a b", b=128), AF.Copy, [bp2], [bstf])
                    LD(ssmp[g * 512:(g + 1) * 512, :].rearrange("(j p) n -> p j n", p=128), stf[:, :, 0:128], [], bstf,
                       R=[bstf])
                    for hf in range(2):
                        for j in range(4):
                            TR(p2[:, j * 128:(j + 1) * 128], RST[:, hf, j * 128:(j + 1) * 128], identf, [bRST, bC], [bp2])
                        ACT(stf[:, :, hf * 128:(hf + 1) * 128], p2[:, :].rearrange("p (a b) -> p a b", b=128), AF.Copy,
                            [bp2], [bstf])
                    LD(retp[g * 512:(g + 1) * 512, :].rearrange("(j p) k -> p j k", p=128), stf, [], bstf, R=[bstf])

        def phase2(t0, n, hT, bhT, hoff, xsrc, xoff, ydst):
            with contextlib.ExitStack() as s2:
                mergedT, bmg = sbuf(s2, "mergedT", [128, 16, 512], BF16)
                with contextlib.ExitStack() as s2a:
                    YT, bYT = sbuf(s2a, "YT", [128, 32, 512], BF16)
                    OT, bOT = sbuf(s2a, "OT", [128, 32, 512], BF16)
                    Ws, bWs = sbuf(s2a, "Wssm_j", [128, 32, 128], BF16)
                    Wr, bWr = sbuf(s2a, "Wret_j", [128, 32, 128], BF16)
                    Wg1, bWg1 = sbuf(s2a, "Wgs_j", [128, 16, 128], BF16)
                    Wg2, bWg2 = sbuf(s2a, "Wgr_j", [128, 16, 128], BF16)
                    sg, bsg = sbuf(s2a, "sg", [128, 512])
                    m1, bm1 = sbuf(s2a, "m1", [128, 512])
                    m2, bm2 = sbuf(s2a, "m2", [128, 512])
                    LD(YT[:, :, 0:n], YTd[:, :, t0:t0 + n].rearrange("k p t -> p k t"), [bYT], bYT)
                    LD(OT[:, :, 0:n], OTd[:, :, t0:t0 + n].rearrange("k p t -> p k t"), [bOT], bOT)
                    for j in range(16):
                        LDC(Ws[:], wssm[:, j * 128:(j + 1) * 128].rearrange("(kb p) n -> p kb n", p=128), [bWs], bWs)
                        LDC(Wr[:], wret[:, j * 128:(j + 1) * 128].rearrange("(kb p) n -> p kb n", p=128), [bWr], bWr)
                        LDC(Wg1[:], w_in[:, OGS + j * 128:OGS + (j + 1) * 128].rearrange("(kb p) n -> p kb n", p=128),
                            [bWg1], bWg1)
                        LDC(Wg2[:], w_in[:, OGR + j * 128:OGR + (j + 1) * 128].rearrange("(kb p) n -> p kb n", p=128),
                            [bWg2], bWg2)
                        for (Wp, bWp, Xp, bXp, Wg, bWg, pa, pb, mo, bmo) in (
                                (Ws, bWs, YT, bYT, Wg1, bWg1, ps[0], ps[1], m1, bm1),
                                (Wr, bWr, OT, bOT, Wg2, bWg2, ps[2], ps[3], m2, bm2)):
                            for kb in range(32):
                                MM(pa[0][:, 0:n], Wp[:, kb, :], Xp[:, kb, 0:n], kb == 0, kb == 31, [bWp, bXp], [pa[1]])
                            for kc in range(16):
                                MM(pb[0][:, 0:n], Wg[:, kc, :], hT[:, kc, hoff:hoff + n], kc == 0, kc == 15, [bWg, bhT], [pb[1]])
                            ACT(sg[:, 0:n], pb[0][:, 0:n], AF.Sigmoid, [pb[1]], [bsg])
                            TT(mo[:, 0:n], sg[:, 0:n], pa[0][:, 0:n], ALU.mult, [bsg, pa[1]], [bmo])
                        TT(mergedT[:, j, 0:n], m1[:, 0:n], m2[:, 0:n], ALU.add, [bm1, bm2], [bmg])
                P.barrier()
                with contextlib.ExitStack() as s2b:
                    Wo, bWo = sbuf(s2b, "Wo", [128, 16, D], BF16)
                    xt, bxt = sbuf(s2b, "p2_xt", [128, D])
                    yo, byo = sbuf(s2b, "p2_yo", [128, D])
                    wpost, bwpost = sbuf(s2b, "p2_wpost", [128, D])
                    ss4, bss4 = sbuf(s2b, "ss4", [128, 8])
                    junk, bjunk = sbuf(s2b, "junk", [128, 512])
                    LD(wpost[:], wpost_d[:, :], [bwpost], bwpost)
                    for nb in range(4):
                        LDC(Wo[:, :, nb * 512:(nb + 1) * 512],
                            wout[:, nb * 512:(nb + 1) * 512].rearrange("(kb p) n -> p kb n", p=128), [bWo], bWo)
                    for c0 in range(0, n, 128):
                        rows = min(128, n - c0)
                        LD(xt[0:rows, :], xsrc[xoff + c0:xoff + c0 + rows, :], [bxt], bxt)
                        ZERO(ss4[:, 0:4], bss4)
                        for nb in range(4):
                            pt, bpt = ps[nb]
                            for j in range(16):
                                MM(pt[0:rows, :], mergedT[:, j, c0:c0 + rows], Wo[:, j, nb * 512:(nb + 1) * 512], j == 0,
                                   j == 15, [bmg, bWo], [bpt])
                            ACT(junk[0:rows, :], pt[0:rows, :], AF.Square, [bpt], [bjunk, bss4], accum_out=ss4[0:rows, nb:nb + 1])
                        P.op("dve", lambda e, rows=rows: e.reduce_sum(out=ss4[0:rows, 4:5], in_=ss4[0:rows, 0:4], axis=AX.X),
                             [bss4], [bss4])
                        TS(ss4[0:rows, 5:6], ss4[0:rows, 4:5], 1.0 / D, EPS, ALU.mult, ALU.add, [bss4], [bss4])
                        ACT(ss4[0:rows, 5:6], ss4[0:rows, 5:6], AF.Sqrt, [bss4], [bss4])
                        P.op("dve", lambda e, rows=rows: e.reciprocal(out=ss4[0:rows, 5:6], in_=ss4[0:rows, 5:6]), [bss4], [bss4])
                        for nb in range(4):
                            pt, bpt = ps[nb]
                            STT(yo[0:rows, nb * 512:(nb + 1) * 512], pt[0:rows, :], ss4[0:rows, 5:6],
                                wpost[0:rows, nb * 512:(nb + 1) * 512], ALU.mult, ALU.mult, [bpt, bss4, bwpost], [byo])
                        TT(yo[0:rows, :], yo[0:rows, :], xt[0:rows, :], ALU.add, [byo, bxt], [byo])
                        LD(ydst[xoff + c0:xoff + c0 + rows, :], yo[0:rows, :], [], byo, R=[byo])
                P.barrier()

        with contextlib.ExitStack() as sA:
            hTp, bhTp = sbuf(sA, "hTp", [128, 16, NP_], BF16)
            with contextlib.ExitStack() as s0:
                phase0(s0, xp, NP_, hTp, bhTp)
            P.barrier()
            with contextlib.ExitStack() as s1:
                phase1(s1, hTp, bhTp, NP_, cosP_d, sinP_d, "pre")
            P.barrier()
        with contextlib.ExitStack() as s0:
            phase0(s0, xm, NM, hTm, bhTm)
            phase0(s0, xs_, NS, hTs, bhTs)
        P.barrier()
        with contextlib.ExitStack() as s1:
            phase1(s1, hTm, bhTm, NM, cosM_d, sinM_d, "main")
        for cb in range(48):
            LDNC(convp[:, cb * 128:(cb + 1) * 128].rearrange("t p -> p t"), convout[:, cb, :], [], bco, R=[bco])
        P.barrier()
        phase2(0, 512, hTm, bhTm, 0, xm, 0, ym)
        phase2(512, 512, hTm, bhTm, 512, xm, 512, ym)
        if WITH_SAMPLE:
            phase2(NM, NS, hTs, bhTs, 0, xs_, 0, ys)
        P.barrier()
        blk = st0.enter_context(nc.Block())
        P.emit(blk)
    return nc


def kernel(x_prompt, x_sample, cache_conv, state_ssm, state_ret, w_pre, w_in, conv_w, conv_b, dt_bias, a_log,
           d_skip, ssm_norm_w, w_proj_ssm, w_proj_ret, w_out, w_post):
    f = lambda a: np.ascontiguousarray(np.asarray(a, dtype=np.float32))
    x_prompt, x_sample = f(x_prompt), f(x_sample)
    consts = make_consts()
    nc = build_program()
    bc = lambda v: np.ascontiguousarray(np.broadcast_to(f(v).reshape(1, -1), (128, f(v).size)))
    shared = {
        "consts": consts, "w_in": f(w_in)[0], "wssm": f(w_proj_ssm)[0], "wret": f(w_proj_ret)[0], "wout": f(w_out)[0],
        "wpre_b": bc(w_pre), "wpost_b": bc(w_post), "normw_b": bc(ssm_norm_w),
        "convw": np.ascontiguousarray(f(conv_w)[0].reshape(4, 48, 128).transpose(2, 1, 0)),
        "convb": np.ascontiguousarray(f(conv_b)[0].reshape(48, 128).T),
        "dtb_b": bc(dt_bias), "alog_b": bc(a_log), "dsk_b": bc(d_skip),
    }
    cS, sS = rope_tables(16384 + (np.arange(64) // 16))
    in_maps = []
    for c in range(8):
        b, hf = c // 2, c % 2
        cM, sM = rope_tables(hf * 1024 + np.arange(1024))
        cP, sP = rope_tables(np.arange(1024))
        sl = slice(16 * c, 16 * c + 16)
        m = dict(shared)
        m.update({
            "xm": np.ascontiguousarray(x_prompt[b, hf * 1024:(hf + 1) * 1024]),
            "xp": np.ascontiguousarray(x_prompt[b, 0:1024]),
            "xs": np.ascontiguousarray(x_sample[sl].transpose(1, 0, 2).reshape(64, D)),
            "flag": np.full((128, 1), float(hf), np.float32),
            "cosM": cM, "sinM": sM, "cosP": cP, "sinP": sP, "cosS": cS, "sinS": sS,
            "cconv": np.ascontiguousarray(f(cache_conv)[0, sl].transpose(1, 0, 2).reshape(48, 6144)),
            "sssm": np.ascontiguousarray(f(state_ssm)[0, sl].reshape(16, 4096, 128)),
            "sret": np.ascontiguousarray(f(state_ret)[0, sl].reshape(16, 4096, 256)),
        })
        in_maps.append(m)
    res = run_bass_kernel_spmd(nc, in_maps, core_ids=list(range(8)))
    R = res.results
    y_prompt = np.zeros((4, 2048, D), np.float32)
    y_sample = np.zeros((128, 4, D), np.float32)
    conv_p = np.zeros((1, 4, 3, 6144), np.float32)
    ssm_p = np.zeros((1, 4, 64, 64, 128), np.float32)
    ret_p = np.zeros((1, 4, 8, 512, 256), np.float32)
    conv_s = np.zeros((1, 128, 3, 6144), np.float32)
    ssm_s = np.zeros((1, 128, 64, 64, 128), np.float32)
    ret_s = np.zeros((1, 128, 8, 512, 256), np.float32)
    for c in range(8):
        b, hf = c // 2, c % 2
        r = R[c]
        y_prompt[b, hf * 1024:(hf + 1) * 1024] = r["ym"]
        y_sample[16 * c:16 * c + 16] = r["ys"].reshape(4, 16, D).transpose(1, 0, 2)
        if hf == 1:
            conv_p[0, b] = r["convp"]
            ssm_p[0, b] = r["ssmp"].reshape(64, 64, 128)
            ret_p[0, b] = r["retp"].reshape(8, 512, 256)
        conv_s[0, 16 * c:16 * c + 16] = r["convs"]
        ssm_s[0, 16 * c:16 * c + 16] = r["ssms"].reshape(16, 64, 64, 128)
        ret_s[0, 16 * c:16 * c + 16] = r["rets"].reshape(16, 8, 512, 256)
    return (y_prompt, y_sample, conv_p, ssm_p, ret_p, conv_s, ssm_s, ret_s)
```

```python
import contextlib
import math
import numpy as np
import concourse.bass as bass
import concourse.mybir as mybir
from concourse.bass_utils import run_bass_kernel_spmd

F32 = mybir.dt.float32
BF16 = mybir.dt.bfloat16
ALU = mybir.AluOpType
AF = mybir.ActivationFunctionType
AX = mybir.AxisListType

D = 2048
NM = 1024
NP_ = 1024
NS = 64
NT = NM + NS
IN_DIM = 26688
OZ, OX, OB, OC, ODT, OQ, OK_, OV, OG, OGS, OGR = 0, 4096, 8192, 9216, 10240, 10304, 12352, 14400, 18496, 22592, 24640
EPS = 1e-6
SC = 256
import os
WITH_SAMPLE = os.environ.get('K_NOSAMPLE') != '1'


class Buf:
    __slots__ = ("name", "w", "r", "dsem", "dcnt", "psum")

    def __init__(self, name):
        self.name = name
        self.psum = name.startswith("ps")
        self.w = []
        self.r = []
        self.dsem = None
        self.dcnt = 0


class Plan:
    ENGS = ("pe", "act", "dve", "pool", "sp")
    SEM_LIMIT = 30000
    SAME_ENGINE_SYNC = {"pe": False, "act": True, "dve": True, "pool": True, "sp": True}

    def __init__(self, nc, stack):
        self.nc = nc
        self.stack = stack
        self.streams = {e: [] for e in self.ENGS}
        self.esem = {}
        self.ecnt = {e: 0 for e in self.ENGS}
        self.seen = {e: {} for e in self.ENGS}
        self.nsem = 0
        for e in self.ENGS:
            self.esem[e] = self.new_sem("e_" + e)
        self.bufs = []
        self.free_ctrs = []
        self.pending_ctrs = []

    def release(self, b):
        if b.dsem is not None:
            if b.dsem[2] == "sp":
                self.pending_ctrs.append(b.dsem)
            b.dsem = None

    def new_sem(self, name):
        self.nsem += 1
        return self.stack.enter_context(self.nc.semaphore(f"{name}_{self.nsem}"))

    def buf(self, name):
        b = Buf(name)
        self.bufs.append(b)
        return b

    def _deps(self, eng, reads, writes, extra=()):
        need = {}

        def add(lst):
            for (s, v) in lst:
                k = id(s)
                if k not in need or need[k][1] < v:
                    need[k] = (s, v)
        for b in reads:
            add(b.w)
            if b.psum:
                add(b.r)
        for b in writes:
            add(b.w)
            add(b.r)
        add(extra)
        out = []
        seen = self.seen[eng]
        own = id(self.esem[eng])
        for k, (s, v) in need.items():
            if k == own and not self.SAME_ENGINE_SYNC[eng]:
                continue
            if seen.get(k, -1) >= v:
                continue
            seen[k] = v
            out.append((s, v))
        return out

    def op(self, eng, fn, reads=(), writes=()):
        waits = self._deps(eng, reads, writes)
        if self.ecnt[eng] >= self.SEM_LIMIT:
            self.esem[eng] = self.new_sem("e_" + eng)
            self.ecnt[eng] = 0
        self.ecnt[eng] += 1
        tok = (self.esem[eng], self.ecnt[eng])
        self.streams[eng].append((waits, fn, (self.esem[eng], 1)))
        for b in reads:
            b.r.append(tok)
        for b in writes:
            b.w = [tok]
            b.r = []
        return tok

    def dma(self, eng, fn, reads=(), writes=(), owner=None):
        waits = self._deps(eng, reads, writes)
        ow = owner
        if ow.dsem is None:
            if eng == "sp" and self.free_ctrs:
                ow.dsem = self.free_ctrs.pop()
            else:
                ow.dsem = [self.new_sem("d"), 0, eng]
        ow.dsem[1] += 16
        tok = (ow.dsem[0], ow.dsem[1])
        self.streams[eng].append((waits, fn, (ow.dsem[0], 16)))
        for b in reads:
            b.r.append(tok)
        for b in writes:
            b.w = [tok]
            b.r = []
        return tok

    def barrier(self):
        toks = []
        for b in self.bufs:
            toks += b.w
            toks += b.r
        for e in self.ENGS:
            if self.ecnt[e] > 0:
                toks.append((self.esem[e], self.ecnt[e]))
        for e in self.ENGS:
            waits = self._deps(e, (), (), extra=toks)
            self.streams[e].append((waits, None, None))
        for b in self.bufs:
            b.w = []
            b.r = []
        self.free_ctrs += self.pending_ctrs
        self.pending_ctrs = []

    def emit(self, block):
        def run(stream):
            def body(e):
                for waits, fn, inc in stream:
                    for (s, v) in waits:
                        e.wait_ge(s, v)
                    if fn is not None:
                        fn(e).then_inc(inc[0], inc[1])
            return body
        block.tensor(run(self.streams["pe"]))
        block.scalar(run(self.streams["act"]))
        block.vector(run(self.streams["dve"]))
        block.gpsimd(run(self.streams["pool"]))
        block.sync(run(self.streams["sp"]))


def _gammas():
    return (1.0 - np.exp2(-5.0 - np.arange(8, dtype=np.float64)))


CONST_LAYOUT = {}


def make_consts():
    cols = []
    off = [0]

    def put(name, arr):
        a = np.zeros((128, arr.shape[1]), np.float32)
        a[:arr.shape[0]] = arr
        CONST_LAYOUT[name] = (off[0], arr.shape[1])
        off[0] += arr.shape[1]
        cols.append(a)
    i = np.arange(128)
    put("ident", np.eye(128, dtype=np.float32))
    put("causal", (i[:, None] <= i[None, :]).astype(np.float32))
    put("strict", (i[:, None] > i[None, :]).astype(np.float32))
    put("ones", np.ones((128, 128), np.float32))
    g = _gammas()
    dk = 256 ** -0.5
    retD = np.zeros((128, 8 * 128), np.float64)
    gpow = np.zeros((128, 8 * 128), np.float64)
    kds = np.zeros((128, 8), np.float64)
    for h in range(8):
        dlt = (i[None, :] - i[:, None])
        m = np.where(dlt >= 0, g[h] ** np.maximum(dlt, 0), 0.0) * dk
        retD[:, h * 128:(h + 1) * 128] = m
        gpow[:, h * 128:(h + 1) * 128] = (g[h] ** (i + 1))[None, :]
        kds[:, h] = g[h] ** (127 - i) * dk
    put("kds", kds.astype(np.float32))
    j = np.arange(64)
    tt, sq = j // 16, j % 16
    same = (sq[:, None] == sq[None, :])
    put("causal_s", (same & (tt[:, None] <= tt[None, :])).astype(np.float32))
    put("strict_s", (same & (tt[:, None] > tt[None, :])).astype(np.float32))
    put("incl_s", (same & (tt[:, None] <= tt[None, :])).astype(np.float32))
    put("seqmask", (sq[:, None] == np.arange(16)[None, :]).astype(np.float32))
    put("same_s", same.astype(np.float32))
    retDs = np.zeros((64, 8 * 64), np.float64)
    gpows = np.zeros((128, 8 * 64), np.float64)
    kdss = np.zeros((64, 8), np.float64)
    for h in range(8):
        dlt = tt[None, :] - tt[:, None]
        m = np.where(same & (dlt >= 0), g[h] ** np.maximum(dlt, 0), 0.0) * dk
        retDs[:, h * 64:(h + 1) * 64] = m
        gpows[:, h * 64:(h + 1) * 64] = (g[h] ** (tt + 1))[None, :]
        kdss[:, h] = g[h] ** (3 - tt) * dk
    put("kds_s", kdss.astype(np.float32))
    cgrp = np.zeros((8, 128, 384), np.float32)
    for h in range(8):
        cgrp[h, :, 0:128] = retD[:, h * 128:(h + 1) * 128]
        cgrp[h, :, 128:256] = gpow[:, h * 128:(h + 1) * 128]
        cgrp[h, 0:64, 256:320] = retDs[:, h * 64:(h + 1) * 64]
        cgrp[h, :, 320:384] = gpows[:, h * 64:(h + 1) * 64]
    return np.ascontiguousarray(np.concatenate(cols, axis=1)), cgrp


def rope_tables(pos):
    half = 128
    inv = (1.0 / (10000.0 ** (np.arange(half, dtype=np.float32) / np.float32(half)))).astype(np.float32)
    ang = pos.astype(np.float32)[None, :] * inv[:, None]
    return np.cos(ang).astype(np.float32), np.sin(ang).astype(np.float32)


def build_program():
    nc = bass.Bass("TRN2", target_bir_lowering=False)
    NCOL = sum(v[1] for v in CONST_LAYOUT.values())

    def din(name, shape, dt=F32):
        return nc.dram_tensor(name, list(shape), dt, kind="ExternalInput").ap()

    def dout(name, shape, dt=F32):
        return nc.dram_tensor(name, list(shape), dt, kind="ExternalOutput").ap()

    xm = din("xm", [NM, D]); xp = din("xp", [NP_, D]); xs_ = din("xs", [NS, D])
    flag_d = din("flag", [128, 1]); consts_d = din("consts", [128, NCOL]); cgrp_d = din("cgrp", [8, 128, 384])
    cosM_d = din("cosM", [128, NM]); sinM_d = din("sinM", [128, NM])
    cosP_d = din("cosP", [128, NP_]); sinP_d = din("sinP", [128, NP_])
    cosS_d = din("cosS", [128, NS]); sinS_d = din("sinS", [128, NS])
    w_in = din("w_in", [D, IN_DIM]); wssm = din("wssm", [4096, D]); wret = din("wret", [4096, D])
    wout = din("wout", [D, D])
    wpre_d = din("wpre_b", [128, D]); wpost_d = din("wpost_b", [128, D]); normw_d = din("normw_b", [128, 4096])
    convw_d = din("convw", [128, 48, 4]); convb_d = din("convb", [128, 48])
    dtb_d = din("dtb_b", [128, 64]); alog_d = din("alog_b", [128, 64]); dsk_d = din("dsk_b", [128, 64])
    cconv_d = din("cconv", [48, 6144]); sssm_d = din("sssm", [16, 4096, 128]); sret_d = din("sret", [16, 4096, 256])

    ym = dout("ym", [NM, D]); ys = dout("ys", [NS, D])
    convp = dout("convp", [3, 6144]); ssmp = dout("ssmp", [4096, 128]); retp = dout("retp", [4096, 256])
    convs = dout("convs", [16, 3, 6144]); ssms = dout("ssms", [16, 4096, 128]); rets = dout("rets", [16, 4096, 256])

    YTd = nc.dram_tensor("YTd", [32, 128, NT], BF16).ap()
    OTd = nc.dram_tensor("OTd", [32, 128, NT], BF16).ap()
    STs_d = nc.dram_tensor("STs_d", [8, 128, 512], F32).ap()
    STr_d = nc.dram_tensor("STr_d", [8, 128, 1024], F32).ap()

    with contextlib.ExitStack() as st0:
        P = Plan(nc, st0)

        uniq = [0]

        def sbuf(stack, name, shape, dt=F32):
            uniq[0] += 1
            nm = f"{name}_{uniq[0]}"
            t = stack.enter_context(nc.sbuf_tensor(nm, list(shape), dt))
            b = P.buf(nm)
            if stack is not st0:
                stack.callback(P.release, b)
            return t, b

        def ZERO(ap, b):
            P.op("dve", lambda e: e.memset(ap, 0.0), [], [b])

        ps = []
        for i in range(6):
            t = st0.enter_context(nc.psum_tensor(f"ps{i}", [128, 512], F32))
            ps.append((t, P.buf(f"ps{i}")))
        psT, bpsT = st0.enter_context(nc.psum_tensor("psT", [128, 512 if os.environ.get("K_NOHALF") == "1" else 1024], BF16)), P.buf("psT")
        psM, bpsM = st0.enter_context(nc.psum_tensor("psM", [128, 256 if os.environ.get("K_NOHALF") == "1" else 512], F32)), P.buf("psM")

        def MM(out, lhsT, rhs, start, stop, R, W):
            P.op("pe", lambda e: e.matmul(out, lhsT=lhsT, rhs=rhs, start=start, stop=stop), R, W)

        def TR(out, in_, ident, R, W):
            P.op("pe", lambda e: e.transpose(out=out, in_=in_, identity=ident), R, W)

        def ACT(out, in_, func, R, W, **kw):
            P.op("act", lambda e: e.activation(out=out, in_=in_, func=func, **kw), R, W)

        def TT(out, a, b, op, R, W):
            P.op("dve", lambda e: e.tensor_tensor(out=out, in0=a, in1=b, op=op), R, W)

        def TS(out, a, s1, s2, op0, op1, R, W):
            if s2 is None:
                P.op("dve", lambda e: e.tensor_scalar(out=out, in0=a, scalar1=s1, scalar2=None, op0=op0), R, W)
            else:
                P.op("dve", lambda e: e.tensor_scalar(out=out, in0=a, scalar1=s1, scalar2=s2, op0=op0, op1=op1), R, W)

        def STT(out, a, s, b, op0, op1, R, W):
            P.op("dve", lambda e: e.scalar_tensor_tensor(out=out, in0=a, scalar=s, in1=b, op0=op0, op1=op1), R, W)

        def CP(out, in_, R, W):
            P.op("dve", lambda e: e.tensor_copy(out=out, in_=in_), R, W)

        def LD(out, in_, W, owner, R=(), eng="sp"):
            P.dma(eng, lambda e: e.dma_start(out=out, in_=in_), R, W, owner)

        def LDC(out, in_, W, owner, R=()):
            P.dma("pool", lambda e: e.dma_start(out=out, in_=in_), R, W, owner)

        def LDNC(out, in_, W, owner, R=()):
            P.dma("sp", lambda e: e.dma_start(out=out, in_=in_, allow_slow_non_contiguous=True), R, W, owner)

        def rstd_from_ss(rs, brs, ss, n):
            TS(rs, ss, 1.0 / n, EPS, ALU.mult, ALU.add, [brs] if ss.tensor is rs.tensor else [brs], [brs])
            ACT(rs, rs, AF.Sqrt, [brs], [brs])
            P.op("dve", lambda e: e.reciprocal(out=rs, in_=rs), [brs], [brs])

        C, bC = sbuf(st0, "consts", [128, NCOL])
        LD(C[:], consts_d[:, :], [bC], bC)

        def cst(name, rows=128):
            o, n = CONST_LAYOUT[name]
            return C[0:rows, o:o + n]
        identb, bidb = sbuf(st0, "identb", [128, 128], BF16)
        CP(identb[:], cst("ident"), [bC], [bidb])
        flag, bflag = sbuf(st0, "flag", [128, 1])
        LD(flag[:], flag_d[:, :], [bflag], bflag)
        hTm, bhTm = sbuf(st0, "hTm", [128, 16, NM], BF16)
        hTs, bhTs = sbuf(st0, "hTs", [128, 16, NS], BF16)
        convtail, bct = sbuf(st0, "convtail", [128, 48, 3])
        convout, bco = sbuf(st0, "convout", [128, 48, 3])
        small, bsmall = sbuf(st0, "small", [128, 3, 64])
        LD(small[:, 0, :], dtb_d[:, :], [bsmall], bsmall)
        LD(small[:, 1, :], alog_d[:, :], [bsmall], bsmall)
        LD(small[:, 2, :], dsk_d[:, :], [bsmall], bsmall)
        ACT(small[:, 1, :], small[:, 1, :], AF.Exp, [bsmall], [bsmall])
        TS(small[:, 1, :], small[:, 1, :], -1.0, None, ALU.mult, None, [bsmall], [bsmall])
        convw, bcw = sbuf(st0, "convw", [128, 48, 4])
        convb, bcb = sbuf(st0, "convb", [128, 48])
        LD(convw[:], convw_d[:, :, :], [bcw], bcw)
        LD(convb[:], convb_d[:, :], [bcb], bcb)

        def phase0(stack, xsrc, ntok, hT, bhT):
            xtb = [sbuf(stack, f"p0_xt{i}", [128, D]) for i in range(2)]
            hbb = [sbuf(stack, f"p0_hb{i}", [128, D], BF16) for i in range(2)]
            wpre, bwpre = sbuf(stack, "p0_wpre", [128, D])
            st_, bst = sbuf(stack, "p0_st", [128, 2])
            LD(wpre[:], wpre_d[:, :], [bwpre], bwpre)
            for t0 in range(0, ntok, 128):
                rows = min(128, ntok - t0)
                (xt, bxt), (hb, bhb) = xtb[(t0 // 128) % 2], hbb[(t0 // 128) % 2]
                LD(xt[0:rows, :], xsrc[t0:t0 + rows, :], [bxt], bxt)
                ZERO(st_[:, 0:1], bst)
                ACT(hb[0:rows, :], xt[0:rows, :], AF.Square, [bxt], [bhb, bst], accum_out=st_[0:rows, 0:1])
                TS(st_[0:rows, 1:2], st_[0:rows, 0:1], 1.0 / D, EPS, ALU.mult, ALU.add, [bst], [bst])
                ACT(st_[0:rows, 1:2], st_[0:rows, 1:2], AF.Sqrt, [bst], [bst])
                P.op("dve", lambda e, rows=rows: e.reciprocal(out=st_[0:rows, 1:2], in_=st_[0:rows, 1:2]), [bst], [bst])
                STT(hb[0:rows, :], xt[0:rows, :], st_[0:rows, 1:2], wpre[0:rows, :], ALU.mult, ALU.mult,
                    [bxt, bst, bwpre, bhb], [bhb])
                for k0 in range(0, 16, 4):
                    for kk in range(4):
                        kc = k0 + kk
                        TR(psT[:, kk * 128:kk * 128 + rows], hb[0:rows, kc * 128:(kc + 1) * 128],
                           identb[0:rows, 0:rows], [bhb, bidb], [bpsT])
                    src = psT[:, 0:512].rearrange("p (a b) -> p a b", b=128)[:, :, 0:rows]
                    ACT(hT[:, k0:k0 + 4, t0:t0 + rows], src, AF.Copy, [bpsT], [bhT])

        BLK = [False]

        def interleave(gens):
            gens = list(gens)
            seq = os.environ.get("K_SEQ") == "1"
            while gens:
                for gg in list(gens):
                    try:
                        while True:
                            BLK[0] = False
                            next(gg)
                            if not seq or BLK[0]:
                                break
                    except StopIteration:
                        gens.remove(gg)

        def phase1(stack, hT, bhT, ntok, cos_d, sin_d, mode):
            main = (mode == "main")
            samp = main and WITH_SAMPLE
            nsc = min(ntok // SC, int(os.environ.get('K_NSC', '99')))
            NG = int(os.environ.get('K_G', '8'))
            Wf, bWf = sbuf(stack, "Wf", [128, 16, 1280 if main else 1024], BF16)
            Wt, bWt = sbuf(stack, "Wt", [128, 16, 1536 if main else 512], BF16)
            if main:
                normw, bnw = sbuf(stack, "normw", [128, 512])
                cg, bcg = sbuf(stack, "cg", [128, 384])
                zs, bzs = sbuf(stack, "zs", [128, 512])
                qkm, bqkm = sbuf(stack, "qkm", [128, 128])
                Ld, bLd = sbuf(stack, "Ld", [128, 4, 128])
                segT, bsegT = sbuf(stack, "segT", [128, 512])
                MT, bMT = sbuf(stack, "MT", [128, 8, 128], BF16)
                xdt, bxdt = sbuf(stack, "xdt", [128, 512], BF16)
                xsk, bxsk = sbuf(stack, "xsk", [128, 512])
                ybar, bybar = sbuf(stack, "ybar", [128, 512], BF16)
                stage, bstage = sbuf(stack, "stage", [128, 4, 128], BF16)
                MTr, bMTr = sbuf(stack, "MTr", [128, 128], BF16)
                qs, bqs = sbuf(stack, "qs", [128, 2, 128], BF16)
                R0buf = [sbuf(stack, f"R0n{i}", [128, 4, 256]) for i in range(2)]
            if samp:
                csS, bcsS = sbuf(stack, "csS", [128, 2, 64])
                LD(csS[:, 0, :], cosS_d[:, :], [bcsS], bcsS)
                LD(csS[:, 1, :], sinS_d[:, :], [bcsS], bcsS)
                S0buf = [sbuf(stack, f"S0n{i}", [128, 4, 128]) for i in range(2)]
                S0b, bS0b = sbuf(stack, "S0b", [128, 4, 128], BF16)
                R0b, bR0b = sbuf(stack, "R0b", [128, 4, 256], BF16)
                cdcol, bcdcol = sbuf(stack, "cdcol", [128, 4, 16])
                Bm, bBm = sbuf(stack, "Bm", [128, 128], BF16)
                kdm, bkdm = sbuf(stack, "kdm", [128, 256], BF16)
                raws_t, braws = sbuf(stack, "raws", [128, 6, 112])
                raws = raws_t[:, :, :].rearrange("p b (t s) -> p b t s", s=16)
                cTs, bcTs = sbuf(stack, "cTs", [128, 6, 64], BF16)
                qkTs, bqkTs = sbuf(stack, "qkTs", [128, 4, 64], BF16)
                yinT_t, byinT = sbuf(stack, "yinT", [128, 4, 64])
                yinT = yinT_t[:, :, :]
            Wdt, bWdt = sbuf(stack, "Wdt", [128, 16, 64], BF16)
            LDC(Wdt[:], w_in[:, ODT:ODT + 64].rearrange("(kc p) n -> p kc n", p=128), [bWdt], bWdt)
            raw, braw = sbuf(stack, "raw", [128, 6, SC + 3])
            acc, bacc = sbuf(stack, "acc", [128, 512])
            cc, bcc = acc[:, 256:384], bacc
            cTb = [sbuf(stack, f"cT{i}", [128, 6, SC], BF16) for i in range(2)]
            qkTb = [sbuf(stack, f"qkT{i}", [128, 4, SC], BF16) for i in range(2)]
            cosT, bcos = sbuf(stack, "cosT", [128, SC])
            sinT, bsin = sbuf(stack, "sinT", [128, SC])
            r1, br1 = sbuf(stack, "r1", [128, 512])
            r2, br2 = sbuf(stack, "r2", [128, 512])
            y1, by1 = sbuf(stack, "y1", [128, 512])
            y2, by2 = y1, by1
            sm, bsm = sbuf(stack, "sm", [128, 12, 8])
            xdd, bxdd = sbuf(stack, "xdd", [128, 512], BF16)
            Btok, bBtok = sbuf(stack, "Btok", [128, 128], BF16)
            ST, bST = sbuf(stack, "ST", [128, 512])
            STb, bSTb = sbuf(stack, "STb", [128, 512], BF16)
            tmpS, btmpS = y1, by1
            RST, bRST = sbuf(stack, "RST", [128, 2, 512])
            RSTb, bRSTb = sbuf(stack, "RSTb", [128, 2, 512], BF16)
            vb, bvb = sbuf(stack, "vb", [128, 512], BF16)
            kd, bkd = sbuf(stack, "kd", [128, 256], BF16)
            rs, brs = sbuf(stack, "rs", [128, 2])
            ai = [0]
            a_done = set()
            b_done = set()
            as_done = set()
            bs_done = set()
            pB, bpB = ps[4]
            p2, bp2 = ps[2]
            p3, bp3 = ps[3]
            p5, bp5 = ps[5]
            identf = cst("ident")
            TH = [0]

            def psT_half():
                if os.environ.get('K_NOHALF') != '1':
                    TH[0] ^= 1
                return TH[0] * 512

            def abank():
                ai[0] ^= 1
                return ps[ai[0]]

            def wld(dst, bdst, lo, c0, n):
                LDC(dst[:, :, lo:lo + n], w_in[:, c0:c0 + n].rearrange("(kc p) n -> p kc n", p=128), [bdst], bdst)

            def rms_to(out_bf, bout, src, bsrc, mul, bmul, S=slice(0, 128)):
                ZERO(rs[:, 0:1], brs)
                ACT(out_bf, src, AF.Square, [bsrc], [bout, brs], accum_out=rs[S, 0:1])
                TS(rs[S, 1:2], rs[S, 0:1], 1.0 / 512, EPS, ALU.mult, ALU.add, [brs], [brs])
                ACT(rs[S, 1:2], rs[S, 1:2], AF.Sqrt, [brs], [brs])
                P.op("dve", lambda e: e.reciprocal(out=rs[S, 1:2], in_=rs[S, 1:2]), [brs], [brs])
                STT(out_bf, src, rs[S, 1:2], mul, ALU.mult, ALU.mult, [bsrc, brs, bmul], [bout])

            def gen_A():
                for g in range(NG):
                    cblk = [g * 4 + 0, g * 4 + 1, g * 4 + 2, g * 4 + 3, 32 + g, 40 + g]
                    wld(Wf, bWf, 0, OX + g * 512, 512)
                    wld(Wf, bWf, 512, OB + g * 128, 128)
                    wld(Wf, bWf, 640, OC + g * 128, 128)
                    wld(Wf, bWf, 768, OK_ + g * 256, 256)
                    if main:
                        wld(Wf, bWf, 1024, OQ + g * 256, 256)
                        CP(raw[:, :, 0:3], convtail[:, g * 6:(g + 1) * 6, :], [bct], [braw])
                    else:
                        P.op("dve", lambda e: e.memset(raw[:, :, 0:3], 0.0), [], [braw])
                    for sc in range(nsc):
                        idx = g * nsc + sc
                        while idx >= 2 and (idx - 2) not in b_done:
                            BLK[0] = True
                            yield
                        cT, bcT = cTb[idx % 2]
                        qkT, bqkT = qkTb[idx % 2]
                        T0 = sc * SC
                        LD(cosT[:], cos_d[:, T0:T0 + SC], [bcos], bcos)
                        LD(sinT[:], sin_d[:, T0:T0 + SC], [bsin], bsin)
                        for bi in range(6):
                            pt, bpt = abank()
                            for kc in range(16):
                                MM(pt[:, 0:SC], Wf[:, kc, bi * 128:(bi + 1) * 128], hT[:, kc, T0:T0 + SC], kc == 0, kc == 15,
                                   [bWf, bhT], [bpt])
                            ACT(raw[:, bi, 3:SC + 3], pt[:, 0:SC], AF.Copy, [bpt], [braw])
                            yield
                        for bi in range(6):
                            cb = cblk[bi]
                            TS(acc[:, 0:SC], raw[:, bi, 3:SC + 3], convw[:, cb, 3:4], convb[:, cb:cb + 1], ALU.mult, ALU.add,
                               [braw, bcw, bcb], [bacc])
                            for tap in (2, 1, 0):
                                STT(acc[:, 0:SC], raw[:, bi, tap:tap + SC], convw[:, cb, tap:tap + 1], acc[:, 0:SC], ALU.mult,
                                    ALU.add, [braw, bcw, bacc], [bacc])
                            ACT(cT[:, bi, :], acc[:, 0:SC], AF.Silu, [bacc], [bcT])
                            yield
                        CP(raw[:, :, 0:3], raw[:, :, SC:SC + 3], [braw], [braw])
                        for which in ((0, 1) if main else (1,)):
                            lo0 = 1024 if which == 0 else 768
                            p0, bp0 = ps[0]
                            p1, bp1 = ps[1]
                            for hf, (pt, bpt) in enumerate(((p0, bp0), (p1, bp1))):
                                for kc in range(16):
                                    MM(pt[:, 0:SC], Wf[:, kc, lo0 + hf * 128:lo0 + (hf + 1) * 128], hT[:, kc, T0:T0 + SC],
                                       kc == 0, kc == 15, [bWf, bhT], [bpt])
                                yield
                            TT(r1[:, 0:SC], p0[:, 0:SC], cosT[:], ALU.mult, [bp0, bcos], [br1])
                            TT(r2[:, 0:SC], p1[:, 0:SC], sinT[:], ALU.mult, [bp1, bsin], [br2])
                            TT(qkT[:, which * 2, :], r1[:, 0:SC], r2[:, 0:SC], ALU.subtract, [br1, br2], [bqkT])
                            TT(r1[:, 0:SC], p1[:, 0:SC], cosT[:], ALU.mult, [bp1, bcos], [br1])
                            TT(r2[:, 0:SC], p0[:, 0:SC], sinT[:], ALU.mult, [bp0, bsin], [br2])
                            TT(qkT[:, which * 2 + 1, :], r1[:, 0:SC], r2[:, 0:SC], ALU.add, [br1, br2], [bqkT])
                            yield
                        a_done.add(idx)
                        yield
                    if main:
                        CP(convout[:, g * 4:g * 4 + 4, :], raw[:, 0:4, 0:3], [braw], [bco])
                        CP(convout[:, 32 + g, :], raw[:, 4, 0:3], [braw], [bco])
                        CP(convout[:, 40 + g, :], raw[:, 5, 0:3], [braw], [bco])
                    else:
                        TS(convtail[:, g * 6:(g + 1) * 6, :], raw[:, :, 0:3], flag[:, 0:1], None, ALU.mult, None,
                           [braw, bflag], [bct])
                    if samp:
                        while g >= 1 and (g - 1) not in bs_done:
                            BLK[0] = True
                            yield
                        for (lo, n, c0) in ((0, 512, g * 512), (512, 128, 4096 + g * 128), (640, 128, 5120 + g * 128)):
                            pt, bpt = abank()
                            for kc in range(16):
                                MM(pt[0:64, 0:n], hTs[:, kc, 0:64], Wf[:, kc, lo:lo + n], kc == 0, kc == 15, [bhTs, bWf], [bpt])
                            ACT(r1[0:64, 0:n], pt[0:64, 0:n], AF.Copy, [bpt], [br1])
                            for t in (1, 2, 3):
                                LD(convs[:, t - 1, c0:c0 + n], r1[16 * t:16 * t + 16, 0:n], [], br1, R=[br1])
                            yield
                        for bi in range(6):
                            cb = cblk[bi]
                            LD(cc[0:48, :], cconv_d[:, cb * 128:(cb + 1) * 128], [bcc], bcc)
                            pt, bpt = abank()
                            TR(pt[:, 0:48], cc[0:48, :], identf[0:48, 0:48], [bcc, bC], [bpt])
                            ACT(raws[:, bi, 0:3, :], pt[:, 0:48].rearrange("p (t s) -> p t s", s=16), AF.Copy, [bpt], [braws])
                            pt, bpt = abank()
                            for kc in range(16):
                                MM(pt[:, 0:64], Wf[:, kc, bi * 128:(bi + 1) * 128], hTs[:, kc, 0:64], kc == 0, kc == 15,
                                   [bWf, bhTs], [bpt])
                            ACT(raws[:, bi, 3:7, :], pt[:, 0:64].rearrange("p (t s) -> p t s", s=16), AF.Copy, [bpt], [braws])
                            yield
                        for bi in range(6):
                            cb = cblk[bi]
                            rf = raws_t[:, bi, :]
                            TS(acc[:, 0:64], rf[:, 48:112], convw[:, cb, 3:4], convb[:, cb:cb + 1], ALU.mult, ALU.add,
                               [braws, bcw, bcb], [bacc])
                            for tap in (2, 1, 0):
                                STT(acc[:, 0:64], rf[:, tap * 16:tap * 16 + 64], convw[:, cb, tap:tap + 1], acc[:, 0:64],
                                    ALU.mult, ALU.add, [braws, bcw, bacc], [bacc])
                            ACT(cTs[:, bi, :], acc[:, 0:64], AF.Silu, [bacc], [bcTs])
                        yield
                        for which in (0, 1):
                            lo0 = 1024 if which == 0 else 768
                            p0, bp0 = ps[0]
                            p1, bp1 = ps[1]
                            for hf, (pt, bpt) in enumerate(((p0, bp0), (p1, bp1))):
                                for kc in range(16):
                                    MM(pt[:, 0:64], Wf[:, kc, lo0 + hf * 128:lo0 + (hf + 1) * 128], hTs[:, kc, 0:64], kc == 0,
                                       kc == 15, [bWf, bhTs], [bpt])
                            cS_, sS_ = csS[:, 0, :], csS[:, 1, :]
                            TT(r1[:, 0:64], p0[:, 0:64], cS_, ALU.mult, [bp0, bcsS], [br1])
                            TT(r2[:, 0:64], p1[:, 0:64], sS_, ALU.mult, [bp1, bcsS], [br2])
                            TT(qkTs[:, which * 2, :], r1[:, 0:64], r2[:, 0:64], ALU.subtract, [br1, br2], [bqkTs])
                            TT(r1[:, 0:64], p1[:, 0:64], cS_, ALU.mult, [bp1, bcsS], [br1])
                            TT(r2[:, 0:64], p0[:, 0:64], sS_, ALU.mult, [bp0, bcsS], [br2])
                            TT(qkTs[:, which * 2 + 1, :], r1[:, 0:64], r2[:, 0:64], ALU.add, [br1, br2], [bqkTs])
                            yield
                        as_done.add(g)
                        yield

            def decay_scalars(g, hsrc, bhsrc, tk, S, incl_ap, tot_ap):
                n = S.stop
                for kc in range(16):
                    MM(psM[S, 0:8], hsrc[:, kc, tk:tk + n], Wdt[:, kc, g * 8:(g + 1) * 8], kc == 0, kc == 15,
                       [bhsrc, bWdt], [bpsM])
                TT(sm[S, 0, :], psM[S, 0:8], small[S, 0, g * 8:(g + 1) * 8], ALU.add, [bpsM, bsmall], [bsm])
                ACT(sm[S, 0, :], sm[S, 0, :], AF.Exp, [bsm], [bsm])
                ACT(sm[S, 1, :], sm[S, 0, :], AF.Ln, [bsm], [bsm], bias=1.0)
                TT(sm[S, 2, :], sm[S, 1, :], small[S, 1, g * 8:(g + 1) * 8], ALU.mult, [bsm, bsmall], [bsm])
                MM(psM[S, 8:16], incl_ap, sm[S, 2, :], True, True, [bC, bsm], [bpsM])
                MM(psM[S, 16:24], tot_ap, sm[S, 2, :], True, True, [bC, bsm], [bpsM])
                ACT(sm[S, 3, :], psM[S, 8:16], AF.Copy, [bpsM], [bsm])
                TT(sm[S, 4, :], psM[S, 16:24], sm[S, 3, :], ALU.subtract, [bpsM, bsm], [bsm])
                ACT(sm[S, 5, :], sm[S, 4, :], AF.Exp, [bsm], [bsm])
                ACT(sm[S, 6, :], sm[S, 3, :], AF.Exp, [bsm], [bsm])
                ACT(sm[S, 7, :], psM[S, 16:24], AF.Exp, [bpsM], [bsm])
                TT(sm[S, 8, :], sm[S, 1, :], sm[S, 5, :], ALU.mult, [bsm], [bsm])

            def gen_B():
                for g in range(NG):
                    wld(Wt, bWt, 0, OV + g * 512, 512)
                    if main:
                        wld(Wt, bWt, 512, OZ + g * 512, 512)
                        wld(Wt, bWt, 1024, OG + g * 512, 512)
                        LD(normw[:], normw_d[:, g * 512:(g + 1) * 512], [bnw], bnw)
                        LD(cg[:], cgrp_d[g], [bcg], bcg)
                        LD(ST[:], STs_d[g], [bST], bST)
                        LD(RST[:], STr_d[g].rearrange("p (a b) -> p a b", b=512), [bRST], bRST)
                    else:
                        P.op("dve", lambda e: e.memset(ST[:], 0.0), [], [bST])
                        P.op("dve", lambda e: e.memset(RST[:], 0.0), [], [bRST])
                    ACT(STb[:], ST[:], AF.Copy, [bST], [bSTb])
                    ACT(RSTb[:], RST[:], AF.Copy, [bRST], [bRSTb])
                    gQ = float(_gammas()[g] ** 128)
                    oK = CONST_LAYOUT["kds"][0]
                    for sc in range(nsc):
                        idx = g * nsc + sc
                        while idx not in a_done:
                            BLK[0] = True
                            yield
                        cT, bcT = cTb[idx % 2]
                        qkT, bqkT = qkTb[idx % 2]
                        T0 = sc * SC
                        for c in range(SC // 128):
                            lc = c * 128
                            tk = T0 + lc
                            BC = int(os.environ.get('K_BCUT', '9'))
                            if BC >= 1:
                                decay_scalars(g, hT, bhT, tk, slice(0, 128), cst("causal"), cst("ones"))
                            yield
                            if BC < 2:
                                continue
                            for kc in range(16):
                                MM(pB[:, :], hT[:, kc, tk:tk + 128], Wt[:, kc, 0:512], kc == 0, kc == 15, [bhT, bWt], [bpB])
                            ACT(vb[:], pB[:, :], AF.Copy, [bpB], [bvb])
                            yield
                            if BC < 3:
                                continue
                            h0 = psT_half()
                            for j in range(4):
                                TR(psT[:, h0 + j * 128:h0 + (j + 1) * 128], cT[:, j, lc:lc + 128], identb[:], [bcT, bidb], [bpsT])
                            xs3 = psT[:, h0:h0 + 512].rearrange("p (r q) -> p r q", q=64)

                            def b8(row):
                                return sm[:, row, :].unsqueeze(2).broadcast_to([128, 8, 64])
                            TT(xdd[:].rearrange("p (r q) -> p r q", q=64), xs3, b8(8), ALU.mult, [bpsT, bsm], [bxdd])
                            if main:
                                TT(xdt[:].rearrange("p (r q) -> p r q", q=64), xs3, b8(1), ALU.mult, [bpsT, bsm], [bxdt])
                                dskb = small[:, 2, g * 8:(g + 1) * 8].unsqueeze(2).broadcast_to([128, 8, 64])
                                TT(xsk[:].rearrange("p (r q) -> p r q", q=64), xs3, dskb, ALU.mult, [bpsT, bsmall], [bxsk])
                            if BC < 4:
                                continue
                            h1 = psT_half()
                            TR(psT[:, h1:h1 + 128], cT[:, 4, lc:lc + 128], identb[:], [bcT, bidb], [bpsT])
                            for hf in range(2):
                                TR(psT[:, h1 + 128 + hf * 128:h1 + 256 + hf * 128], qkT[:, 2 + hf, lc:lc + 128], identb[:],
                                   [bqkT, bidb], [bpsT])
                            ACT(Btok[:], psT[:, h1:h1 + 128], AF.Copy, [bpsT], [bBtok, bpsT])
                            TS(kd[:], psT[:, h1 + 128:h1 + 384], C[:, oK + g:oK + g + 1], None, ALU.mult, None, [bpsT, bC], [bkd])
                            yield
                            if main:
                                MM(psM[:, 128:256], cT[:, 4, lc:lc + 128], cT[:, 5, lc:lc + 128], True, True, [bcT], [bpsM])
                                TT(qkm[:], psM[:, 128:256], cst("causal"), ALU.mult, [bpsM, bC], [bqkm])
                                for hh in range(2):
                                    TT(Ld[:], cst("strict").unsqueeze(1).broadcast_to([128, 4, 128]),
                                       sm[:, 2, hh * 4:hh * 4 + 4].unsqueeze(2).broadcast_to([128, 4, 128]), ALU.mult,
                                       [bC, bsm], [bLd])
                                    for r in range(4):
                                        MM(p2[:, r * 128:(r + 1) * 128], Ld[:, r, :], cst("causal"), True, True, [bLd, bC], [bp2])
                                    ACT(segT[:], p2[:, :], AF.Exp, [bp2], [bsegT])
                                    TT(MT[:, hh * 4:hh * 4 + 4, :], segT[:].rearrange("p (r l) -> p r l", l=128),
                                       qkm[:].unsqueeze(1).broadcast_to([128, 4, 128]), ALU.mult, [bsegT, bqkm], [bMT])
                                    yield
                                for r in range(8):
                                    MM(p3[:, r * 64:(r + 1) * 64], MT[:, r, :], xdt[:, r * 64:(r + 1) * 64], True, True,
                                       [bMT, bxdt], [bp3])
                                MM(p5[:, :], cT[:, 5, lc:lc + 128], STb[:], True, True, [bcT, bSTb], [bp5])
                                for kc in range(16):
                                    MM(pB[:, :], hT[:, kc, tk:tk + 128], Wt[:, kc, 512:1024], kc == 0, kc == 15, [bhT, bWt], [bpB])
                                ACT(zs[:], pB[:, :], AF.Silu, [bpB], [bzs])
                                yield
                                TT(y1[:].rearrange("p (r q) -> p r q", q=64), p5[:, :].rearrange("p (r q) -> p r q", q=64),
                                   b8(6), ALU.mult, [bp5, bsm], [by1])
                                TT(y1[:], y1[:], p3[:, :], ALU.add, [by1, bp3], [by1])
                                TT(y1[:], y1[:], xsk[:], ALU.add, [by1, bxsk], [by1])
                                TT(y1[:], y1[:], zs[:], ALU.mult, [by1, bzs], [by1])
                                rms_to(ybar[:], bybar, y1[:], by1, normw[:], bnw)
                                h0 = psT_half()
                                for j in range(4):
                                    TR(psT[:, h0 + j * 128:h0 + (j + 1) * 128], ybar[:, j * 128:(j + 1) * 128], identb[:],
                                       [bybar, bidb], [bpsT])
                                ACT(stage[:], psT[:, h0:h0 + 512].rearrange("p (a b) -> p a b", b=128), AF.Copy, [bpsT], [bstage])
                                LD(YTd[g * 4:g * 4 + 4, :, tk:tk + 128].rearrange("j p t -> p j t"), stage[:], [], bstage,
                                   R=[bstage])
                                yield
                            if BC < 5:
                                continue
                            MM(p5[:, :], Btok[:], xdd[:], True, True, [bBtok, bxdd], [bp5])
                            TT(tmpS[:].rearrange("p (r q) -> p r q", q=64), ST[:].rearrange("p (r q) -> p r q", q=64),
                               b8(7), ALU.mult, [bST, bsm], [btmpS])
                            TT(ST[:], tmpS[:], p5[:, :], ALU.add, [btmpS, bp5], [bST])
                            ACT(STb[:], ST[:], AF.Copy, [bST], [bSTb])
                            yield
                            if main:
                                for hf in range(2):
                                    MM(psM[:, 128:256], qkT[:, 2 + hf, lc:lc + 128], qkT[:, hf, lc:lc + 128], hf == 0, hf == 1,
                                       [bqkT], [bpsM])
                                TT(MTr[:], psM[:, 128:256], cg[:, 0:128], ALU.mult, [bpsM, bcg], [bMTr])
                                TT(qs[:], qkT[:, 0:2, lc:lc + 128], cg[:, 128:256].unsqueeze(1).broadcast_to([128, 2, 128]),
                                   ALU.mult, [bqkT, bcg], [bqs])
                                MM(p3[:, :], MTr[:], vb[:], True, False, [bMTr, bvb], [bp3])
                                for hf in range(2):
                                    MM(p3[:, :], qs[:, hf, :], RSTb[:, hf, :], False, hf == 1, [bqs, bRSTb], [bp3])
                                for kc in range(16):
                                    MM(pB[:, :], hT[:, kc, tk:tk + 128], Wt[:, kc, 1024:1536], kc == 0, kc == 15, [bhT, bWt], [bpB])
                                ACT(zs[:], pB[:, :], AF.Silu, [bpB], [bzs])
                                yield
                                rms_to(ybar[:], bybar, p3[:, :], bp3, zs[:], bzs)
                                h0 = psT_half()
                                for j in range(4):
                                    TR(psT[:, h0 + j * 128:h0 + (j + 1) * 128], ybar[:, j * 128:(j + 1) * 128], identb[:],
                                       [bybar, bidb], [bpsT])
                                ACT(stage[:], psT[:, h0:h0 + 512].rearrange("p (a b) -> p a b", b=128), AF.Copy, [bpsT], [bstage])
                                LD(OTd[g * 4:g * 4 + 4, :, tk:tk + 128].rearrange("j p t -> p j t"), stage[:], [], bstage,
                                   R=[bstage])
                                yield
                            if BC < 6:
                                continue
                            for hf in range(2):
                                pq, bpq = (p5, bp5) if hf == 0 else (p2, bp2)
                                MM(pq[:, :], kd[:, hf * 128:(hf + 1) * 128], vb[:], True, True, [bkd, bvb], [bpq])
                                STT(RST[:, hf, :], RST[:, hf, :], gQ, pq[:, :], ALU.mult, ALU.add, [bRST, bpq], [bRST])
                            ACT(RSTb[:], RST[:], AF.Copy, [bRST], [bRSTb])
                            yield
                        b_done.add(idx)
                        yield
                    if samp:
                        while g not in as_done:
                            BLK[0] = True
                            yield
                        yield from sample_B(g)
                        bs_done.add(g)
                    if not main:
                        TS(ST[:], ST[:], flag[:, 0:1], None, ALU.mult, None, [bST, bflag], [bST])
                        TS(RST[:], RST[:], flag[:, 0:1], None, ALU.mult, None, [bRST, bflag], [bRST])
                        LD(STs_d[g], ST[:], [], bST, R=[bST])
                        LD(STr_d[g].rearrange("p (a b) -> p a b", b=512), RST[:], [], bRST, R=[bRST])
                    else:
                        stf, bstf = R0buf[0]
                        for j in range(4):
                            TR(p2[:, j * 128:(j + 1) * 128], ST[:, j * 128:(j + 1) * 128], identf, [bST, bC], [bp2])
                        ACT(stf[:, :, 0:128], p2[:, :].rearrange("p (a b) -> p a b", b=128), AF.Copy, [bp2], [bstf])
                        LD(ssmp[g * 512:(g + 1) * 512, :].rearrange("(j p) n -> p j n", p=128), stf[:, :, 0:128], [], bstf,
                           R=[bstf])
                        stf, bstf = R0buf[1]
                        for hf in range(2):
                            for j in range(4):
                                TR(p2[:, j * 128:(j + 1) * 128], RST[:, hf, j * 128:(j + 1) * 128], identf, [bRST, bC], [bp2])
                            ACT(stf[:, :, hf * 128:(hf + 1) * 128], p2[:, :].rearrange("p (a b) -> p a b", b=128), AF.Copy,
                                [bp2], [bstf])
                        LD(retp[g * 512:(g + 1) * 512, :].rearrange("(j p) k -> p j k", p=128), stf[:], [], bstf, R=[bstf])
                    yield

            def sample_B(g):
                seqm = cst("seqmask", 64)
                S = slice(0, 64)
                decay_scalars(g, hTs, bhTs, 0, S, cst("incl_s", 64), cst("same_s", 64))
                CP(segT[S, :].rearrange("p (r q) -> p r q", q=64), sm[S, 2, :].unsqueeze(2).broadcast_to([64, 8, 64]),
                   [bsm], [bsegT])
                for j in range(4):
                    MM(psM[:, 32 + j * 16:32 + (j + 1) * 16], segT[S, j * 128:(j + 1) * 128], seqm, True, True,
                       [bsegT, bC], [bpsM])
                ACT(cdcol[:], psM[:, 32:96].rearrange("p (j s) -> p j s", s=16), AF.Exp, [bpsM], [bcdcol])
                yield

                def b8s(row):
                    return sm[S, row, :].unsqueeze(2).broadcast_to([64, 8, 64])
                h0 = psT_half()
                for j in range(4):
                    TR(psT[S, h0 + j * 128:h0 + (j + 1) * 128], cTs[:, j, :], identb[:], [bcTs, bidb], [bpsT])
                xs3 = psT[S, h0:h0 + 512].rearrange("p (r q) -> p r q", q=64)
                TT(xdd[S, :].rearrange("p (r q) -> p r q", q=64), xs3, b8s(8), ALU.mult, [bpsT, bsm], [bxdd])
                TT(xdt[S, :].rearrange("p (r q) -> p r q", q=64), xs3, b8s(1), ALU.mult, [bpsT, bsm], [bxdt])
                dskb = small[S, 2, g * 8:(g + 1) * 8].unsqueeze(2).broadcast_to([64, 8, 64])
                TT(xsk[S, :].rearrange("p (r q) -> p r q", q=64), xs3, dskb, ALU.mult, [bpsT, bsmall], [bxsk])
                h1 = psT_half()
                TR(psT[S, h1:h1 + 128], cTs[:, 4, :], identb[:], [bcTs, bidb], [bpsT])
                for hf in range(2):
                    TR(psT[S, h1 + 128 + hf * 128:h1 + 256 + hf * 128], qkTs[:, 2 + hf, :], identb[:], [bqkTs, bidb], [bpsT])
                ACT(Btok[S, :], psT[S, h1:h1 + 128], AF.Copy, [bpsT], [bBtok, bpsT])
                oK = CONST_LAYOUT["kds_s"][0]
                TS(kd[S, :], psT[S, h1 + 128:h1 + 384], C[S, oK + g:oK + g + 1], None, ALU.mult, None, [bpsT, bC], [bkd])
                yield
                MM(psM[S, 128:192], cTs[:, 4, :], cTs[:, 5, :], True, True, [bcTs], [bpsM])
                TT(qkm[S, 0:64], psM[S, 128:192], cst("causal_s", 64), ALU.mult, [bpsM, bC], [bqkm])
                for hh in range(2):
                    TT(Ld[S, :, 0:64], cst("strict_s", 64).unsqueeze(1).broadcast_to([64, 4, 64]),
                       sm[S, 2, hh * 4:hh * 4 + 4].unsqueeze(2).broadcast_to([64, 4, 64]), ALU.mult, [bC, bsm], [bLd])
                    for r in range(4):
                        MM(p2[S, r * 64:(r + 1) * 64], Ld[S, r, 0:64], cst("incl_s", 64), True, True, [bLd, bC], [bp2])
                    ACT(segT[S, 0:256], p2[S, 0:256], AF.Exp, [bp2], [bsegT])
                    TT(MT[S, hh * 4:hh * 4 + 4, 0:64], segT[S, 0:256].rearrange("p (r l) -> p r l", l=64),
                       qkm[S, 0:64].unsqueeze(1).broadcast_to([64, 4, 64]), ALU.mult, [bsegT, bqkm], [bMT])
                for r in range(8):
                    MM(p3[S, r * 64:(r + 1) * 64], MT[S, r, 0:64], xdt[S, r * 64:(r + 1) * 64], True, True, [bMT, bxdt], [bp3])
                yield
                for kc in range(16):
                    MM(pB[S, :], hTs[:, kc, 0:64], Wt[:, kc, 0:512], kc == 0, kc == 15, [bhTs, bWt], [bpB])
                ACT(vb[S, :], pB[S, :], AF.Copy, [bpB], [bvb])
                for hf in range(2):
                    MM(psM[S, 128:192], qkTs[:, 2 + hf, :], qkTs[:, hf, :], hf == 0, hf == 1, [bqkTs], [bpsM])
                TT(MTr[S, 0:64], psM[S, 128:192], cg[S, 256:320], ALU.mult, [bpsM, bcg], [bMTr])
                qs_s = qs[:, :, 0:64]
                TT(qs_s, qkTs[:, 0:2, :], cg[:, 320:384].unsqueeze(1).broadcast_to([128, 2, 64]), ALU.mult, [bqkTs, bcg], [bqs])
                MM(p2[S, :], MTr[S, 0:64], vb[S, :], True, True, [bMTr, bvb], [bp2])
                yield
                g4 = float(_gammas()[g] ** 4)

                def ssd_seqs():
                    for s in range(16):
                        S0n, bS0n = S0buf[s % 2]
                        LD(S0n[:], sssm_d[s, g * 512:(g + 1) * 512, :].rearrange("(j p) n -> p j n", p=128), [bS0n], bS0n)
                        P.op("dve", lambda e, S0n=S0n: e.tensor_copy(out=S0b[:], in_=S0n[:]), [bS0n], [bS0b])
                        h = psT_half()
                        for j in range(4):
                            TR(psT[:, h + j * 128:h + (j + 1) * 128], S0b[:, j, :], identb[:], [bS0b, bidb], [bpsT])
                        ACT(STb[:], psT[:, h:h + 512], AF.Copy, [bpsT], [bSTb])
                        for j in range(4):
                            MM(p5[:, j * 64 + s:(j + 1) * 64:16], STb[:, j * 128:(j + 1) * 128], cTs[:, 5, s:64:16], True, True,
                               [bSTb, bcTs], [bp5])
                        TS(Bm[S, :], Btok[S, :], seqm[:, s:s + 1], None, ALU.mult, None, [bBtok, bC], [bBm])
                        for j in range(4):
                            MM(pB[:, j * 128:(j + 1) * 128], xdd[S, j * 128:(j + 1) * 128], Bm[S, :], True, True,
                               [bxdd, bBm], [bpB])
                        TT(S0n[:], S0n[:], cdcol[:, :, s].unsqueeze(2).broadcast_to([128, 4, 128]), ALU.mult,
                           [bS0n, bcdcol], [bS0n])
                        TT(S0n[:], S0n[:], pB[:, :].rearrange("p (j n) -> p j n", n=128), ALU.add, [bS0n, bpB], [bS0n])
                        LD(ssms[s, g * 512:(g + 1) * 512, :].rearrange("(j p) n -> p j n", p=128), S0n[:], [], bS0n, R=[bS0n])
                        yield

                def ret_seqs():
                    for s in range(16):
                        R0n, bR0n = R0buf[s % 2]
                        LD(R0n[:], sret_d[s, g * 512:(g + 1) * 512, :].rearrange("(j p) k -> p j k", p=128), [bR0n], bR0n)
                        P.op("dve", lambda e, R0n=R0n: e.tensor_copy(out=R0b[:], in_=R0n[:]), [bR0n], [bR0b])
                        for hf in range(2):
                            h = psT_half()
                            for j in range(4):
                                TR(psT[:, h + j * 128:h + (j + 1) * 128], R0b[:, j, hf * 128:(hf + 1) * 128], identb[:],
                                   [bR0b, bidb], [bpsT])
                            ACT(RSTb[:, hf, :], psT[:, h:h + 512], AF.Copy, [bpsT], [bRSTb])
                        for j in range(4):
                            for hf in range(2):
                                MM(p5[:, 256 + j * 64 + s:256 + (j + 1) * 64:16], RSTb[:, hf, j * 128:(j + 1) * 128],
                                   qs_s[:, hf, s:64:16], hf == 0, hf == 1, [bRSTb, bqs], [bp5])
                        TS(kdm[S, :], kd[S, :], seqm[:, s:s + 1], None, ALU.mult, None, [bkd, bC], [bkdm])
                        for jj in range(2):
                            for j2 in range(2):
                                j = jj * 2 + j2
                                MM(p3[:, j2 * 256:(j2 + 1) * 256] if False else pB[:, j2 * 256:(j2 + 1) * 256],
                                   vb[S, j * 128:(j + 1) * 128], kdm[S, :], True, True, [bvb, bkdm], [bpB])
                            STT(R0n[:, jj * 2:jj * 2 + 2, :], R0n[:, jj * 2:jj * 2 + 2, :], g4,
                                pB[:, :].rearrange("p (j k) -> p j k", k=256), ALU.mult, ALU.add, [bR0n, bpB], [bR0n])
                        LD(rets[s, g * 512:(g + 1) * 512, :].rearrange("(j p) k -> p j k", p=128), R0n[:], [], bR0n, R=[bR0n])
                        yield

                subs = [ssd_seqs(), ret_seqs()]
                while subs:
                    for sg_ in list(subs):
                        try:
                            next(sg_)
                        except StopIteration:
                            subs.remove(sg_)
                    yield
                ACT(yinT, p5[:, 0:256].rearrange("p (j t) -> p j t", t=64), AF.Copy, [bp5], [byinT])
                for j in range(4):
                    TR(pB[S, j * 128:(j + 1) * 128], yinT[:, j, :], identf, [byinT, bC], [bpB])
                TT(y1[S, :].rearrange("p (r q) -> p r q", q=64), pB[S, :].rearrange("p (r q) -> p r q", q=64), b8s(6),
                   ALU.mult, [bpB, bsm], [by1])
                TT(y1[S, :], y1[S, :], p3[S, :], ALU.add, [by1, bp3], [by1])
                TT(y1[S, :], y1[S, :], xsk[S, :], ALU.add, [by1, bxsk], [by1])
                for kc in range(16):
                    MM(pB[S, :], hTs[:, kc, 0:64], Wt[:, kc, 512:1024], kc == 0, kc == 15, [bhTs, bWt], [bpB])
                ACT(zs[S, :], pB[S, :], AF.Silu, [bpB], [bzs])
                TT(y1[S, :], y1[S, :], zs[S, :], ALU.mult, [by1, bzs], [by1])
                rms_to(ybar[S, :], bybar, y1[S, :], by1, normw[S, :], bnw, S)
                h0 = psT_half()
                for j in range(4):
                    TR(psT[:, h0 + j * 64:h0 + (j + 1) * 64], ybar[S, j * 128:(j + 1) * 128], identb[0:64, 0:64],
                       [bybar, bidb], [bpsT])
                stage_s = stage[:, :, 0:64]
                ACT(stage_s, psT[:, h0:h0 + 256].rearrange("p (a b) -> p a b", b=64), AF.Copy, [bpsT], [bstage])
                LD(YTd[g * 4:g * 4 + 4, :, NM:NM + 64].rearrange("j p t -> p j t"), stage_s, [], bstage, R=[bstage])
                yield
                ACT(yinT, p5[:, 256:512].rearrange("p (j t) -> p j t", t=64), AF.Copy, [bp5], [byinT])
                for j in range(4):
                    TR(pB[S, j * 128:(j + 1) * 128], yinT[:, j, :], identf, [byinT, bC], [bpB])
                ACT(y1[S, :], pB[S, :], AF.Copy, [bpB], [by1])
                TT(y1[S, :], y1[S, :], p2[S, :], ALU.add, [by1, bp2], [by1])
                for kc in range(16):
                    MM(pB[S, :], hTs[:, kc, 0:64], Wt[:, kc, 1024:1536], kc == 0, kc == 15, [bhTs, bWt], [bpB])
                ACT(zs[S, :], pB[S, :], AF.Silu, [bpB], [bzs])
                rms_to(ybar[S, :], bybar, y1[S, :], by1, zs[S, :], bzs, S)
                h0 = psT_half()
                for j in range(4):
                    TR(psT[:, h0 + j * 64:h0 + (j + 1) * 64], ybar[S, j * 128:(j + 1) * 128], identb[0:64, 0:64],
                       [bybar, bidb], [bpsT])
                ACT(stage_s, psT[:, h0:h0 + 256].rearrange("p (a b) -> p a b", b=64), AF.Copy, [bpsT], [bstage])
                LD(OTd[g * 4:g * 4 + 4, :, NM:NM + 64].rearrange("j p t -> p j t"), stage_s, [], bstage, R=[bstage])
                yield

            KCUT = os.environ.get('K_CUT', '')
            if KCUT == 'A':
                b_done.update(range(100)); bs_done.update(range(8))
                interleave([gen_A()])
            elif KCUT == 'B':
                a_done.update(range(100)); as_done.update(range(8))
                interleave([gen_B()])
            else:
                interleave([gen_A(), gen_B()])

        def phase2(t0, n, hT, bhT, hoff, xsrc, xoff, ydst):
            with contextlib.ExitStack() as s2:
                mergedT, bmg = sbuf(s2, "mergedT", [128, 16, 512], BF16)
                with contextlib.ExitStack() as s2a:
                    YT, bYT = sbuf(s2a, "YT", [128, 32, 512], BF16)
                    OT, bOT = sbuf(s2a, "OT", [128, 32, 512], BF16)
                    Wsb = [sbuf(s2a, f"Wssm_j{i}", [128, 32, 128], BF16) for i in range(2)]
                    Wrb = [sbuf(s2a, f"Wret_j{i}", [128, 32, 128], BF16) for i in range(2)]
                    Wg1b = [sbuf(s2a, f"Wgs_j{i}", [128, 16, 128], BF16) for i in range(2)]
                    Wg2b = [sbuf(s2a, f"Wgr_j{i}", [128, 16, 128], BF16) for i in range(2)]
                    sg, bsg = sbuf(s2a, "sg", [128, 512])
                    m1, bm1 = sbuf(s2a, "m1", [128, 512])
                    m2, bm2 = sbuf(s2a, "m2", [128, 512])
                    LD(YT[:, :, 0:n], YTd[:, :, t0:t0 + n].rearrange("k p t -> p k t"), [bYT], bYT)
                    LD(OT[:, :, 0:n], OTd[:, :, t0:t0 + n].rearrange("k p t -> p k t"), [bOT], bOT)
                    for j in range(16):
                        (Ws, bWs), (Wr, bWr), (Wg1, bWg1), (Wg2, bWg2) = Wsb[j % 2], Wrb[j % 2], Wg1b[j % 2], Wg2b[j % 2]
                        LDC(Ws[:], wssm[:, j * 128:(j + 1) * 128].rearrange("(kb p) n -> p kb n", p=128), [bWs], bWs)
                        LDC(Wr[:], wret[:, j * 128:(j + 1) * 128].rearrange("(kb p) n -> p kb n", p=128), [bWr], bWr)
                        LDC(Wg1[:], w_in[:, OGS + j * 128:OGS + (j + 1) * 128].rearrange("(kb p) n -> p kb n", p=128),
                            [bWg1], bWg1)
                        LDC(Wg2[:], w_in[:, OGR + j * 128:OGR + (j + 1) * 128].rearrange("(kb p) n -> p kb n", p=128),
                            [bWg2], bWg2)
                        for (Wp, bWp, Xp, bXp, Wg, bWg, pa, pb, mo, bmo) in (
                                (Ws, bWs, YT, bYT, Wg1, bWg1, ps[0], ps[1], m1, bm1),
                                (Wr, bWr, OT, bOT, Wg2, bWg2, ps[2], ps[3], m2, bm2)):
                            for kb in range(32):
                                MM(pa[0][:, 0:n], Wp[:, kb, :], Xp[:, kb, 0:n], kb == 0, kb == 31, [bWp, bXp], [pa[1]])
                            for kc in range(16):
                                MM(pb[0][:, 0:n], Wg[:, kc, :], hT[:, kc, hoff:hoff + n], kc == 0, kc == 15, [bWg, bhT], [pb[1]])
                            ACT(sg[:, 0:n], pb[0][:, 0:n], AF.Sigmoid, [pb[1]], [bsg])
                            TT(mo[:, 0:n], sg[:, 0:n], pa[0][:, 0:n], ALU.mult, [bsg, pa[1]], [bmo])
                        TT(mergedT[:, j, 0:n], m1[:, 0:n], m2[:, 0:n], ALU.add, [bm1, bm2], [bmg])
                P.barrier()
                with contextlib.ExitStack() as s2b:
                    Wo, bWo = sbuf(s2b, "Wo", [128, 16, D], BF16)
                    xt, bxt = sbuf(s2b, "p2_xt", [128, D])
                    yo, byo = sbuf(s2b, "p2_yo", [128, D])
                    wpost, bwpost = sbuf(s2b, "p2_wpost", [128, D])
                    ss4, bss4 = sbuf(s2b, "ss4", [128, 8])
                    junk, bjunk = sbuf(s2b, "junk", [128, 512])
                    LD(wpost[:], wpost_d[:, :], [bwpost], bwpost)
                    for nb in range(4):
                        LDC(Wo[:, :, nb * 512:(nb + 1) * 512],
                            wout[:, nb * 512:(nb + 1) * 512].rearrange("(kb p) n -> p kb n", p=128), [bWo], bWo)
                    for c0 in range(0, n, 128):
                        rows = min(128, n - c0)
                        LD(xt[0:rows, :], xsrc[xoff + c0:xoff + c0 + rows, :], [bxt], bxt)
                        ZERO(ss4[:, 0:4], bss4)
                        for nb in range(4):
                            pt, bpt = ps[nb]
                            for j in range(16):
                                MM(pt[0:rows, :], mergedT[:, j, c0:c0 + rows], Wo[:, j, nb * 512:(nb + 1) * 512], j == 0,
                                   j == 15, [bmg, bWo], [bpt])
                            ACT(junk[0:rows, :], pt[0:rows, :], AF.Square, [bpt], [bjunk, bss4], accum_out=ss4[0:rows, nb:nb + 1])
                        P.op("dve", lambda e, rows=rows: e.reduce_sum(out=ss4[0:rows, 4:5], in_=ss4[0:rows, 0:4], axis=AX.X),
                             [bss4], [bss4])
                        TS(ss4[0:rows, 5:6], ss4[0:rows, 4:5], 1.0 / D, EPS, ALU.mult, ALU.add, [bss4], [bss4])
                        ACT(ss4[0:rows, 5:6], ss4[0:rows, 5:6], AF.Sqrt, [bss4], [bss4])
                        P.op("dve", lambda e, rows=rows: e.reciprocal(out=ss4[0:rows, 5:6], in_=ss4[0:rows, 5:6]), [bss4], [bss4])
                        for nb in range(4):
                            pt, bpt = ps[nb]
                            STT(yo[0:rows, nb * 512:(nb + 1) * 512], pt[0:rows, :], ss4[0:rows, 5:6],
                                wpost[0:rows, nb * 512:(nb + 1) * 512], ALU.mult, ALU.mult, [bpt, bss4, bwpost], [byo])
                        TT(yo[0:rows, :], yo[0:rows, :], xt[0:rows, :], ALU.add, [byo, bxt], [byo])
                        LD(ydst[xoff + c0:xoff + c0 + rows, :], yo[0:rows, :], [], byo, R=[byo])
                P.barrier()

        KSTOP = int(os.environ.get("K_STOP", "9"))
        with contextlib.ExitStack() as sA:
            hTp, bhTp = sbuf(sA, "hTp", [128, 16, NP_], BF16)
            with contextlib.ExitStack() as s0:
                phase0(s0, xp, NP_, hTp, bhTp)
            P.barrier()
            if KSTOP >= 2:
                with contextlib.ExitStack() as s1:
                    phase1(s1, hTp, bhTp, NP_, cosP_d, sinP_d, "pre")
                P.barrier()
        if KSTOP >= 3:
            with contextlib.ExitStack() as s0:
                phase0(s0, xm, NM, hTm, bhTm)
                phase0(s0, xs_, NS, hTs, bhTs)
            P.barrier()
        if KSTOP >= 4:
            with contextlib.ExitStack() as s1:
                phase1(s1, hTm, bhTm, NM, cosM_d, sinM_d, "main")
            for cb in range(48):
                LDNC(convp[:, cb * 128:(cb + 1) * 128].rearrange("t p -> p t"), convout[:, cb, :], [], bco, R=[bco])
            P.barrier()
        if KSTOP >= 5:
            phase2(0, 512, hTm, bhTm, 0, xm, 0, ym)
            phase2(512, 512, hTm, bhTm, 512, xm, 512, ym)
            if WITH_SAMPLE:
                phase2(NM, NS, hTs, bhTs, 0, xs_, 0, ys)
        P.barrier()
        blk = st0.enter_context(nc.Block())
        P.emit(blk)
    return nc


def kernel(x_prompt, x_sample, cache_conv, state_ssm, state_ret, w_pre, w_in, conv_w, conv_b, dt_bias, a_log,
           d_skip, ssm_norm_w, w_proj_ssm, w_proj_ret, w_out, w_post):
    f = lambda a: np.ascontiguousarray(np.asarray(a, dtype=np.float32))
    x_prompt, x_sample = f(x_prompt), f(x_sample)
    consts, cgrp = make_consts()
    nc = build_program()
    bc = lambda v: np.ascontiguousarray(np.broadcast_to(f(v).reshape(1, -1), (128, f(v).size)))
    shared = {
        "consts": consts, "cgrp": cgrp, "w_in": f(w_in)[0], "wssm": f(w_proj_ssm)[0], "wret": f(w_proj_ret)[0], "wout": f(w_out)[0],
        "wpre_b": bc(w_pre), "wpost_b": bc(w_post), "normw_b": bc(ssm_norm_w),
        "convw": np.ascontiguousarray(f(conv_w)[0].reshape(4, 48, 128).transpose(2, 1, 0)),
        "convb": np.ascontiguousarray(f(conv_b)[0].reshape(48, 128).T),
        "dtb_b": bc(dt_bias), "alog_b": bc(a_log), "dsk_b": bc(d_skip),
    }
    cS, sS = rope_tables(16384 + (np.arange(64) // 16))
    in_maps = []
    for c in range(8):
        b, hf = c // 2, c % 2
        cM, sM = rope_tables(hf * 1024 + np.arange(1024))
        cP, sP = rope_tables(np.arange(1024))
        sl = slice(16 * c, 16 * c + 16)
        m = dict(shared)
        m.update({
            "xm": np.ascontiguousarray(x_prompt[b, hf * 1024:(hf + 1) * 1024]),
            "xp": np.ascontiguousarray(x_prompt[b, 0:1024]),
            "xs": np.ascontiguousarray(x_sample[sl].transpose(1, 0, 2).reshape(64, D)),
            "flag": np.full((128, 1), float(hf), np.float32),
            "cosM": cM, "sinM": sM, "cosP": cP, "sinP": sP, "cosS": cS, "sinS": sS,
            "cconv": np.ascontiguousarray(f(cache_conv)[0, sl].transpose(1, 0, 2).reshape(48, 6144)),
            "sssm": np.ascontiguousarray(f(state_ssm)[0, sl].reshape(16, 4096, 128)),
            "sret": np.ascontiguousarray(f(state_ret)[0, sl].reshape(16, 4096, 256)),
        })
        in_maps.append(m)
    res = run_bass_kernel_spmd(nc, in_maps, core_ids=list(range(8)))
    R = res.results
    y_prompt = np.zeros((4, 2048, D), np.float32)
    y_sample = np.zeros((128, 4, D), np.float32)
    conv_p = np.zeros((1, 4, 3, 6144), np.float32)
    ssm_p = np.zeros((1, 4, 64, 64, 128), np.float32)
    ret_p = np.zeros((1, 4, 8, 512, 256), np.float32)
    conv_s = np.zeros((1, 128, 3, 6144), np.float32)
    ssm_s = np.zeros((1, 128, 64, 64, 128), np.float32)
    ret_s = np.zeros((1, 128, 8, 512, 256), np.float32)
    for c in range(8):
        b, hf = c // 2, c % 2
        r = R[c]
        y_prompt[b, hf * 1024:(hf + 1) * 1024] = r["ym"]
        y_sample[16 * c:16 * c + 16] = r["ys"].reshape(4, 16, D).transpose(1, 0, 2)
        if hf == 1:
            conv_p[0, b] = r["convp"]
            ssm_p[0, b] = r["ssmp"].reshape(64, 64, 128)
            ret_p[0, b] = r["retp"].reshape(8, 512, 256)
        conv_s[0, 16 * c:16 * c + 16] = r["convs"]
        ssm_s[0, 16 * c:16 * c + 16] = r["ssms"].reshape(16, 64, 64, 128)
        ret_s[0, 16 * c:16 * c + 16] = r["rets"].reshape(16, 8, 512, 256)
    return (y_prompt, y_sample, conv_p, ssm_p, ret_p, conv_s, ssm_s, ret_s)
```

```python
import os
import contextlib
import math
import numpy as np
import concourse.bass as bass
import concourse.mybir as mybir
from concourse.bass_utils import run_bass_kernel_spmd

F32 = mybir.dt.float32
BF16 = mybir.dt.bfloat16
ALU = mybir.AluOpType
AF = mybir.ActivationFunctionType
AX = mybir.AxisListType

D = 2048
NM = 1024
NP_ = 1024
NS = 64
NT = NM + NS
IN_DIM = 26688
OZ, OX, OB, OC, ODT, OQ, OK_, OV, OG, OGS, OGR = 0, 4096, 8192, 9216, 10240, 10304, 12352, 14400, 18496, 22592, 24640
EPS = 1e-6
SC = 256
import os
WITH_SAMPLE = os.environ.get('K_NOSAMPLE') != '1'


class Buf:
    __slots__ = ("name", "w", "r", "dsem", "dcnt", "psum")

    def __init__(self, name):
        self.name = name
        self.psum = name.startswith("ps")
        self.w = []
        self.r = []
        self.dsem = None
        self.dcnt = 0


class Plan:
    ENGS = ("pe", "act", "dve", "pool", "sp")
    SEM_LIMIT = 30000
    SAME_ENGINE_SYNC = {"pe": False, "act": True, "dve": True, "pool": True, "sp": True}

    def __init__(self, nc, stack):
        self.nc = nc
        self.stack = stack
        self.streams = {e: [] for e in self.ENGS}
        self.esem = {}
        self.ecnt = {e: 0 for e in self.ENGS}
        self.seen = {e: {} for e in self.ENGS}
        self.nsem = 0
        for e in self.ENGS:
            self.esem[e] = self.new_sem("e_" + e)
        self.bufs = []
        self.free_ctrs = []
        self.pending_ctrs = []

    def release(self, b):
        if b.dsem is not None:
            if b.dsem[2] == "sp":
                self.pending_ctrs.append(b.dsem)
            b.dsem = None

    def new_sem(self, name):
        self.nsem += 1
        return self.stack.enter_context(self.nc.semaphore(f"{name}_{self.nsem}"))

    def buf(self, name):
        b = Buf(name)
        self.bufs.append(b)
        return b

    def _deps(self, eng, reads, writes, extra=()):
        need = {}

        def add(lst):
            for (s, v) in lst:
                k = id(s)
                if k not in need or need[k][1] < v:
                    need[k] = (s, v)
        for b in reads:
            add(b.w)
            if b.psum:
                add(b.r)
        for b in writes:
            add(b.w)
            add(b.r)
        add(extra)
        out = []
        seen = self.seen[eng]
        own = id(self.esem[eng])
        for k, (s, v) in need.items():
            if k == own and not self.SAME_ENGINE_SYNC[eng]:
                continue
            if seen.get(k, -1) >= v:
                continue
            seen[k] = v
            out.append((s, v))
        return out

    def op(self, eng, fn, reads=(), writes=()):
        waits = self._deps(eng, reads, writes)
        if self.ecnt[eng] >= self.SEM_LIMIT:
            self.esem[eng] = self.new_sem("e_" + eng)
            self.ecnt[eng] = 0
        self.ecnt[eng] += 1
        tok = (self.esem[eng], self.ecnt[eng])
        self.streams[eng].append((waits, fn, (self.esem[eng], 1)))
        for b in reads:
            b.r.append(tok)
        for b in writes:
            b.w = [tok]
            b.r = []
        return tok

    def dma(self, eng, fn, reads=(), writes=(), owner=None):
        waits = self._deps(eng, reads, writes)
        ow = owner
        if ow.dsem is None:
            if eng == "sp" and self.free_ctrs:
                ow.dsem = self.free_ctrs.pop()
            else:
                ow.dsem = [self.new_sem("d"), 0, eng]
        ow.dsem[1] += 16
        tok = (ow.dsem[0], ow.dsem[1])
        self.streams[eng].append((waits, fn, (ow.dsem[0], 16)))
        for b in reads:
            b.r.append(tok)
        for b in writes:
            b.w = [tok]
            b.r = []
        return tok

    def barrier(self):
        toks = []
        for b in self.bufs:
            toks += b.w
            toks += b.r
        for e in self.ENGS:
            if self.ecnt[e] > 0:
                toks.append((self.esem[e], self.ecnt[e]))
        for e in self.ENGS:
            waits = self._deps(e, (), (), extra=toks)
            self.streams[e].append((waits, None, None))
        for b in self.bufs:
            b.w = []
            b.r = []
        self.free_ctrs += self.pending_ctrs
        self.pending_ctrs = []

    def emit(self, block):
        def run(stream):
            def body(e):
                for waits, fn, inc in stream:
                    for (s, v) in waits:
                        e.wait_ge(s, v)
                    if fn is not None:
                        fn(e).then_inc(inc[0], inc[1])
            return body
        block.tensor(run(self.streams["pe"]))
        block.scalar(run(self.streams["act"]))
        block.vector(run(self.streams["dve"]))
        block.gpsimd(run(self.streams["pool"]))
        block.sync(run(self.streams["sp"]))


def _gammas():
    return (1.0 - np.exp2(-5.0 - np.arange(8, dtype=np.float64)))


CONST_LAYOUT = {}


def make_consts():
    cols = []
    off = [0]

    def put(name, arr):
        a = np.zeros((128, arr.shape[1]), np.float32)
        a[:arr.shape[0]] = arr
        CONST_LAYOUT[name] = (off[0], arr.shape[1])
        off[0] += arr.shape[1]
        cols.append(a)
    i = np.arange(128)
    put("ident", np.eye(128, dtype=np.float32))
    put("causal", (i[:, None] <= i[None, :]).astype(np.float32))
    put("strict", (i[:, None] > i[None, :]).astype(np.float32))
    put("ones", np.ones((128, 128), np.float32))
    g = _gammas()
    dk = 256 ** -0.5
    retD = np.zeros((128, 8 * 128), np.float64)
    gpow = np.zeros((128, 8 * 128), np.float64)
    kds = np.zeros((128, 8), np.float64)
    for h in range(8):
        dlt = (i[None, :] - i[:, None])
        m = np.where(dlt >= 0, g[h] ** np.maximum(dlt, 0), 0.0) * dk
        retD[:, h * 128:(h + 1) * 128] = m
        gpow[:, h * 128:(h + 1) * 128] = (g[h] ** (i + 1))[None, :]
        kds[:, h] = g[h] ** (127 - i) * dk
    put("kds", kds.astype(np.float32))
    j = np.arange(64)
    tt, sq = j // 16, j % 16
    same = (sq[:, None] == sq[None, :])
    put("causal_s", (same & (tt[:, None] <= tt[None, :])).astype(np.float32))
    put("strict_s", (same & (tt[:, None] > tt[None, :])).astype(np.float32))
    put("incl_s", (same & (tt[:, None] <= tt[None, :])).astype(np.float32))
    put("seqmask", (sq[:, None] == np.arange(16)[None, :]).astype(np.float32))
    put("same_s", same.astype(np.float32))
    retDs = np.zeros((64, 8 * 64), np.float64)
    gpows = np.zeros((128, 8 * 64), np.float64)
    kdss = np.zeros((64, 8), np.float64)
    for h in range(8):
        dlt = tt[None, :] - tt[:, None]
        m = np.where(same & (dlt >= 0), g[h] ** np.maximum(dlt, 0), 0.0) * dk
        retDs[:, h * 64:(h + 1) * 64] = m
        gpows[:, h * 64:(h + 1) * 64] = (g[h] ** (tt + 1))[None, :]
        kdss[:, h] = g[h] ** (3 - tt) * dk
    put("kds_s", kdss.astype(np.float32))
    cgrp = np.zeros((8, 128, 384), np.float32)
    for h in range(8):
        cgrp[h, :, 0:128] = retD[:, h * 128:(h + 1) * 128]
        cgrp[h, :, 128:256] = gpow[:, h * 128:(h + 1) * 128]
        cgrp[h, 0:64, 256:320] = retDs[:, h * 64:(h + 1) * 64]
        cgrp[h, :, 320:384] = gpows[:, h * 64:(h + 1) * 64]
    return np.ascontiguousarray(np.concatenate(cols, axis=1)), cgrp


def rope_tables(pos):
    half = 128
    inv = (1.0 / (10000.0 ** (np.arange(half, dtype=np.float32) / np.float32(half)))).astype(np.float32)
    ang = pos.astype(np.float32)[None, :] * inv[:, None]
    return np.cos(ang).astype(np.float32), np.sin(ang).astype(np.float32)


def build_program():
    nc = bass.Bass("TRN2", target_bir_lowering=False)
    NCOL = sum(v[1] for v in CONST_LAYOUT.values())

    def din(name, shape, dt=F32):
        return nc.dram_tensor(name, list(shape), dt, kind="ExternalInput").ap()

    def dout(name, shape, dt=F32):
        return nc.dram_tensor(name, list(shape), dt, kind="ExternalOutput").ap()

    xm = din("xm", [NM, D]); xp = din("xp", [NP_, D]); xs_ = din("xs", [NS, D])
    flag_d = din("flag", [128, 1]); consts_d = din("consts", [128, NCOL]); cgrp_d = din("cgrp", [8, 128, 384])
    cosM_d = din("cosM", [128, NM]); sinM_d = din("sinM", [128, NM])
    cosP_d = din("cosP", [128, NP_]); sinP_d = din("sinP", [128, NP_])
    cosS_d = din("cosS", [128, NS]); sinS_d = din("sinS", [128, NS])
    w_in = din("w_in", [D, IN_DIM]); wssm = din("wssm", [4096, D]); wret = din("wret", [4096, D])
    wout = din("wout", [D, D])
    wpre_d = din("wpre_b", [128, D]); wpost_d = din("wpost_b", [128, D]); normw_d = din("normw_b", [128, 4096])
    convw_d = din("convw", [128, 48, 4]); convb_d = din("convb", [128, 48])
    dtb_d = din("dtb_b", [128, 64]); alog_d = din("alog_b", [128, 64]); dsk_d = din("dsk_b", [128, 64])
    cconv_d = din("cconv", [48, 6144]); sssm_d = din("sssm", [16, 4096, 128]); sret_d = din("sret", [16, 4096, 256])

    ym = dout("ym", [NM, D]); ys = dout("ys", [NS, D])
    convp = dout("convp", [3, 6144]); ssmp = dout("ssmp", [4096, 128]); retp = dout("retp", [4096, 256])
    convs = dout("convs", [16, 3, 6144]); ssms = dout("ssms", [16, 4096, 128]); rets = dout("rets", [16, 4096, 256])

    YTd = nc.dram_tensor("YTd", [32, 128, NT], BF16).ap()
    OTd = nc.dram_tensor("OTd", [32, 128, NT], BF16).ap()
    STs_d = nc.dram_tensor("STs_d", [8, 128, 512], F32).ap()
    STr_d = nc.dram_tensor("STr_d", [8, 128, 1024], F32).ap()

    with contextlib.ExitStack() as st0:
        P = Plan(nc, st0)

        uniq = [0]

        def sbuf(stack, name, shape, dt=F32):
            uniq[0] += 1
            nm = f"{name}_{uniq[0]}"
            t = stack.enter_context(nc.sbuf_tensor(nm, list(shape), dt))
            b = P.buf(nm)
            if stack is not st0:
                stack.callback(P.release, b)
            return t, b

        def ZERO(ap, b):
            P.op("dve", lambda e: e.memset(ap, 0.0), [], [b])

        ps = []
        for i in range(6):
            t = st0.enter_context(nc.psum_tensor(f"ps{i}", [128, 512], F32))
            ps.append((t, P.buf(f"ps{i}")))
        psT, bpsT = st0.enter_context(nc.psum_tensor("psT", [128, 512 if os.environ.get("K_NOHALF") == "1" else 1024], BF16)), P.buf("psT")
        psM, bpsM = st0.enter_context(nc.psum_tensor("psM", [128, 256 if os.environ.get("K_NOHALF") == "1" else 512], F32)), P.buf("psM")

        def MM(out, lhsT, rhs, start, stop, R, W):
            P.op("pe", lambda e: e.matmul(out, lhsT=lhsT, rhs=rhs, start=start, stop=stop), R, W)

        def TR(out, in_, ident, R, W):
            P.op("pe", lambda e: e.transpose(out=out, in_=in_, identity=ident), R, W)

        def ACT(out, in_, func, R, W, **kw):
            P.op("act", lambda e: e.activation(out=out, in_=in_, func=func, **kw), R, W)

        def TT(out, a, b, op, R, W):
            P.op("dve", lambda e: e.tensor_tensor(out=out, in0=a, in1=b, op=op), R, W)

        def TS(out, a, s1, s2, op0, op1, R, W):
            if s2 is None:
                P.op("dve", lambda e: e.tensor_scalar(out=out, in0=a, scalar1=s1, scalar2=None, op0=op0), R, W)
            else:
                P.op("dve", lambda e: e.tensor_scalar(out=out, in0=a, scalar1=s1, scalar2=s2, op0=op0, op1=op1), R, W)

        def STT(out, a, s, b, op0, op1, R, W):
            P.op("dve", lambda e: e.scalar_tensor_tensor(out=out, in0=a, scalar=s, in1=b, op0=op0, op1=op1), R, W)

        def CP(out, in_, R, W):
            P.op("dve", lambda e: e.tensor_copy(out=out, in_=in_), R, W)

        def LD(out, in_, W, owner, R=(), eng="sp"):
            P.dma(eng, lambda e: e.dma_start(out=out, in_=in_), R, W, owner)

        def LDC(out, in_, W, owner, R=()):
            P.dma("pool", lambda e: e.dma_start(out=out, in_=in_), R, W, owner)

        def LDNC(out, in_, W, owner, R=()):
            P.dma("sp", lambda e: e.dma_start(out=out, in_=in_, allow_slow_non_contiguous=True), R, W, owner)

        def rstd_from_ss(rs, brs, ss, n):
            TS(rs, ss, 1.0 / n, EPS, ALU.mult, ALU.add, [brs] if ss.tensor is rs.tensor else [brs], [brs])
            ACT(rs, rs, AF.Sqrt, [brs], [brs])
            P.op("dve", lambda e: e.reciprocal(out=rs, in_=rs), [brs], [brs])

        C, bC = sbuf(st0, "consts", [128, NCOL])
        LD(C[:], consts_d[:, :], [bC], bC)

        def cst(name, rows=128):
            o, n = CONST_LAYOUT[name]
            return C[0:rows, o:o + n]
        identb, bidb = sbuf(st0, "identb", [128, 128], BF16)
        CP(identb[:], cst("ident"), [bC], [bidb])
        flag, bflag = sbuf(st0, "flag", [128, 1])
        LD(flag[:], flag_d[:, :], [bflag], bflag)
        hTm, bhTm = sbuf(st0, "hTm", [128, 16, NM], BF16)
        hTs, bhTs = sbuf(st0, "hTs", [128, 16, NS], BF16)
        convtail, bct = sbuf(st0, "convtail", [128, 48, 3])
        convout, bco = sbuf(st0, "convout", [128, 48, 3])
        small, bsmall = sbuf(st0, "small", [128, 3, 64])
        LD(small[:, 0, :], dtb_d[:, :], [bsmall], bsmall)
        LD(small[:, 1, :], alog_d[:, :], [bsmall], bsmall)
        LD(small[:, 2, :], dsk_d[:, :], [bsmall], bsmall)
        ACT(small[:, 1, :], small[:, 1, :], AF.Exp, [bsmall], [bsmall])
        TS(small[:, 1, :], small[:, 1, :], -1.0, None, ALU.mult, None, [bsmall], [bsmall])
        convw, bcw = sbuf(st0, "convw", [128, 48, 4])
        convb, bcb = sbuf(st0, "convb", [128, 48])
        LD(convw[:], convw_d[:, :, :], [bcw], bcw)
        LD(convb[:], convb_d[:, :], [bcb], bcb)

        def phase0(stack, xsrc, ntok, hT, bhT):
            xtb = [sbuf(stack, f"p0_xt{i}", [128, D]) for i in range(2)]
            hbb = [sbuf(stack, f"p0_hb{i}", [128, D], BF16) for i in range(2)]
            wpre, bwpre = sbuf(stack, "p0_wpre", [128, D])
            st_, bst = sbuf(stack, "p0_st", [128, 2])
            LD(wpre[:], wpre_d[:, :], [bwpre], bwpre)
            for t0 in range(0, ntok, 128):
                rows = min(128, ntok - t0)
                (xt, bxt), (hb, bhb) = xtb[(t0 // 128) % 2], hbb[(t0 // 128) % 2]
                LD(xt[0:rows, :], xsrc[t0:t0 + rows, :], [bxt], bxt)
                ZERO(st_[:, 0:1], bst)
                ACT(hb[0:rows, :], xt[0:rows, :], AF.Square, [bxt], [bhb, bst], accum_out=st_[0:rows, 0:1])
                TS(st_[0:rows, 1:2], st_[0:rows, 0:1], 1.0 / D, EPS, ALU.mult, ALU.add, [bst], [bst])
                ACT(st_[0:rows, 1:2], st_[0:rows, 1:2], AF.Sqrt, [bst], [bst])
                P.op("dve", lambda e, rows=rows: e.reciprocal(out=st_[0:rows, 1:2], in_=st_[0:rows, 1:2]), [bst], [bst])
                STT(hb[0:rows, :], xt[0:rows, :], st_[0:rows, 1:2], wpre[0:rows, :], ALU.mult, ALU.mult,
                    [bxt, bst, bwpre, bhb], [bhb])
                for k0 in range(0, 16, 4):
                    for kk in range(4):
                        kc = k0 + kk
                        TR(psT[:, kk * 128:kk * 128 + rows], hb[0:rows, kc * 128:(kc + 1) * 128],
                           identb[0:rows, 0:rows], [bhb, bidb], [bpsT])
                    src = psT[:, 0:512].rearrange("p (a b) -> p a b", b=128)[:, :, 0:rows]
                    ACT(hT[:, k0:k0 + 4, t0:t0 + rows], src, AF.Copy, [bpsT], [bhT])

        BLK = [False]

        def interleave(gens):
            gens = list(gens)
            seq = os.environ.get("K_SEQ") == "1"
            while gens:
                for gg in list(gens):
                    try:
                        while True:
                            BLK[0] = False
                            next(gg)
                            if not seq or BLK[0]:
                                break
                    except StopIteration:
                        gens.remove(gg)

        def phase1(stack, hT, bhT, ntok, cos_d, sin_d, mode):
            main = (mode == "main")
            SCL = SC if main else 512
            samp = main and WITH_SAMPLE
            nsc = min(ntok // SCL, int(os.environ.get('K_NSC', '99')))
            NG = int(os.environ.get('K_G', '8'))
            Wf, bWf = sbuf(stack, "Wf", [128, 16, 1280 if main else 1024], BF16)
            Wt, bWt = sbuf(stack, "Wt", [128, 16, 1536 if main else 512], BF16)
            if main:
                normw, bnw = sbuf(stack, "normw", [128, 512])
                cg, bcg = sbuf(stack, "cg", [128, 384])
                zs, bzs = sbuf(stack, "zs", [128, 512])
                qkm, bqkm = sbuf(stack, "qkm", [128, 128])
                Ld, bLd = sbuf(stack, "Ld", [128, 4, 128])
                segT, bsegT = sbuf(stack, "segT", [128, 512])
                MT, bMT = sbuf(stack, "MT", [128, 8, 128], BF16)
                xdt, bxdt = sbuf(stack, "xdt", [128, 512], BF16)
                xsk, bxsk = sbuf(stack, "xsk", [128, 512])
                ybar, bybar = sbuf(stack, "ybar", [128, 512], BF16)
                stage, bstage = sbuf(stack, "stage", [128, 4, 128], BF16)
                MTr, bMTr = sbuf(stack, "MTr", [128, 128], BF16)
                qs, bqs = sbuf(stack, "qs", [128, 2, 128], BF16)
                R0buf = [sbuf(stack, f"R0n{i}", [128, 4, 256]) for i in range(2)]
            if samp:
                csS, bcsS = sbuf(stack, "csS", [128, 2, 64])
                LD(csS[:, 0, :], cosS_d[:, :], [bcsS], bcsS)
                LD(csS[:, 1, :], sinS_d[:, :], [bcsS], bcsS)
                S0buf = [sbuf(stack, f"S0n{i}", [128, 4, 128]) for i in range(2)]
                S0b, bS0b = sbuf(stack, "S0b", [128, 4, 128], BF16)
                R0b, bR0b = sbuf(stack, "R0b", [128, 4, 256], BF16)
                cdcol, bcdcol = sbuf(stack, "cdcol", [128, 4, 16])
                Bm, bBm = sbuf(stack, "Bm", [128, 128], BF16)
                kdm, bkdm = sbuf(stack, "kdm", [128, 256], BF16)
                raws_t, braws = sbuf(stack, "raws", [128, 6, 112])
                raws = raws_t[:, :, :].rearrange("p b (t s) -> p b t s", s=16)
                cTs, bcTs = sbuf(stack, "cTs", [128, 6, 64], BF16)
                qkTs, bqkTs = sbuf(stack, "qkTs", [128, 4, 64], BF16)
                yinT_t, byinT = sbuf(stack, "yinT", [128, 4, 64])
                yinT = yinT_t[:, :, :]
            Wdt, bWdt = sbuf(stack, "Wdt", [128, 16, 64], BF16)
            LDC(Wdt[:], w_in[:, ODT:ODT + 64].rearrange("(kc p) n -> p kc n", p=128), [bWdt], bWdt)
            raw, braw = sbuf(stack, "raw", [128, 6, SCL + 3])
            acc, bacc = sbuf(stack, "acc", [128, 512])
            cc, bcc = acc[:, 256:384], bacc
            cTb = [sbuf(stack, f"cT{i}", [128, 6, SCL], BF16) for i in range(2)]
            qkTb = [sbuf(stack, f"qkT{i}", [128, 4, SCL], BF16) for i in range(2)]
            cosT, bcos = sbuf(stack, "cosT", [128, SCL])
            sinT, bsin = sbuf(stack, "sinT", [128, SCL])
            r1, br1 = sbuf(stack, "r1", [128, 512])
            r2, br2 = sbuf(stack, "r2", [128, 512])
            y1, by1 = sbuf(stack, "y1", [128, 512])
            y2, by2 = y1, by1
            sm, bsm = sbuf(stack, "sm", [128, 12, 8])
            xdd, bxdd = sbuf(stack, "xdd", [128, 512], BF16)
            Btok, bBtok = sbuf(stack, "Btok", [128, 128], BF16)
            ST, bST = sbuf(stack, "ST", [128, 512])
            STb, bSTb = sbuf(stack, "STb", [128, 512], BF16)
            tmpS, btmpS = y1, by1
            RST, bRST = sbuf(stack, "RST", [128, 2, 512])
            RSTb, bRSTb = sbuf(stack, "RSTb", [128, 2, 512], BF16)
            vb, bvb = sbuf(stack, "vb", [128, 512], BF16)
            kd, bkd = sbuf(stack, "kd", [128, 256], BF16)
            rs, brs = sbuf(stack, "rs", [128, 2])
            ai = [0]
            a_done = set()
            b_done = set()
            as_done = set()
            bs_done = set()
            pB, bpB = ps[4]
            p2, bp2 = ps[2]
            p3, bp3 = ps[3]
            p5, bp5 = ps[5]
            identf = cst("ident")
            TH = [0]

            def psT_half():
                if os.environ.get('K_NOHALF') != '1':
                    TH[0] ^= 1
                return TH[0] * 512

            def abank():
                ai[0] ^= 1
                return ps[ai[0]]

            def wld(dst, bdst, lo, c0, n):
                LDC(dst[:, :, lo:lo + n], w_in[:, c0:c0 + n].rearrange("(kc p) n -> p kc n", p=128), [bdst], bdst)

            def rms_to(out_bf, bout, src, bsrc, mul, bmul, S=slice(0, 128)):
                ZERO(rs[:, 0:1], brs)
                ACT(out_bf, src, AF.Square, [bsrc], [bout, brs], accum_out=rs[S, 0:1])
                TS(rs[S, 1:2], rs[S, 0:1], 1.0 / 512, EPS, ALU.mult, ALU.add, [brs], [brs])
                ACT(rs[S, 1:2], rs[S, 1:2], AF.Sqrt, [brs], [brs])
                P.op("dve", lambda e: e.reciprocal(out=rs[S, 1:2], in_=rs[S, 1:2]), [brs], [brs])
                STT(out_bf, src, rs[S, 1:2], mul, ALU.mult, ALU.mult, [bsrc, brs, bmul], [bout])

            def gen_A():
                for g in range(NG):
                    cblk = [g * 4 + 0, g * 4 + 1, g * 4 + 2, g * 4 + 3, 32 + g, 40 + g]
                    wld(Wf, bWf, 0, OX + g * 512, 512)
                    wld(Wf, bWf, 512, OB + g * 128, 128)
                    wld(Wf, bWf, 640, OC + g * 128, 128)
                    wld(Wf, bWf, 768, OK_ + g * 256, 256)
                    if main:
                        wld(Wf, bWf, 1024, OQ + g * 256, 256)
                        CP(raw[:, :, 0:3], convtail[:, g * 6:(g + 1) * 6, :], [bct], [braw])
                    else:
                        P.op("dve", lambda e: e.memset(raw[:, :, 0:3], 0.0), [], [braw])
                    for sc in range(nsc):
                        idx = g * nsc + sc
                        while idx >= 2 and (idx - 2) not in b_done:
                            BLK[0] = True
                            yield
                        cT, bcT = cTb[idx % 2]
                        qkT, bqkT = qkTb[idx % 2]
                        T0 = sc * SCL
                        LD(cosT[:], cos_d[:, T0:T0 + SCL], [bcos], bcos)
                        LD(sinT[:], sin_d[:, T0:T0 + SCL], [bsin], bsin)
                        for bi in range(6):
                            pt, bpt = abank()
                            for kc in range(16):
                                MM(pt[:, 0:SCL], Wf[:, kc, bi * 128:(bi + 1) * 128], hT[:, kc, T0:T0 + SCL], kc == 0, kc == 15,
                                   [bWf, bhT], [bpt])
                            ACT(raw[:, bi, 3:SCL + 3], pt[:, 0:SCL], AF.Copy, [bpt], [braw])
                            yield
                        for bi in range(6):
                            cb = cblk[bi]
                            TS(acc[:, 0:SCL], raw[:, bi, 3:SCL + 3], convw[:, cb, 3:4], convb[:, cb:cb + 1], ALU.mult, ALU.add,
                               [braw, bcw, bcb], [bacc])
                            for tap in (2, 1, 0):
                                STT(acc[:, 0:SCL], raw[:, bi, tap:tap + SCL], convw[:, cb, tap:tap + 1], acc[:, 0:SCL], ALU.mult,
                                    ALU.add, [braw, bcw, bacc], [bacc])
                            ACT(cT[:, bi, :], acc[:, 0:SCL], AF.Silu, [bacc], [bcT])
                            yield
                        CP(raw[:, :, 0:3], raw[:, :, SCL:SCL + 3], [braw], [braw])
                        for which in ((0, 1) if main else (1,)):
                            lo0 = 1024 if which == 0 else 768
                            p0, bp0 = ps[0]
                            p1, bp1 = ps[1]
                            for hf, (pt, bpt) in enumerate(((p0, bp0), (p1, bp1))):
                                for kc in range(16):
                                    MM(pt[:, 0:SCL], Wf[:, kc, lo0 + hf * 128:lo0 + (hf + 1) * 128], hT[:, kc, T0:T0 + SCL],
                                       kc == 0, kc == 15, [bWf, bhT], [bpt])
                                yield
                            TT(r1[:, 0:SCL], p0[:, 0:SCL], cosT[:], ALU.mult, [bp0, bcos], [br1])
                            TT(r2[:, 0:SCL], p1[:, 0:SCL], sinT[:], ALU.mult, [bp1, bsin], [br2])
                            TT(qkT[:, which * 2, :], r1[:, 0:SCL], r2[:, 0:SCL], ALU.subtract, [br1, br2], [bqkT])
                            TT(r1[:, 0:SCL], p1[:, 0:SCL], cosT[:], ALU.mult, [bp1, bcos], [br1])
                            TT(r2[:, 0:SCL], p0[:, 0:SCL], sinT[:], ALU.mult, [bp0, bsin], [br2])
                            TT(qkT[:, which * 2 + 1, :], r1[:, 0:SCL], r2[:, 0:SCL], ALU.add, [br1, br2], [bqkT])
                            yield
                        a_done.add(idx)
                        yield
                    if main:
                        CP(convout[:, g * 4:g * 4 + 4, :], raw[:, 0:4, 0:3], [braw], [bco])
                        CP(convout[:, 32 + g, :], raw[:, 4, 0:3], [braw], [bco])
                        CP(convout[:, 40 + g, :], raw[:, 5, 0:3], [braw], [bco])
                    else:
                        TS(convtail[:, g * 6:(g + 1) * 6, :], raw[:, :, 0:3], flag[:, 0:1], None, ALU.mult, None,
                           [braw, bflag], [bct])
                    if samp:
                        while g >= 1 and (g - 1) not in bs_done:
                            BLK[0] = True
                            yield
                        for (lo, n, c0) in ((0, 512, g * 512), (512, 128, 4096 + g * 128), (640, 128, 5120 + g * 128)):
                            pt, bpt = abank()
                            for kc in range(16):
                                MM(pt[0:64, 0:n], hTs[:, kc, 0:64], Wf[:, kc, lo:lo + n], kc == 0, kc == 15, [bhTs, bWf], [bpt])
                            ACT(r1[0:64, 0:n], pt[0:64, 0:n], AF.Copy, [bpt], [br1])
                            for t in (1, 2, 3):
                                LD(convs[:, t - 1, c0:c0 + n], r1[16 * t:16 * t + 16, 0:n], [], br1, R=[br1])
                            yield
                        for bi in range(6):
                            cb = cblk[bi]
                            LD(cc[0:48, :], cconv_d[:, cb * 128:(cb + 1) * 128], [bcc], bcc)
                            pt, bpt = abank()
                            TR(pt[:, 0:48], cc[0:48, :], identf[0:48, 0:48], [bcc, bC], [bpt])
                            ACT(raws[:, bi, 0:3, :], pt[:, 0:48].rearrange("p (t s) -> p t s", s=16), AF.Copy, [bpt], [braws])
                            pt, bpt = abank()
                            for kc in range(16):
                                MM(pt[:, 0:64], Wf[:, kc, bi * 128:(bi + 1) * 128], hTs[:, kc, 0:64], kc == 0, kc == 15,
                                   [bWf, bhTs], [bpt])
                            ACT(raws[:, bi, 3:7, :], pt[:, 0:64].rearrange("p (t s) -> p t s", s=16), AF.Copy, [bpt], [braws])
                            yield
                        for bi in range(6):
                            cb = cblk[bi]
                            rf = raws_t[:, bi, :]
                            TS(acc[:, 0:64], rf[:, 48:112], convw[:, cb, 3:4], convb[:, cb:cb + 1], ALU.mult, ALU.add,
                               [braws, bcw, bcb], [bacc])
                            for tap in (2, 1, 0):
                                STT(acc[:, 0:64], rf[:, tap * 16:tap * 16 + 64], convw[:, cb, tap:tap + 1], acc[:, 0:64],
                                    ALU.mult, ALU.add, [braws, bcw, bacc], [bacc])
                            ACT(cTs[:, bi, :], acc[:, 0:64], AF.Silu, [bacc], [bcTs])
                        yield
                        for which in (0, 1):
                            lo0 = 1024 if which == 0 else 768
                            p0, bp0 = ps[0]
                            p1, bp1 = ps[1]
                            for hf, (pt, bpt) in enumerate(((p0, bp0), (p1, bp1))):
                                for kc in range(16):
                                    MM(pt[:, 0:64], Wf[:, kc, lo0 + hf * 128:lo0 + (hf + 1) * 128], hTs[:, kc, 0:64], kc == 0,
                                       kc == 15, [bWf, bhTs], [bpt])
                            cS_, sS_ = csS[:, 0, :], csS[:, 1, :]
                            TT(r1[:, 0:64], p0[:, 0:64], cS_, ALU.mult, [bp0, bcsS], [br1])
                            TT(r2[:, 0:64], p1[:, 0:64], sS_, ALU.mult, [bp1, bcsS], [br2])
                            TT(qkTs[:, which * 2, :], r1[:, 0:64], r2[:, 0:64], ALU.subtract, [br1, br2], [bqkTs])
                            TT(r1[:, 0:64], p1[:, 0:64], cS_, ALU.mult, [bp1, bcsS], [br1])
                            TT(r2[:, 0:64], p0[:, 0:64], sS_, ALU.mult, [bp0, bcsS], [br2])
                            TT(qkTs[:, which * 2 + 1, :], r1[:, 0:64], r2[:, 0:64], ALU.add, [br1, br2], [bqkTs])
                            yield
                        as_done.add(g)
                        yield

            def decay_scalars(g, hsrc, bhsrc, tk, S, incl_ap, tot_ap):
                n = S.stop
                for kc in range(16):
                    MM(psM[S, 0:8], hsrc[:, kc, tk:tk + n], Wdt[:, kc, g * 8:(g + 1) * 8], kc == 0, kc == 15,
                       [bhsrc, bWdt], [bpsM])
                TT(sm[S, 0, :], psM[S, 0:8], small[S, 0, g * 8:(g + 1) * 8], ALU.add, [bpsM, bsmall], [bsm])
                ACT(sm[S, 0, :], sm[S, 0, :], AF.Exp, [bsm], [bsm])
                ACT(sm[S, 1, :], sm[S, 0, :], AF.Ln, [bsm], [bsm], bias=1.0)
                TT(sm[S, 2, :], sm[S, 1, :], small[S, 1, g * 8:(g + 1) * 8], ALU.mult, [bsm, bsmall], [bsm])
                MM(psM[S, 8:16], incl_ap, sm[S, 2, :], True, True, [bC, bsm], [bpsM])
                MM(psM[S, 16:24], tot_ap, sm[S, 2, :], True, True, [bC, bsm], [bpsM])
                ACT(sm[S, 3, :], psM[S, 8:16], AF.Copy, [bpsM], [bsm])
                TT(sm[S, 4, :], psM[S, 16:24], sm[S, 3, :], ALU.subtract, [bpsM, bsm], [bsm])
                ACT(sm[S, 5, :], sm[S, 4, :], AF.Exp, [bsm], [bsm])
                ACT(sm[S, 6, :], sm[S, 3, :], AF.Exp, [bsm], [bsm])
                ACT(sm[S, 7, :], psM[S, 16:24], AF.Exp, [bpsM], [bsm])
                TT(sm[S, 8, :], sm[S, 1, :], sm[S, 5, :], ALU.mult, [bsm], [bsm])

            def gen_B():
                for g in range(NG):
                    wld(Wt, bWt, 0, OV + g * 512, 512)
                    if main:
                        wld(Wt, bWt, 512, OZ + g * 512, 512)
                        wld(Wt, bWt, 1024, OG + g * 512, 512)
                        LD(normw[:], normw_d[:, g * 512:(g + 1) * 512], [bnw], bnw)
                        LD(cg[:], cgrp_d[g], [bcg], bcg)
                        LD(ST[:], STs_d[g], [bST], bST)
                        LD(RST[:], STr_d[g].rearrange("p (a b) -> p a b", b=512), [bRST], bRST)
                    else:
                        P.op("dve", lambda e: e.memset(ST[:], 0.0), [], [bST])
                        P.op("dve", lambda e: e.memset(RST[:], 0.0), [], [bRST])
                    ACT(STb[:], ST[:], AF.Copy, [bST], [bSTb])
                    ACT(RSTb[:], RST[:], AF.Copy, [bRST], [bRSTb])
                    gQ = float(_gammas()[g] ** 128)
                    oK = CONST_LAYOUT["kds"][0]
                    for sc in range(nsc):
                        idx = g * nsc + sc
                        while idx not in a_done:
                            BLK[0] = True
                            yield
                        cT, bcT = cTb[idx % 2]
                        qkT, bqkT = qkTb[idx % 2]
                        T0 = sc * SCL
                        for c in range(SCL // 128):
                            lc = c * 128
                            tk = T0 + lc
                            BC = int(os.environ.get('K_BCUT', '9'))
                            if BC >= 1:
                                decay_scalars(g, hT, bhT, tk, slice(0, 128), cst("causal"), cst("ones"))
                            yield
                            if BC < 2:
                                continue
                            for kc in range(16):
                                MM(pB[:, :], hT[:, kc, tk:tk + 128], Wt[:, kc, 0:512], kc == 0, kc == 15, [bhT, bWt], [bpB])
                            ACT(vb[:], pB[:, :], AF.Copy, [bpB], [bvb])
                            yield
                            if BC < 3:
                                continue
                            h0 = psT_half()
                            for j in range(4):
                                TR(psT[:, h0 + j * 128:h0 + (j + 1) * 128], cT[:, j, lc:lc + 128], identb[:], [bcT, bidb], [bpsT])
                            xs3 = psT[:, h0:h0 + 512].rearrange("p (r q) -> p r q", q=64)

                            def b8(row):
                                return sm[:, row, :].unsqueeze(2).broadcast_to([128, 8, 64])
                            TT(xdd[:].rearrange("p (r q) -> p r q", q=64), xs3, b8(8), ALU.mult, [bpsT, bsm], [bxdd])
                            if main:
                                TT(xdt[:].rearrange("p (r q) -> p r q", q=64), xs3, b8(1), ALU.mult, [bpsT, bsm], [bxdt])
                                dskb = small[:, 2, g * 8:(g + 1) * 8].unsqueeze(2).broadcast_to([128, 8, 64])
                                TT(xsk[:].rearrange("p (r q) -> p r q", q=64), xs3, dskb, ALU.mult, [bpsT, bsmall], [bxsk])
                            if BC < 4:
                                continue
                            h1 = psT_half()
                            TR(psT[:, h1:h1 + 128], cT[:, 4, lc:lc + 128], identb[:], [bcT, bidb], [bpsT])
                            for hf in range(2):
                                TR(psT[:, h1 + 128 + hf * 128:h1 + 256 + hf * 128], qkT[:, 2 + hf, lc:lc + 128], identb[:],
                                   [bqkT, bidb], [bpsT])
                            ACT(Btok[:], psT[:, h1:h1 + 128], AF.Copy, [bpsT], [bBtok, bpsT])
                            TS(kd[:], psT[:, h1 + 128:h1 + 384], C[:, oK + g:oK + g + 1], None, ALU.mult, None, [bpsT, bC], [bkd])
                            yield
                            if main:
                                MM(psM[:, 128:256], cT[:, 4, lc:lc + 128], cT[:, 5, lc:lc + 128], True, True, [bcT], [bpsM])
                                TT(qkm[:], psM[:, 128:256], cst("causal"), ALU.mult, [bpsM, bC], [bqkm])
                                for hh in range(2):
                                    TT(Ld[:], cst("strict").unsqueeze(1).broadcast_to([128, 4, 128]),
                                       sm[:, 2, hh * 4:hh * 4 + 4].unsqueeze(2).broadcast_to([128, 4, 128]), ALU.mult,
                                       [bC, bsm], [bLd])
                                    for r in range(4):
                                        MM(p2[:, r * 128:(r + 1) * 128], Ld[:, r, :], cst("causal"), True, True, [bLd, bC], [bp2])
                                    ACT(segT[:], p2[:, :], AF.Exp, [bp2], [bsegT])
                                    TT(MT[:, hh * 4:hh * 4 + 4, :], segT[:].rearrange("p (r l) -> p r l", l=128),
                                       qkm[:].unsqueeze(1).broadcast_to([128, 4, 128]), ALU.mult, [bsegT, bqkm], [bMT])
                                    yield
                                for r in range(8):
                                    MM(p3[:, r * 64:(r + 1) * 64], MT[:, r, :], xdt[:, r * 64:(r + 1) * 64], True, True,
                                       [bMT, bxdt], [bp3])
                                MM(p5[:, :], cT[:, 5, lc:lc + 128], STb[:], True, True, [bcT, bSTb], [bp5])
                                for kc in range(16):
                                    MM(pB[:, :], hT[:, kc, tk:tk + 128], Wt[:, kc, 512:1024], kc == 0, kc == 15, [bhT, bWt], [bpB])
                                ACT(zs[:], pB[:, :], AF.Silu, [bpB], [bzs])
                                yield
                                TT(y1[:].rearrange("p (r q) -> p r q", q=64), p5[:, :].rearrange("p (r q) -> p r q", q=64),
                                   b8(6), ALU.mult, [bp5, bsm], [by1])
                                TT(y1[:], y1[:], p3[:, :], ALU.add, [by1, bp3], [by1])
                                TT(y1[:], y1[:], xsk[:], ALU.add, [by1, bxsk], [by1])
                                TT(y1[:], y1[:], zs[:], ALU.mult, [by1, bzs], [by1])
                                rms_to(ybar[:], bybar, y1[:], by1, normw[:], bnw)
                                h0 = psT_half()
                                for j in range(4):
                                    TR(psT[:, h0 + j * 128:h0 + (j + 1) * 128], ybar[:, j * 128:(j + 1) * 128], identb[:],
                                       [bybar, bidb], [bpsT])
                                ACT(stage[:], psT[:, h0:h0 + 512].rearrange("p (a b) -> p a b", b=128), AF.Copy, [bpsT], [bstage])
                                LD(YTd[g * 4:g * 4 + 4, :, tk:tk + 128].rearrange("j p t -> p j t"), stage[:], [], bstage,
                                   R=[bstage])
                                yield
                            if BC < 5:
                                continue
                            MM(p5[:, :], Btok[:], xdd[:], True, True, [bBtok, bxdd], [bp5])
                            TT(tmpS[:].rearrange("p (r q) -> p r q", q=64), ST[:].rearrange("p (r q) -> p r q", q=64),
                               b8(7), ALU.mult, [bST, bsm], [btmpS])
                            TT(ST[:], tmpS[:], p5[:, :], ALU.add, [btmpS, bp5], [bST])
                            ACT(STb[:], ST[:], AF.Copy, [bST], [bSTb])
                            yield
                            if main:
                                for hf in range(2):
                                    MM(psM[:, 128:256], qkT[:, 2 + hf, lc:lc + 128], qkT[:, hf, lc:lc + 128], hf == 0, hf == 1,
                                       [bqkT], [bpsM])
                                TT(MTr[:], psM[:, 128:256], cg[:, 0:128], ALU.mult, [bpsM, bcg], [bMTr])
                                TT(qs[:], qkT[:, 0:2, lc:lc + 128], cg[:, 128:256].unsqueeze(1).broadcast_to([128, 2, 128]),
                                   ALU.mult, [bqkT, bcg], [bqs])
                                MM(p3[:, :], MTr[:], vb[:], True, False, [bMTr, bvb], [bp3])
                                for hf in range(2):
                                    MM(p3[:, :], qs[:, hf, :], RSTb[:, hf, :], False, hf == 1, [bqs, bRSTb], [bp3])
                                for kc in range(16):
                                    MM(pB[:, :], hT[:, kc, tk:tk + 128], Wt[:, kc, 1024:1536], kc == 0, kc == 15, [bhT, bWt], [bpB])
                                ACT(zs[:], pB[:, :], AF.Silu, [bpB], [bzs])
                                yield
                                rms_to(ybar[:], bybar, p3[:, :], bp3, zs[:], bzs)
                                h0 = psT_half()
                                for j in range(4):
                                    TR(psT[:, h0 + j * 128:h0 + (j + 1) * 128], ybar[:, j * 128:(j + 1) * 128], identb[:],
                                       [bybar, bidb], [bpsT])
                                ACT(stage[:], psT[:, h0:h0 + 512].rearrange("p (a b) -> p a b", b=128), AF.Copy, [bpsT], [bstage])
                                LD(OTd[g * 4:g * 4 + 4, :, tk:tk + 128].rearrange("j p t -> p j t"), stage[:], [], bstage,
                                   R=[bstage])
                                yield
                            if BC < 6:
                                continue
                            for hf in range(2):
                                pq, bpq = (p5, bp5) if hf == 0 else (p2, bp2)
                                MM(pq[:, :], kd[:, hf * 128:(hf + 1) * 128], vb[:], True, True, [bkd, bvb], [bpq])
                                STT(RST[:, hf, :], RST[:, hf, :], gQ, pq[:, :], ALU.mult, ALU.add, [bRST, bpq], [bRST])
                            ACT(RSTb[:], RST[:], AF.Copy, [bRST], [bRSTb])
                            yield
                        b_done.add(idx)
                        yield
                    if samp:
                        while g not in as_done:
                            BLK[0] = True
                            yield
                        yield from sample_B(g)
                        bs_done.add(g)
                    if not main:
                        TS(ST[:], ST[:], flag[:, 0:1], None, ALU.mult, None, [bST, bflag], [bST])
                        TS(RST[:], RST[:], flag[:, 0:1], None, ALU.mult, None, [bRST, bflag], [bRST])
                        LD(STs_d[g], ST[:], [], bST, R=[bST])
                        LD(STr_d[g].rearrange("p (a b) -> p a b", b=512), RST[:], [], bRST, R=[bRST])
                    else:
                        stf, bstf = R0buf[0]
                        for j in range(4):
                            TR(p2[:, j * 128:(j + 1) * 128], ST[:, j * 128:(j + 1) * 128], identf, [bST, bC], [bp2])
                        ACT(stf[:, :, 0:128], p2[:, :].rearrange("p (a b) -> p a b", b=128), AF.Copy, [bp2], [bstf])
                        LD(ssmp[g * 512:(g + 1) * 512, :].rearrange("(j p) n -> p j n", p=128), stf[:, :, 0:128], [], bstf,
                           R=[bstf])
                        stf, bstf = R0buf[1]
                        for hf in range(2):
                            for j in range(4):
                                TR(p2[:, j * 128:(j + 1) * 128], RST[:, hf, j * 128:(j + 1) * 128], identf, [bRST, bC], [bp2])
                            ACT(stf[:, :, hf * 128:(hf + 1) * 128], p2[:, :].rearrange("p (a b) -> p a b", b=128), AF.Copy,
                                [bp2], [bstf])
                        LD(retp[g * 512:(g + 1) * 512, :].rearrange("(j p) k -> p j k", p=128), stf[:], [], bstf, R=[bstf])
                    yield

            def sample_B(g):
                seqm = cst("seqmask", 64)
                S = slice(0, 64)
                decay_scalars(g, hTs, bhTs, 0, S, cst("incl_s", 64), cst("same_s", 64))
                CP(segT[S, :].rearrange("p (r q) -> p r q", q=64), sm[S, 2, :].unsqueeze(2).broadcast_to([64, 8, 64]),
                   [bsm], [bsegT])
                for j in range(4):
                    MM(psM[:, 32 + j * 16:32 + (j + 1) * 16], segT[S, j * 128:(j + 1) * 128], seqm, True, True,
                       [bsegT, bC], [bpsM])
                ACT(cdcol[:], psM[:, 32:96].rearrange("p (j s) -> p j s", s=16), AF.Exp, [bpsM], [bcdcol])
                yield

                def b8s(row):
                    return sm[S, row, :].unsqueeze(2).broadcast_to([64, 8, 64])
                h0 = psT_half()
                for j in range(4):
                    TR(psT[S, h0 + j * 128:h0 + (j + 1) * 128], cTs[:, j, :], identb[:], [bcTs, bidb], [bpsT])
                xs3 = psT[S, h0:h0 + 512].rearrange("p (r q) -> p r q", q=64)
                TT(xdd[S, :].rearrange("p (r q) -> p r q", q=64), xs3, b8s(8), ALU.mult, [bpsT, bsm], [bxdd])
                TT(xdt[S, :].rearrange("p (r q) -> p r q", q=64), xs3, b8s(1), ALU.mult, [bpsT, bsm], [bxdt])
                dskb = small[S, 2, g * 8:(g + 1) * 8].unsqueeze(2).broadcast_to([64, 8, 64])
                TT(xsk[S, :].rearrange("p (r q) -> p r q", q=64), xs3, dskb, ALU.mult, [bpsT, bsmall], [bxsk])
                h1 = psT_half()
                TR(psT[S, h1:h1 + 128], cTs[:, 4, :], identb[:], [bcTs, bidb], [bpsT])
                for hf in range(2):
                    TR(psT[S, h1 + 128 + hf * 128:h1 + 256 + hf * 128], qkTs[:, 2 + hf, :], identb[:], [bqkTs, bidb], [bpsT])
                ACT(Btok[S, :], psT[S, h1:h1 + 128], AF.Copy, [bpsT], [bBtok, bpsT])
                oK = CONST_LAYOUT["kds_s"][0]
                TS(kd[S, :], psT[S, h1 + 128:h1 + 384], C[S, oK + g:oK + g + 1], None, ALU.mult, None, [bpsT, bC], [bkd])
                yield
                MM(psM[S, 128:192], cTs[:, 4, :], cTs[:, 5, :], True, True, [bcTs], [bpsM])
                TT(qkm[S, 0:64], psM[S, 128:192], cst("causal_s", 64), ALU.mult, [bpsM, bC], [bqkm])
                for hh in range(2):
                    TT(Ld[S, :, 0:64], cst("strict_s", 64).unsqueeze(1).broadcast_to([64, 4, 64]),
                       sm[S, 2, hh * 4:hh * 4 + 4].unsqueeze(2).broadcast_to([64, 4, 64]), ALU.mult, [bC, bsm], [bLd])
                    for r in range(4):
                        MM(p2[S, r * 64:(r + 1) * 64], Ld[S, r, 0:64], cst("incl_s", 64), True, True, [bLd, bC], [bp2])
                    ACT(segT[S, 0:256], p2[S, 0:256], AF.Exp, [bp2], [bsegT])
                    TT(MT[S, hh * 4:hh * 4 + 4, 0:64], segT[S, 0:256].rearrange("p (r l) -> p r l", l=64),
                       qkm[S, 0:64].unsqueeze(1).broadcast_to([64, 4, 64]), ALU.mult, [bsegT, bqkm], [bMT])
                for r in range(8):
                    MM(p3[S, r * 64:(r + 1) * 64], MT[S, r, 0:64], xdt[S, r * 64:(r + 1) * 64], True, True, [bMT, bxdt], [bp3])
                yield
                for kc in range(16):
                    MM(pB[S, :], hTs[:, kc, 0:64], Wt[:, kc, 0:512], kc == 0, kc == 15, [bhTs, bWt], [bpB])
                ACT(vb[S, :], pB[S, :], AF.Copy, [bpB], [bvb])
                for hf in range(2):
                    MM(psM[S, 128:192], qkTs[:, 2 + hf, :], qkTs[:, hf, :], hf == 0, hf == 1, [bqkTs], [bpsM])
                TT(MTr[S, 0:64], psM[S, 128:192], cg[S, 256:320], ALU.mult, [bpsM, bcg], [bMTr])
                qs_s = qs[:, :, 0:64]
                TT(qs_s, qkTs[:, 0:2, :], cg[:, 320:384].unsqueeze(1).broadcast_to([128, 2, 64]), ALU.mult, [bqkTs, bcg], [bqs])
                MM(p2[S, :], MTr[S, 0:64], vb[S, :], True, True, [bMTr, bvb], [bp2])
                yield
                g4 = float(_gammas()[g] ** 4)

                def ssd_seqs():
                    for s in range(16):
                        S0n, bS0n = S0buf[s % 2]
                        LD(S0n[:], sssm_d[s, g * 512:(g + 1) * 512, :].rearrange("(j p) n -> p j n", p=128), [bS0n], bS0n)
                        P.op("dve", lambda e, S0n=S0n: e.tensor_copy(out=S0b[:], in_=S0n[:]), [bS0n], [bS0b])
                        h = psT_half()
                        for j in range(4):
                            TR(psT[:, h + j * 128:h + (j + 1) * 128], S0b[:, j, :], identb[:], [bS0b, bidb], [bpsT])
                        ACT(STb[:], psT[:, h:h + 512], AF.Copy, [bpsT], [bSTb])
                        for j in range(4):
                            MM(p5[:, j * 64 + s:(j + 1) * 64:16], STb[:, j * 128:(j + 1) * 128], cTs[:, 5, s:64:16], True, True,
                               [bSTb, bcTs], [bp5])
                        TS(Bm[S, :], Btok[S, :], seqm[:, s:s + 1], None, ALU.mult, None, [bBtok, bC], [bBm])
                        for j in range(4):
                            MM(pB[:, j * 128:(j + 1) * 128], xdd[S, j * 128:(j + 1) * 128], Bm[S, :], True, True,
                               [bxdd, bBm], [bpB])
                        TT(S0n[:], S0n[:], cdcol[:, :, s].unsqueeze(2).broadcast_to([128, 4, 128]), ALU.mult,
                           [bS0n, bcdcol], [bS0n])
                        TT(S0n[:], S0n[:], pB[:, :].rearrange("p (j n) -> p j n", n=128), ALU.add, [bS0n, bpB], [bS0n])
                        LD(ssms[s, g * 512:(g + 1) * 512, :].rearrange("(j p) n -> p j n", p=128), S0n[:], [], bS0n, R=[bS0n])
                        yield

                def ret_seqs():
                    for s in range(16):
                        R0n, bR0n = R0buf[s % 2]
                        LD(R0n[:], sret_d[s, g * 512:(g + 1) * 512, :].rearrange("(j p) k -> p j k", p=128), [bR0n], bR0n)
                        P.op("dve", lambda e, R0n=R0n: e.tensor_copy(out=R0b[:], in_=R0n[:]), [bR0n], [bR0b])
                        for hf in range(2):
                            h = psT_half()
                            for j in range(4):
                                TR(psT[:, h + j * 128:h + (j + 1) * 128], R0b[:, j, hf * 128:(hf + 1) * 128], identb[:],
                                   [bR0b, bidb], [bpsT])
                            ACT(RSTb[:, hf, :], psT[:, h:h + 512], AF.Copy, [bpsT], [bRSTb])
                        for j in range(4):
                            for hf in range(2):
                                MM(p5[:, 256 + j * 64 + s:256 + (j + 1) * 64:16], RSTb[:, hf, j * 128:(j + 1) * 128],
                                   qs_s[:, hf, s:64:16], hf == 0, hf == 1, [bRSTb, bqs], [bp5])
                        TS(kdm[S, :], kd[S, :], seqm[:, s:s + 1], None, ALU.mult, None, [bkd, bC], [bkdm])
                        for jj in range(2):
                            for j2 in range(2):
                                j = jj * 2 + j2
                                MM(p3[:, j2 * 256:(j2 + 1) * 256] if False else pB[:, j2 * 256:(j2 + 1) * 256],
                                   vb[S, j * 128:(j + 1) * 128], kdm[S, :], True, True, [bvb, bkdm], [bpB])
                            STT(R0n[:, jj * 2:jj * 2 + 2, :], R0n[:, jj * 2:jj * 2 + 2, :], g4,
                                pB[:, :].rearrange("p (j k) -> p j k", k=256), ALU.mult, ALU.add, [bR0n, bpB], [bR0n])
                        LD(rets[s, g * 512:(g + 1) * 512, :].rearrange("(j p) k -> p j k", p=128), R0n[:], [], bR0n, R=[bR0n])
                        yield

                subs = [ssd_seqs(), ret_seqs()]
                while subs:
                    for sg_ in list(subs):
                        try:
                            next(sg_)
                        except StopIteration:
                            subs.remove(sg_)
                    yield
                ACT(yinT, p5[:, 0:256].rearrange("p (j t) -> p j t", t=64), AF.Copy, [bp5], [byinT])
                for j in range(4):
                    TR(pB[S, j * 128:(j + 1) * 128], yinT[:, j, :], identf, [byinT, bC], [bpB])
                TT(y1[S, :].rearrange("p (r q) -> p r q", q=64), pB[S, :].rearrange("p (r q) -> p r q", q=64), b8s(6),
                   ALU.mult, [bpB, bsm], [by1])
                TT(y1[S, :], y1[S, :], p3[S, :], ALU.add, [by1, bp3], [by1])
                TT(y1[S, :], y1[S, :], xsk[S, :], ALU.add, [by1, bxsk], [by1])
                for kc in range(16):
                    MM(pB[S, :], hTs[:, kc, 0:64], Wt[:, kc, 512:1024], kc == 0, kc == 15, [bhTs, bWt], [bpB])
                ACT(zs[S, :], pB[S, :], AF.Silu, [bpB], [bzs])
                TT(y1[S, :], y1[S, :], zs[S, :], ALU.mult, [by1, bzs], [by1])
                rms_to(ybar[S, :], bybar, y1[S, :], by1, normw[S, :], bnw, S)
                h0 = psT_half()
                for j in range(4):
                    TR(psT[:, h0 + j * 64:h0 + (j + 1) * 64], ybar[S, j * 128:(j + 1) * 128], identb[0:64, 0:64],
                       [bybar, bidb], [bpsT])
                stage_s = stage[:, :, 0:64]
                ACT(stage_s, psT[:, h0:h0 + 256].rearrange("p (a b) -> p a b", b=64), AF.Copy, [bpsT], [bstage])
                LD(YTd[g * 4:g * 4 + 4, :, NM:NM + 64].rearrange("j p t -> p j t"), stage_s, [], bstage, R=[bstage])
                yield
                ACT(yinT, p5[:, 256:512].rearrange("p (j t) -> p j t", t=64), AF.Copy, [bp5], [byinT])
                for j in range(4):
                    TR(pB[S, j * 128:(j + 1) * 128], yinT[:, j, :], identf, [byinT, bC], [bpB])
                ACT(y1[S, :], pB[S, :], AF.Copy, [bpB], [by1])
                TT(y1[S, :], y1[S, :], p2[S, :], ALU.add, [by1, bp2], [by1])
                for kc in range(16):
                    MM(pB[S, :], hTs[:, kc, 0:64], Wt[:, kc, 1024:1536], kc == 0, kc == 15, [bhTs, bWt], [bpB])
                ACT(zs[S, :], pB[S, :], AF.Silu, [bpB], [bzs])
                rms_to(ybar[S, :], bybar, y1[S, :], by1, zs[S, :], bzs, S)
                h0 = psT_half()
                for j in range(4):
                    TR(psT[:, h0 + j * 64:h0 + (j + 1) * 64], ybar[S, j * 128:(j + 1) * 128], identb[0:64, 0:64],
                       [bybar, bidb], [bpsT])
                ACT(stage_s, psT[:, h0:h0 + 256].rearrange("p (a b) -> p a b", b=64), AF.Copy, [bpsT], [bstage])
                LD(OTd[g * 4:g * 4 + 4, :, NM:NM + 64].rearrange("j p t -> p j t"), stage_s, [], bstage, R=[bstage])
                yield

            KCUT = os.environ.get('K_CUT', '')
            if KCUT == 'A':
                b_done.update(range(100)); bs_done.update(range(8))
                interleave([gen_A()])
            elif KCUT == 'B':
                a_done.update(range(100)); as_done.update(range(8))
                interleave([gen_B()])
            else:
                interleave([gen_A(), gen_B()])

        def phase2(t0, segs, chunks):
            n = sum(sg_[3] for sg_ in segs)
            NW = 576 if n > 512 else 512
            with contextlib.ExitStack() as s2:
                mergedT, bmg = sbuf(s2, "mergedT", [128, 16, NW], BF16)
                with contextlib.ExitStack() as s2a:
                    YT, bYT = sbuf(s2a, "YT", [128, 32, NW], BF16)
                    OT, bOT = sbuf(s2a, "OT", [128, 32, NW], BF16)
                    Wsb = [sbuf(s2a, f"Wssm_j{i}", [128, 32, 128], BF16) for i in range(2)]
                    Wrb = [sbuf(s2a, f"Wret_j{i}", [128, 32, 128], BF16) for i in range(2)]
                    Wg1b = [sbuf(s2a, f"Wgs_j{i}", [128, 16, 128], BF16) for i in range(2)]
                    Wg2b = [sbuf(s2a, f"Wgr_j{i}", [128, 16, 128], BF16) for i in range(2)]
                    sg, bsg = sbuf(s2a, "sg", [128, NW])
                    m1, bm1 = sbuf(s2a, "m1", [128, NW])
                    m2, bm2 = sbuf(s2a, "m2", [128, NW])
                    LD(YT[:, :, 0:n], YTd[:, :, t0:t0 + n].rearrange("k p t -> p k t"), [bYT], bYT)
                    LD(OT[:, :, 0:n], OTd[:, :, t0:t0 + n].rearrange("k p t -> p k t"), [bOT], bOT)
                    for j in range(16):
                        (Ws, bWs), (Wr, bWr), (Wg1, bWg1), (Wg2, bWg2) = Wsb[j % 2], Wrb[j % 2], Wg1b[j % 2], Wg2b[j % 2]
                        LDC(Ws[:], wssm[:, j * 128:(j + 1) * 128].rearrange("(kb p) n -> p kb n", p=128), [bWs], bWs)
                        LDC(Wr[:], wret[:, j * 128:(j + 1) * 128].rearrange("(kb p) n -> p kb n", p=128), [bWr], bWr)
                        LDC(Wg1[:], w_in[:, OGS + j * 128:OGS + (j + 1) * 128].rearrange("(kb p) n -> p kb n", p=128),
                            [bWg1], bWg1)
                        LDC(Wg2[:], w_in[:, OGR + j * 128:OGR + (j + 1) * 128].rearrange("(kb p) n -> p kb n", p=128),
                            [bWg2], bWg2)
                        for bi_, (Wp, bWp, Xp, bXp, Wg, bWg, pa, pb, mo, bmo) in enumerate((
                                (Ws, bWs, YT, bYT, Wg1, bWg1, ps[0], ps[1], m1, bm1),
                                (Wr, bWr, OT, bOT, Wg2, bWg2, ps[2], ps[3], m2, bm2))):
                            c0 = 0
                            for si, (hT, bhT, hoff, nc_) in enumerate(segs):
                                if si == 0:
                                    pav, bpa, pbv, bpb = pa[0][:, 0:nc_], pa[1], pb[0][:, 0:nc_], pb[1]
                                else:
                                    o_ = bi_ * 128
                                    pav, bpa = ps[4][0][:, o_:o_ + nc_], ps[4][1]
                                    pbv, bpb = ps[4][0][:, o_ + 64:o_ + 64 + nc_], ps[4][1]
                                for kb in range(32):
                                    MM(pav, Wp[:, kb, :], Xp[:, kb, c0:c0 + nc_], kb == 0, kb == 31, [bWp, bXp], [bpa])
                                for kc in range(16):
                                    MM(pbv, Wg[:, kc, :], hT[:, kc, hoff:hoff + nc_], kc == 0, kc == 15, [bWg, bhT], [bpb])
                                ACT(sg[:, c0:c0 + nc_], pbv, AF.Sigmoid, [bpb], [bsg])
                                TT(mo[:, c0:c0 + nc_], sg[:, c0:c0 + nc_], pav, ALU.mult, [bsg, bpa], [bmo])
                                c0 += nc_
                        TT(mergedT[:, j, 0:n], m1[:, 0:n], m2[:, 0:n], ALU.add, [bm1, bm2], [bmg])
                P.barrier()
                with contextlib.ExitStack() as s2b:
                    Wo, bWo = sbuf(s2b, "Wo", [128, 16, D], BF16)
                    xtb = [sbuf(s2b, f"p2_xt{i}", [128, D]) for i in range(2)]
                    yob = [sbuf(s2b, f"p2_yo{i}", [128, D]) for i in range(2)]
                    wpost, bwpost = sbuf(s2b, "p2_wpost", [128, D])
                    ss4, bss4 = sbuf(s2b, "ss4", [128, 8])
                    junk, bjunk = sbuf(s2b, "junk", [128, 512])
                    LD(wpost[:], wpost_d[:, :], [bwpost], bwpost)
                    for nb in range(4):
                        LDC(Wo[:, :, nb * 512:(nb + 1) * 512],
                            wout[:, nb * 512:(nb + 1) * 512].rearrange("(kb p) n -> p kb n", p=128), [bWo], bWo)
                    for ci, (c0, rows, xsrc, xoff, ydst, yoff) in enumerate(chunks):
                        (xt, bxt), (yo, byo) = xtb[ci % 2], yob[ci % 2]
                        LD(xt[0:rows, :], xsrc[xoff:xoff + rows, :], [bxt], bxt)
                        ZERO(ss4[:, 0:4], bss4)
                        for nb in range(4):
                            pt, bpt = ps[nb]
                            for j in range(16):
                                MM(pt[0:rows, :], mergedT[:, j, c0:c0 + rows], Wo[:, j, nb * 512:(nb + 1) * 512], j == 0,
                                   j == 15, [bmg, bWo], [bpt])
                            ACT(junk[0:rows, :], pt[0:rows, :], AF.Square, [bpt], [bjunk, bss4], accum_out=ss4[0:rows, nb:nb + 1])
                        P.op("dve", lambda e, rows=rows: e.reduce_sum(out=ss4[0:rows, 4:5], in_=ss4[0:rows, 0:4], axis=AX.X),
                             [bss4], [bss4])
                        TS(ss4[0:rows, 5:6], ss4[0:rows, 4:5], 1.0 / D, EPS, ALU.mult, ALU.add, [bss4], [bss4])
                        ACT(ss4[0:rows, 5:6], ss4[0:rows, 5:6], AF.Sqrt, [bss4], [bss4])
                        P.op("dve", lambda e, rows=rows: e.reciprocal(out=ss4[0:rows, 5:6], in_=ss4[0:rows, 5:6]), [bss4], [bss4])
                        for nb in range(4):
                            pt, bpt = ps[nb]
                            STT(yo[0:rows, nb * 512:(nb + 1) * 512], pt[0:rows, :], ss4[0:rows, 5:6],
                                wpost[0:rows, nb * 512:(nb + 1) * 512], ALU.mult, ALU.mult, [bpt, bss4, bwpost], [byo])
                        TT(yo[0:rows, :], yo[0:rows, :], xt[0:rows, :], ALU.add, [byo, bxt], [byo])
                        LD(ydst[yoff:yoff + rows, :], yo[0:rows, :], [], byo, R=[byo])
                P.barrier()

        KSTOP = int(os.environ.get("K_STOP", "9"))
        with contextlib.ExitStack() as sA:
            hTp, bhTp = sbuf(sA, "hTp", [128, 16, NP_], BF16)
            with contextlib.ExitStack() as s0:
                phase0(s0, xp, NP_, hTp, bhTp)
            P.barrier()
            if KSTOP >= 2:
                with contextlib.ExitStack() as s1:
                    phase1(s1, hTp, bhTp, NP_, cosP_d, sinP_d, "pre")
                P.barrier()
        if KSTOP >= 3:
            with contextlib.ExitStack() as s0:
                phase0(s0, xm, NM, hTm, bhTm)
                phase0(s0, xs_, NS, hTs, bhTs)
            P.barrier()
        if KSTOP >= 4:
            with contextlib.ExitStack() as s1:
                phase1(s1, hTm, bhTm, NM, cosM_d, sinM_d, "main")
            for cb in range(48):
                LDNC(convp[:, cb * 128:(cb + 1) * 128].rearrange("t p -> p t"), convout[:, cb, :], [], bco, R=[bco])
            P.barrier()
        if KSTOP >= 5:
            phase2(0, [(hTm, bhTm, 0, 512)], [(c * 128, 128, xm, c * 128, ym, c * 128) for c in range(4)])
            ch2 = [(c * 128, 128, xm, 512 + c * 128, ym, 512 + c * 128) for c in range(4)]
            if WITH_SAMPLE:
                phase2(512, [(hTm, bhTm, 512, 512), (hTs, bhTs, 0, NS)], ch2 + [(512, NS, xs_, 0, ys, 0)])
            else:
                phase2(512, [(hTm, bhTm, 512, 512)], ch2)
        P.barrier()
        blk = st0.enter_context(nc.Block())
        P.emit(blk)
    return nc


def kernel(x_prompt, x_sample, cache_conv, state_ssm, state_ret, w_pre, w_in, conv_w, conv_b, dt_bias, a_log,
           d_skip, ssm_norm_w, w_proj_ssm, w_proj_ret, w_out, w_post):
    f = lambda a: np.ascontiguousarray(np.asarray(a, dtype=np.float32))
    x_prompt, x_sample = f(x_prompt), f(x_sample)
    consts, cgrp = make_consts()
    nc = build_program()
    bc = lambda v: np.ascontiguousarray(np.broadcast_to(f(v).reshape(1, -1), (128, f(v).size)))
    shared = {
        "consts": consts, "cgrp": cgrp, "w_in": f(w_in)[0], "wssm": f(w_proj_ssm)[0], "wret": f(w_proj_ret)[0], "wout": f(w_out)[0],
        "wpre_b": bc(w_pre), "wpost_b": bc(w_post), "normw_b": bc(ssm_norm_w),
        "convw": np.ascontiguousarray(f(conv_w)[0].reshape(4, 48, 128).transpose(2, 1, 0)),
        "convb": np.ascontiguousarray(f(conv_b)[0].reshape(48, 128).T),
        "dtb_b": bc(dt_bias), "alog_b": bc(a_log), "dsk_b": bc(d_skip),
    }
    cS, sS = rope_tables(16384 + (np.arange(64) // 16))
    in_maps = []
    for c in range(8):
        b, hf = c // 2, c % 2
        cM, sM = rope_tables(hf * 1024 + np.arange(1024))
        cP, sP = rope_tables(np.arange(1024))
        sl = slice(16 * c, 16 * c + 16)
        m = dict(shared)
        m.update({
            "xm": np.ascontiguousarray(x_prompt[b, hf * 1024:(hf + 1) * 1024]),
            "xp": np.ascontiguousarray(x_prompt[b, 0:1024]),
            "xs": np.ascontiguousarray(x_sample[sl].transpose(1, 0, 2).reshape(64, D)),
            "flag": np.full((128, 1), float(hf), np.float32),
            "cosM": cM, "sinM": sM, "cosP": cP, "sinP": sP, "cosS": cS, "sinS": sS,
            "cconv": np.ascontiguousarray(f(cache_conv)[0, sl].transpose(1, 0, 2).reshape(48, 6144)),
            "sssm": np.ascontiguousarray(f(state_ssm)[0, sl].reshape(16, 4096, 128)),
            "sret": np.ascontiguousarray(f(state_ret)[0, sl].reshape(16, 4096, 256)),
        })
        in_maps.append(m)
    res = run_bass_kernel_spmd(nc, in_maps, core_ids=list(range(8)))
    R = res.results
    y_prompt = np.zeros((4, 2048, D), np.float32)
    y_sample = np.zeros((128, 4, D), np.float32)
    conv_p = np.zeros((1, 4, 3, 6144), np.float32)
    ssm_p = np.zeros((1, 4, 64, 64, 128), np.float32)
    ret_p = np.zeros((1, 4, 8, 512, 256), np.float32)
    conv_s = np.zeros((1, 128, 3, 6144), np.float32)
    ssm_s = np.zeros((1, 128, 64, 64, 128), np.float32)
    ret_s = np.zeros((1, 128, 8, 512, 256), np.float32)
    for c in range(8):
        b, hf = c // 2, c % 2
        r = R[c]
        y_prompt[b, hf * 1024:(hf + 1) * 1024] = r["ym"]
        y_sample[16 * c:16 * c + 16] = r["ys"].reshape(4, 16, D).transpose(1, 0, 2)
        if hf == 1:
            conv_p[0, b] = r["convp"]
            ssm_p[0, b] = r["ssmp"].reshape(64, 64, 128)
            ret_p[0, b] = r["retp"].reshape(8, 512, 256)
        conv_s[0, 16 * c:16 * c + 16] = r["convs"]
        ssm_s[0, 16 * c:16 * c + 16] = r["ssms"].reshape(16, 64, 64, 128)
        ret_s[0, 16 * c:16 * c + 16] = r["rets"].reshape(16, 8, 512, 256)
    return (y_prompt, y_sample, conv_p, ssm_p, ret_p, conv_s, ssm_s, ret_s)
```

```python
import os
import contextlib
import math
import numpy as np
import concourse.bass as bass
import concourse.mybir as mybir
from concourse.bass_utils import run_bass_kernel_spmd

F32 = mybir.dt.float32
BF16 = mybir.dt.bfloat16
ALU = mybir.AluOpType
AF = mybir.ActivationFunctionType
AX = mybir.AxisListType

D = 2048
NM = 1024
NP_ = 1024
NS = 64
NT = NM + NS
IN_DIM = 26688
OZ, OX, OB, OC, ODT, OQ, OK_, OV, OG, OGS, OGR = 0, 4096, 8192, 9216, 10240, 10304, 12352, 14400, 18496, 22592, 24640
EPS = 1e-6
SC = 256
import os
WITH_SAMPLE = os.environ.get('K_NOSAMPLE') != '1'


class Buf:
    __slots__ = ("name", "w", "r", "dsem", "dcnt", "psum")

    def __init__(self, name):
        self.name = name
        self.psum = name.startswith("ps")
        self.w = []
        self.r = []
        self.dsem = None
        self.dcnt = 0


class Plan:
    ENGS = ("pe", "act", "dve", "pool", "sp")
    SEM_LIMIT = 30000
    SAME_ENGINE_SYNC = {"pe": False, "act": True, "dve": True, "pool": True, "sp": True}

    def __init__(self, nc, stack):
        self.nc = nc
        self.stack = stack
        self.streams = {e: [] for e in self.ENGS}
        self.esem = {}
        self.ecnt = {e: 0 for e in self.ENGS}
        self.seen = {e: {} for e in self.ENGS}
        self.nsem = 0
        for e in self.ENGS:
            self.esem[e] = self.new_sem("e_" + e)
        self.bufs = []
        self.free_ctrs = []
        self.pending_ctrs = []

    def release(self, b):
        if b.dsem is not None:
            if b.dsem[2] == "sp":
                self.pending_ctrs.append(b.dsem)
            b.dsem = None

    def new_sem(self, name):
        self.nsem += 1
        return self.stack.enter_context(self.nc.semaphore(f"{name}_{self.nsem}"))

    def buf(self, name):
        b = Buf(name)
        self.bufs.append(b)
        return b

    def _deps(self, eng, reads, writes, extra=()):
        need = {}

        def add(lst):
            for (s, v) in lst:
                k = id(s)
                if k not in need or need[k][1] < v:
                    need[k] = (s, v)
        for b in reads:
            add(b.w)
            if b.psum:
                add(b.r)
        for b in writes:
            add(b.w)
            add(b.r)
        add(extra)
        out = []
        seen = self.seen[eng]
        own = id(self.esem[eng])
        for k, (s, v) in need.items():
            if k == own and not self.SAME_ENGINE_SYNC[eng]:
                continue
            if seen.get(k, -1) >= v:
                continue
            seen[k] = v
            out.append((s, v))
        return out

    def op(self, eng, fn, reads=(), writes=()):
        waits = self._deps(eng, reads, writes)
        if self.ecnt[eng] >= self.SEM_LIMIT:
            self.esem[eng] = self.new_sem("e_" + eng)
            self.ecnt[eng] = 0
        self.ecnt[eng] += 1
        tok = (self.esem[eng], self.ecnt[eng])
        self.streams[eng].append((waits, fn, (self.esem[eng], 1)))
        for b in reads:
            b.r.append(tok)
        for b in writes:
            b.w = [tok]
            b.r = []
        return tok

    def dma(self, eng, fn, reads=(), writes=(), owner=None):
        waits = self._deps(eng, reads, writes)
        ow = owner
        if ow.dsem is None:
            if eng == "sp" and self.free_ctrs:
                ow.dsem = self.free_ctrs.pop()
            else:
                ow.dsem = [self.new_sem("d"), 0, eng]
        ow.dsem[1] += 16
        tok = (ow.dsem[0], ow.dsem[1])
        self.streams[eng].append((waits, fn, (ow.dsem[0], 16)))
        for b in reads:
            b.r.append(tok)
        for b in writes:
            b.w = [tok]
            b.r = []
        return tok

    def barrier(self):
        toks = []
        for b in self.bufs:
            toks += b.w
            toks += b.r
        for e in self.ENGS:
            if self.ecnt[e] > 0:
                toks.append((self.esem[e], self.ecnt[e]))
        for e in self.ENGS:
            waits = self._deps(e, (), (), extra=toks)
            self.streams[e].append((waits, None, None))
        for b in self.bufs:
            b.w = []
            b.r = []
        self.free_ctrs += self.pending_ctrs
        self.pending_ctrs = []

    def emit(self, block):
        def run(stream):
            def body(e):
                for waits, fn, inc in stream:
                    for (s, v) in waits:
                        e.wait_ge(s, v)
                    if fn is not None:
                        fn(e).then_inc(inc[0], inc[1])
            return body
        block.tensor(run(self.streams["pe"]))
        block.scalar(run(self.streams["act"]))
        block.vector(run(self.streams["dve"]))
        block.gpsimd(run(self.streams["pool"]))
        block.sync(run(self.streams["sp"]))


def _gammas():
    return (1.0 - np.exp2(-5.0 - np.arange(8, dtype=np.float64)))


CONST_LAYOUT = {}


def make_consts():
    cols = []
    off = [0]

    def put(name, arr):
        a = np.zeros((128, arr.shape[1]), np.float32)
        a[:arr.shape[0]] = arr
        CONST_LAYOUT[name] = (off[0], arr.shape[1])
        off[0] += arr.shape[1]
        cols.append(a)
    i = np.arange(128)
    put("ident", np.eye(128, dtype=np.float32))
    put("causal", (i[:, None] <= i[None, :]).astype(np.float32))
    put("strict", (i[:, None] > i[None, :]).astype(np.float32))
    put("ones", np.ones((128, 128), np.float32))
    g = _gammas()
    dk = 256 ** -0.5
    retD = np.zeros((128, 8 * 128), np.float64)
    gpow = np.zeros((128, 8 * 128), np.float64)
    kds = np.zeros((128, 8), np.float64)
    for h in range(8):
        dlt = (i[None, :] - i[:, None])
        m = np.where(dlt >= 0, g[h] ** np.maximum(dlt, 0), 0.0) * dk
        retD[:, h * 128:(h + 1) * 128] = m
        gpow[:, h * 128:(h + 1) * 128] = (g[h] ** (i + 1))[None, :]
        kds[:, h] = g[h] ** (127 - i) * dk
    put("kds", kds.astype(np.float32))
    j = np.arange(64)
    tt, sq = j // 16, j % 16
    same = (sq[:, None] == sq[None, :])
    put("causal_s", (same & (tt[:, None] <= tt[None, :])).astype(np.float32))
    put("strict_s", (same & (tt[:, None] > tt[None, :])).astype(np.float32))
    put("incl_s", (same & (tt[:, None] <= tt[None, :])).astype(np.float32))
    put("seqmask", (sq[:, None] == np.arange(16)[None, :]).astype(np.float32))
    put("same_s", same.astype(np.float32))
    retDs = np.zeros((64, 8 * 64), np.float64)
    gpows = np.zeros((128, 8 * 64), np.float64)
    kdss = np.zeros((64, 8), np.float64)
    for h in range(8):
        dlt = tt[None, :] - tt[:, None]
        m = np.where(same & (dlt >= 0), g[h] ** np.maximum(dlt, 0), 0.0) * dk
        retDs[:, h * 64:(h + 1) * 64] = m
        gpows[:, h * 64:(h + 1) * 64] = (g[h] ** (tt + 1))[None, :]
        kdss[:, h] = g[h] ** (3 - tt) * dk
    put("kds_s", kdss.astype(np.float32))
    cgrp = np.zeros((8, 128, 384), np.float32)
    for h in range(8):
        cgrp[h, :, 0:128] = retD[:, h * 128:(h + 1) * 128]
        cgrp[h, :, 128:256] = gpow[:, h * 128:(h + 1) * 128]
        cgrp[h, 0:64, 256:320] = retDs[:, h * 64:(h + 1) * 64]
        cgrp[h, :, 320:384] = gpows[:, h * 64:(h + 1) * 64]
    return np.ascontiguousarray(np.concatenate(cols, axis=1)), cgrp


def rope_tables(pos):
    half = 128
    inv = (1.0 / (10000.0 ** (np.arange(half, dtype=np.float32) / np.float32(half)))).astype(np.float32)
    ang = pos.astype(np.float32)[None, :] * inv[:, None]
    return np.cos(ang).astype(np.float32), np.sin(ang).astype(np.float32)


def build_program():
    nc = bass.Bass("TRN2", target_bir_lowering=False)
    NCOL = sum(v[1] for v in CONST_LAYOUT.values())

    def din(name, shape, dt=F32):
        return nc.dram_tensor(name, list(shape), dt, kind="ExternalInput").ap()

    def dout(name, shape, dt=F32):
        return nc.dram_tensor(name, list(shape), dt, kind="ExternalOutput").ap()

    xm = din("xm", [NM, D]); xp = din("xp", [NP_, D]); xs_ = din("xs", [NS, D])
    flag_d = din("flag", [128, 1]); consts_d = din("consts", [128, NCOL]); cgrp_d = din("cgrp", [8, 128, 384])
    cosM_d = din("cosM", [128, NM]); sinM_d = din("sinM", [128, NM])
    cosP_d = din("cosP", [128, NP_]); sinP_d = din("sinP", [128, NP_])
    cosS_d = din("cosS", [128, NS]); sinS_d = din("sinS", [128, NS])
    w_in = din("w_in", [D, IN_DIM]); wssm = din("wssm", [4096, D]); wret = din("wret", [4096, D])
    wout = din("wout", [D, D])
    wpre_d = din("wpre_b", [128, D]); wpost_d = din("wpost_b", [128, D]); normw_d = din("normw_b", [128, 4096])
    convw_d = din("convw", [128, 48, 4]); convb_d = din("convb", [128, 48])
    dtb_d = din("dtb_b", [128, 64]); alog_d = din("alog_b", [128, 64]); dsk_d = din("dsk_b", [128, 64])
    cconv_d = din("cconv", [48, 6144]); sssm_d = din("sssm", [16, 4096, 128]); sret_d = din("sret", [16, 4096, 256])

    ym = dout("ym", [NM, D]); ys = dout("ys", [NS, D])
    convp = dout("convp", [3, 6144]); ssmp = dout("ssmp", [4096, 128]); retp = dout("retp", [4096, 256])
    convs = dout("convs", [16, 3, 6144]); ssms = dout("ssms", [16, 4096, 128]); rets = dout("rets", [16, 4096, 256])

    YTd = nc.dram_tensor("YTd", [32, 128, NT], BF16).ap()
    OTd = nc.dram_tensor("OTd", [32, 128, NT], BF16).ap()
    STs_d = nc.dram_tensor("STs_d", [8, 128, 512], F32).ap()
    STr_d = nc.dram_tensor("STr_d", [8, 128, 1024], F32).ap()

    with contextlib.ExitStack() as st0:
        P = Plan(nc, st0)

        uniq = [0]

        def sbuf(stack, name, shape, dt=F32):
            uniq[0] += 1
            nm = f"{name}_{uniq[0]}"
            t = stack.enter_context(nc.sbuf_tensor(nm, list(shape), dt))
            b = P.buf(nm)
            if stack is not st0:
                stack.callback(P.release, b)
            return t, b

        def ZERO(ap, b):
            P.op("dve", lambda e: e.memset(ap, 0.0), [], [b])

        ps = []
        for i in range(6):
            t = st0.enter_context(nc.psum_tensor(f"ps{i}", [128, 512], F32))
            ps.append((t, P.buf(f"ps{i}")))
        psT, bpsT = st0.enter_context(nc.psum_tensor("psT", [128, 512 if os.environ.get("K_NOHALF") == "1" else 1024], BF16)), P.buf("psT")
        psM, bpsM = st0.enter_context(nc.psum_tensor("psM", [128, 256 if os.environ.get("K_NOHALF") == "1" else 512], F32)), P.buf("psM")

        def MM(out, lhsT, rhs, start, stop, R, W):
            P.op("pe", lambda e: e.matmul(out, lhsT=lhsT, rhs=rhs, start=start, stop=stop), R, W)

        def TR(out, in_, ident, R, W):
            P.op("pe", lambda e: e.transpose(out=out, in_=in_, identity=ident), R, W)

        def ACT(out, in_, func, R, W, **kw):
            P.op("act", lambda e: e.activation(out=out, in_=in_, func=func, **kw), R, W)

        def TT(out, a, b, op, R, W):
            P.op("dve", lambda e: e.tensor_tensor(out=out, in0=a, in1=b, op=op), R, W)

        def TS(out, a, s1, s2, op0, op1, R, W):
            if s2 is None:
                P.op("dve", lambda e: e.tensor_scalar(out=out, in0=a, scalar1=s1, scalar2=None, op0=op0), R, W)
            else:
                P.op("dve", lambda e: e.tensor_scalar(out=out, in0=a, scalar1=s1, scalar2=s2, op0=op0, op1=op1), R, W)

        def STT(out, a, s, b, op0, op1, R, W):
            P.op("dve", lambda e: e.scalar_tensor_tensor(out=out, in0=a, scalar=s, in1=b, op0=op0, op1=op1), R, W)

        def CP(out, in_, R, W):
            P.op("dve", lambda e: e.tensor_copy(out=out, in_=in_), R, W)

        def LD(out, in_, W, owner, R=(), eng="sp"):
            P.dma(eng, lambda e: e.dma_start(out=out, in_=in_), R, W, owner)

        def LDC(out, in_, W, owner, R=()):
            P.dma("pool", lambda e: e.dma_start(out=out, in_=in_), R, W, owner)

        def LDNC(out, in_, W, owner, R=()):
            P.dma("sp", lambda e: e.dma_start(out=out, in_=in_, allow_slow_non_contiguous=True), R, W, owner)

        def rstd_from_ss(rs, brs, ss, n):
            TS(rs, ss, 1.0 / n, EPS, ALU.mult, ALU.add, [brs] if ss.tensor is rs.tensor else [brs], [brs])
            ACT(rs, rs, AF.Sqrt, [brs], [brs])
            P.op("dve", lambda e: e.reciprocal(out=rs, in_=rs), [brs], [brs])

        C, bC = sbuf(st0, "consts", [128, NCOL])
        LD(C[:], consts_d[:, :], [bC], bC)

        def cst(name, rows=128):
            o, n = CONST_LAYOUT[name]
            return C[0:rows, o:o + n]
        identb, bidb = sbuf(st0, "identb", [128, 128], BF16)
        CP(identb[:], cst("ident"), [bC], [bidb])
        flag, bflag = sbuf(st0, "flag", [128, 1])
        LD(flag[:], flag_d[:, :], [bflag], bflag)
        hTm, bhTm = sbuf(st0, "hTm", [128, 16, NM], BF16)
        hTs, bhTs = sbuf(st0, "hTs", [128, 16, NS], BF16)
        convtail, bct = sbuf(st0, "convtail", [128, 48, 3])
        convout, bco = sbuf(st0, "convout", [128, 48, 3])
        small, bsmall = sbuf(st0, "small", [128, 3, 64])
        LD(small[:, 0, :], dtb_d[:, :], [bsmall], bsmall)
        LD(small[:, 1, :], alog_d[:, :], [bsmall], bsmall)
        LD(small[:, 2, :], dsk_d[:, :], [bsmall], bsmall)
        ACT(small[:, 1, :], small[:, 1, :], AF.Exp, [bsmall], [bsmall])
        TS(small[:, 1, :], small[:, 1, :], -1.0, None, ALU.mult, None, [bsmall], [bsmall])
        convw, bcw = sbuf(st0, "convw", [128, 48, 4])
        convb, bcb = sbuf(st0, "convb", [128, 48])
        LD(convw[:], convw_d[:, :, :], [bcw], bcw)
        LD(convb[:], convb_d[:, :], [bcb], bcb)

        def phase0(stack, xsrc, ntok, hT, bhT):
            xtb = [sbuf(stack, f"p0_xt{i}", [128, D]) for i in range(2)]
            hbb = [sbuf(stack, f"p0_hb{i}", [128, D], BF16) for i in range(2)]
            wpre, bwpre = sbuf(stack, "p0_wpre", [128, D])
            st_, bst = sbuf(stack, "p0_st", [128, 2])
            LD(wpre[:], wpre_d[:, :], [bwpre], bwpre)
            for t0 in range(0, ntok, 128):
                rows = min(128, ntok - t0)
                (xt, bxt), (hb, bhb) = xtb[(t0 // 128) % 2], hbb[(t0 // 128) % 2]
                LD(xt[0:rows, :], xsrc[t0:t0 + rows, :], [bxt], bxt)
                ZERO(st_[:, 0:1], bst)
                ACT(hb[0:rows, :], xt[0:rows, :], AF.Square, [bxt], [bhb, bst], accum_out=st_[0:rows, 0:1])
                TS(st_[0:rows, 1:2], st_[0:rows, 0:1], 1.0 / D, EPS, ALU.mult, ALU.add, [bst], [bst])
                ACT(st_[0:rows, 1:2], st_[0:rows, 1:2], AF.Sqrt, [bst], [bst])
                P.op("dve", lambda e, rows=rows: e.reciprocal(out=st_[0:rows, 1:2], in_=st_[0:rows, 1:2]), [bst], [bst])
                STT(hb[0:rows, :], xt[0:rows, :], st_[0:rows, 1:2], wpre[0:rows, :], ALU.mult, ALU.mult,
                    [bxt, bst, bwpre, bhb], [bhb])
                for k0 in range(0, 16, 4):
                    for kk in range(4):
                        kc = k0 + kk
                        TR(psT[:, kk * 128:kk * 128 + rows], hb[0:rows, kc * 128:(kc + 1) * 128],
                           identb[0:rows, 0:rows], [bhb, bidb], [bpsT])
                    src = psT[:, 0:512].rearrange("p (a b) -> p a b", b=128)[:, :, 0:rows]
                    ACT(hT[:, k0:k0 + 4, t0:t0 + rows], src, AF.Copy, [bpsT], [bhT])

        BLK = [False]

        def interleave(gens):
            gens = list(gens)
            seq = os.environ.get("K_SEQ") == "1"
            while gens:
                for gg in list(gens):
                    try:
                        while True:
                            BLK[0] = False
                            next(gg)
                            if not seq or BLK[0]:
                                break
                    except StopIteration:
                        gens.remove(gg)

        def phase1(stack, hT, bhT, ntok, cos_d, sin_d, mode):
            main = (mode == "main")
            SCL = SC if main else 512
            samp = main and WITH_SAMPLE
            nsc = min(ntok // SCL, int(os.environ.get('K_NSC', '99')))
            NG = int(os.environ.get('K_G', '8'))
            Wf, bWf = sbuf(stack, "Wf", [128, 16, 1280 if main else 1024], BF16)
            Wt, bWt = sbuf(stack, "Wt", [128, 16, 1536 if main else 512], BF16)
            if main:
                normw, bnw = sbuf(stack, "normw", [128, 512])
                cg, bcg = sbuf(stack, "cg", [128, 384])
                zs, bzs = sbuf(stack, "zs", [128, 512])
                qkm, bqkm = sbuf(stack, "qkm", [128, 128])
                Ld, bLd = sbuf(stack, "Ld", [128, 4, 128])
                segT, bsegT = sbuf(stack, "segT", [128, 512])
                MT, bMT = sbuf(stack, "MT", [128, 8, 128], BF16)
                xdt, bxdt = sbuf(stack, "xdt", [128, 512], BF16)
                xsk, bxsk = sbuf(stack, "xsk", [128, 512])
                ybar, bybar = sbuf(stack, "ybar", [128, 512], BF16)
                stage, bstage = sbuf(stack, "stage", [128, 4, 128], BF16)
                MTr, bMTr = sbuf(stack, "MTr", [128, 128], BF16)
                qs, bqs = sbuf(stack, "qs", [128, 2, 128], BF16)
                R0buf = [sbuf(stack, f"R0n{i}", [128, 4, 256]) for i in range(2)]
            if samp:
                csS, bcsS = sbuf(stack, "csS", [128, 2, 64])
                LD(csS[:, 0, :], cosS_d[:, :], [bcsS], bcsS)
                LD(csS[:, 1, :], sinS_d[:, :], [bcsS], bcsS)
                S0buf = [sbuf(stack, f"S0n{i}", [128, 4, 128]) for i in range(2)]
                bS0o = [P.buf(f"S0o{i}") for i in range(2)]
                bR0o = [P.buf(f"R0o{i}") for i in range(2)]
                S0b, bS0b = sbuf(stack, "S0b", [128, 4, 128], BF16)
                R0b, bR0b = sbuf(stack, "R0b", [128, 4, 256], BF16)
                cdcol, bcdcol = sbuf(stack, "cdcol", [128, 4, 16])
                Bm, bBm = sbuf(stack, "Bm", [128, 128], BF16)
                kdm, bkdm = sbuf(stack, "kdm", [128, 256], BF16)
                raws_t, braws = sbuf(stack, "raws", [128, 6, 112])
                raws = raws_t[:, :, :].rearrange("p b (t s) -> p b t s", s=16)
                cTs, bcTs = sbuf(stack, "cTs", [128, 6, 64], BF16)
                qkTs, bqkTs = sbuf(stack, "qkTs", [128, 4, 64], BF16)
                yinT_t, byinT = sbuf(stack, "yinT", [128, 4, 64])
                yinT = yinT_t[:, :, :]
            Wdt, bWdt = sbuf(stack, "Wdt", [128, 16, 64], BF16)
            LDC(Wdt[:], w_in[:, ODT:ODT + 64].rearrange("(kc p) n -> p kc n", p=128), [bWdt], bWdt)
            raw, braw = sbuf(stack, "raw", [128, 6, SCL + 3])
            acc, bacc = sbuf(stack, "acc", [128, 512])
            cc, bcc = acc[:, 256:384], bacc
            cTb = [sbuf(stack, f"cT{i}", [128, 6, SCL], BF16) for i in range(2)]
            qkTb = [sbuf(stack, f"qkT{i}", [128, 4, SCL], BF16) for i in range(2)]
            cosT, bcos = sbuf(stack, "cosT", [128, SCL])
            sinT, bsin = sbuf(stack, "sinT", [128, SCL])
            r1, br1 = sbuf(stack, "r1", [128, 512])
            r2, br2 = sbuf(stack, "r2", [128, 512])
            y1, by1 = sbuf(stack, "y1", [128, 512])
            y2, by2 = y1, by1
            sm, bsm = sbuf(stack, "sm", [128, 12, 8])
            xdd, bxdd = sbuf(stack, "xdd", [128, 512], BF16)
            Btok, bBtok = sbuf(stack, "Btok", [128, 128], BF16)
            ST, bST = sbuf(stack, "ST", [128, 512])
            STb, bSTb = sbuf(stack, "STb", [128, 512], BF16)
            tmpS, btmpS = y1, by1
            RST, bRST = sbuf(stack, "RST", [128, 2, 512])
            RSTb, bRSTb = sbuf(stack, "RSTb", [128, 2, 512], BF16)
            vb, bvb = sbuf(stack, "vb", [128, 512], BF16)
            kd, bkd = sbuf(stack, "kd", [128, 256], BF16)
            rs, brs = sbuf(stack, "rs", [128, 2])
            ai = [0]
            a_done = set()
            b_done = set()
            as_done = set()
            bs_done = set()
            pB, bpB = ps[4]
            p2, bp2 = ps[2]
            p3, bp3 = ps[3]
            p5, bp5 = ps[5]
            identf = cst("ident")
            TH = [0]

            def psT_half():
                if os.environ.get('K_NOHALF') != '1':
                    TH[0] ^= 1
                return TH[0] * 512

            def abank():
                ai[0] ^= 1
                return ps[ai[0]]

            def wld(dst, bdst, lo, c0, n):
                LDC(dst[:, :, lo:lo + n], w_in[:, c0:c0 + n].rearrange("(kc p) n -> p kc n", p=128), [bdst], bdst)

            def rms_to(out_bf, bout, src, bsrc, mul, bmul, S=slice(0, 128)):
                ZERO(rs[:, 0:1], brs)
                ACT(out_bf, src, AF.Square, [bsrc], [bout, brs], accum_out=rs[S, 0:1])
                TS(rs[S, 1:2], rs[S, 0:1], 1.0 / 512, EPS, ALU.mult, ALU.add, [brs], [brs])
                ACT(rs[S, 1:2], rs[S, 1:2], AF.Sqrt, [brs], [brs])
                P.op("dve", lambda e: e.reciprocal(out=rs[S, 1:2], in_=rs[S, 1:2]), [brs], [brs])
                STT(out_bf, src, rs[S, 1:2], mul, ALU.mult, ALU.mult, [bsrc, brs, bmul], [bout])

            def gen_A():
                for g in range(NG):
                    cblk = [g * 4 + 0, g * 4 + 1, g * 4 + 2, g * 4 + 3, 32 + g, 40 + g]
                    wld(Wf, bWf, 0, OX + g * 512, 512)
                    wld(Wf, bWf, 512, OB + g * 128, 128)
                    wld(Wf, bWf, 640, OC + g * 128, 128)
                    wld(Wf, bWf, 768, OK_ + g * 256, 256)
                    if main:
                        wld(Wf, bWf, 1024, OQ + g * 256, 256)
                        CP(raw[:, :, 0:3], convtail[:, g * 6:(g + 1) * 6, :], [bct], [braw])
                    else:
                        P.op("dve", lambda e: e.memset(raw[:, :, 0:3], 0.0), [], [braw])
                    for sc in range(nsc):
                        idx = g * nsc + sc
                        while idx >= 2 and (idx - 2) not in b_done:
                            BLK[0] = True
                            yield
                        cT, bcT = cTb[idx % 2]
                        qkT, bqkT = qkTb[idx % 2]
                        T0 = sc * SCL
                        LD(cosT[:], cos_d[:, T0:T0 + SCL], [bcos], bcos)
                        LD(sinT[:], sin_d[:, T0:T0 + SCL], [bsin], bsin)
                        for bi in range(6):
                            pt, bpt = abank()
                            for kc in range(16):
                                MM(pt[:, 0:SCL], Wf[:, kc, bi * 128:(bi + 1) * 128], hT[:, kc, T0:T0 + SCL], kc == 0, kc == 15,
                                   [bWf, bhT], [bpt])
                            ACT(raw[:, bi, 3:SCL + 3], pt[:, 0:SCL], AF.Copy, [bpt], [braw])
                            yield
                        for bi in range(6):
                            cb = cblk[bi]
                            TS(acc[:, 0:SCL], raw[:, bi, 3:SCL + 3], convw[:, cb, 3:4], convb[:, cb:cb + 1], ALU.mult, ALU.add,
                               [braw, bcw, bcb], [bacc])
                            for tap in (2, 1, 0):
                                STT(acc[:, 0:SCL], raw[:, bi, tap:tap + SCL], convw[:, cb, tap:tap + 1], acc[:, 0:SCL], ALU.mult,
                                    ALU.add, [braw, bcw, bacc], [bacc])
                            ACT(cT[:, bi, :], acc[:, 0:SCL], AF.Silu, [bacc], [bcT])
                            yield
                        CP(raw[:, :, 0:3], raw[:, :, SCL:SCL + 3], [braw], [braw])
                        for which in ((0, 1) if main else (1,)):
                            lo0 = 1024 if which == 0 else 768
                            p0, bp0 = ps[0]
                            p1, bp1 = ps[1]
                            for hf, (pt, bpt) in enumerate(((p0, bp0), (p1, bp1))):
                                for kc in range(16):
                                    MM(pt[:, 0:SCL], Wf[:, kc, lo0 + hf * 128:lo0 + (hf + 1) * 128], hT[:, kc, T0:T0 + SCL],
                                       kc == 0, kc == 15, [bWf, bhT], [bpt])
                                yield
                            TT(r1[:, 0:SCL], p0[:, 0:SCL], cosT[:], ALU.mult, [bp0, bcos], [br1])
                            TT(r2[:, 0:SCL], p1[:, 0:SCL], sinT[:], ALU.mult, [bp1, bsin], [br2])
                            TT(qkT[:, which * 2, :], r1[:, 0:SCL], r2[:, 0:SCL], ALU.subtract, [br1, br2], [bqkT])
                            TT(r1[:, 0:SCL], p1[:, 0:SCL], cosT[:], ALU.mult, [bp1, bcos], [br1])
                            TT(r2[:, 0:SCL], p0[:, 0:SCL], sinT[:], ALU.mult, [bp0, bsin], [br2])
                            TT(qkT[:, which * 2 + 1, :], r1[:, 0:SCL], r2[:, 0:SCL], ALU.add, [br1, br2], [bqkT])
                            yield
                        a_done.add(idx)
                        yield
                    if main:
                        CP(convout[:, g * 4:g * 4 + 4, :], raw[:, 0:4, 0:3], [braw], [bco])
                        CP(convout[:, 32 + g, :], raw[:, 4, 0:3], [braw], [bco])
                        CP(convout[:, 40 + g, :], raw[:, 5, 0:3], [braw], [bco])
                    else:
                        TS(convtail[:, g * 6:(g + 1) * 6, :], raw[:, :, 0:3], flag[:, 0:1], None, ALU.mult, None,
                           [braw, bflag], [bct])
                    if samp:
                        while g >= 1 and (g - 1) not in bs_done:
                            BLK[0] = True
                            yield
                        for (lo, n, c0) in ((0, 512, g * 512), (512, 128, 4096 + g * 128), (640, 128, 5120 + g * 128)):
                            pt, bpt = abank()
                            for kc in range(16):
                                MM(pt[0:64, 0:n], hTs[:, kc, 0:64], Wf[:, kc, lo:lo + n], kc == 0, kc == 15, [bhTs, bWf], [bpt])
                            ACT(r1[0:64, 0:n], pt[0:64, 0:n], AF.Copy, [bpt], [br1])
                            for t in (1, 2, 3):
                                LD(convs[:, t - 1, c0:c0 + n], r1[16 * t:16 * t + 16, 0:n], [], br1, R=[br1])
                            yield
                        for bi in range(6):
                            cb = cblk[bi]
                            LD(cc[0:48, :], cconv_d[:, cb * 128:(cb + 1) * 128], [bcc], bcc)
                            pt, bpt = abank()
                            TR(pt[:, 0:48], cc[0:48, :], identf[0:48, 0:48], [bcc, bC], [bpt])
                            ACT(raws[:, bi, 0:3, :], pt[:, 0:48].rearrange("p (t s) -> p t s", s=16), AF.Copy, [bpt], [braws])
                            pt, bpt = abank()
                            for kc in range(16):
                                MM(pt[:, 0:64], Wf[:, kc, bi * 128:(bi + 1) * 128], hTs[:, kc, 0:64], kc == 0, kc == 15,
                                   [bWf, bhTs], [bpt])
                            ACT(raws[:, bi, 3:7, :], pt[:, 0:64].rearrange("p (t s) -> p t s", s=16), AF.Copy, [bpt], [braws])
                            yield
                        for bi in range(6):
                            cb = cblk[bi]
                            rf = raws_t[:, bi, :]
                            TS(acc[:, 0:64], rf[:, 48:112], convw[:, cb, 3:4], convb[:, cb:cb + 1], ALU.mult, ALU.add,
                               [braws, bcw, bcb], [bacc])
                            for tap in (2, 1, 0):
                                STT(acc[:, 0:64], rf[:, tap * 16:tap * 16 + 64], convw[:, cb, tap:tap + 1], acc[:, 0:64],
                                    ALU.mult, ALU.add, [braws, bcw, bacc], [bacc])
                            ACT(cTs[:, bi, :], acc[:, 0:64], AF.Silu, [bacc], [bcTs])
                        yield
                        for which in (0, 1):
                            lo0 = 1024 if which == 0 else 768
                            p0, bp0 = ps[0]
                            p1, bp1 = ps[1]
                            for hf, (pt, bpt) in enumerate(((p0, bp0), (p1, bp1))):
                                for kc in range(16):
                                    MM(pt[:, 0:64], Wf[:, kc, lo0 + hf * 128:lo0 + (hf + 1) * 128], hTs[:, kc, 0:64], kc == 0,
                                       kc == 15, [bWf, bhTs], [bpt])
                            cS_, sS_ = csS[:, 0, :], csS[:, 1, :]
                            TT(r1[:, 0:64], p0[:, 0:64], cS_, ALU.mult, [bp0, bcsS], [br1])
                            TT(r2[:, 0:64], p1[:, 0:64], sS_, ALU.mult, [bp1, bcsS], [br2])
                            TT(qkTs[:, which * 2, :], r1[:, 0:64], r2[:, 0:64], ALU.subtract, [br1, br2], [bqkTs])
                            TT(r1[:, 0:64], p1[:, 0:64], cS_, ALU.mult, [bp1, bcsS], [br1])
                            TT(r2[:, 0:64], p0[:, 0:64], sS_, ALU.mult, [bp0, bcsS], [br2])
                            TT(qkTs[:, which * 2 + 1, :], r1[:, 0:64], r2[:, 0:64], ALU.add, [br1, br2], [bqkTs])
                            yield
                        as_done.add(g)
                        yield

            def decay_scalars(g, hsrc, bhsrc, tk, S, incl_ap, tot_ap):
                n = S.stop
                for kc in range(16):
                    MM(psM[S, 0:8], hsrc[:, kc, tk:tk + n], Wdt[:, kc, g * 8:(g + 1) * 8], kc == 0, kc == 15,
                       [bhsrc, bWdt], [bpsM])
                TT(sm[S, 0, :], psM[S, 0:8], small[S, 0, g * 8:(g + 1) * 8], ALU.add, [bpsM, bsmall], [bsm])
                ACT(sm[S, 0, :], sm[S, 0, :], AF.Exp, [bsm], [bsm])
                ACT(sm[S, 1, :], sm[S, 0, :], AF.Ln, [bsm], [bsm], bias=1.0)
                TT(sm[S, 2, :], sm[S, 1, :], small[S, 1, g * 8:(g + 1) * 8], ALU.mult, [bsm, bsmall], [bsm])
                MM(psM[S, 8:16], incl_ap, sm[S, 2, :], True, True, [bC, bsm], [bpsM])
                MM(psM[S, 16:24], tot_ap, sm[S, 2, :], True, True, [bC, bsm], [bpsM])
                ACT(sm[S, 3, :], psM[S, 8:16], AF.Copy, [bpsM], [bsm])
                TT(sm[S, 4, :], psM[S, 16:24], sm[S, 3, :], ALU.subtract, [bpsM, bsm], [bsm])
                ACT(sm[S, 5, :], sm[S, 4, :], AF.Exp, [bsm], [bsm])
                ACT(sm[S, 6, :], sm[S, 3, :], AF.Exp, [bsm], [bsm])
                ACT(sm[S, 7, :], psM[S, 16:24], AF.Exp, [bpsM], [bsm])
                TT(sm[S, 8, :], sm[S, 1, :], sm[S, 5, :], ALU.mult, [bsm], [bsm])

            def gen_B():
                for g in range(NG):
                    wld(Wt, bWt, 0, OV + g * 512, 512)
                    if main:
                        wld(Wt, bWt, 512, OZ + g * 512, 512)
                        wld(Wt, bWt, 1024, OG + g * 512, 512)
                        LD(normw[:], normw_d[:, g * 512:(g + 1) * 512], [bnw], bnw)
                        LD(cg[:], cgrp_d[g], [bcg], bcg)
                        LD(ST[:], STs_d[g], [bST], bST)
                        LD(RST[:], STr_d[g].rearrange("p (a b) -> p a b", b=512), [bRST], bRST)
                    else:
                        P.op("dve", lambda e: e.memset(ST[:], 0.0), [], [bST])
                        P.op("dve", lambda e: e.memset(RST[:], 0.0), [], [bRST])
                    ACT(STb[:], ST[:], AF.Copy, [bST], [bSTb])
                    ACT(RSTb[:], RST[:], AF.Copy, [bRST], [bRSTb])
                    gQ = float(_gammas()[g] ** 128)
                    oK = CONST_LAYOUT["kds"][0]
                    for sc in range(nsc):
                        idx = g * nsc + sc
                        while idx not in a_done:
                            BLK[0] = True
                            yield
                        cT, bcT = cTb[idx % 2]
                        qkT, bqkT = qkTb[idx % 2]
                        T0 = sc * SCL
                        for c in range(SCL // 128):
                            lc = c * 128
                            tk = T0 + lc
                            BC = int(os.environ.get('K_BCUT', '9'))
                            if BC >= 1:
                                decay_scalars(g, hT, bhT, tk, slice(0, 128), cst("causal"), cst("ones"))
                            yield
                            if BC < 2:
                                continue
                            for kc in range(16):
                                MM(pB[:, :], hT[:, kc, tk:tk + 128], Wt[:, kc, 0:512], kc == 0, kc == 15, [bhT, bWt], [bpB])
                            ACT(vb[:], pB[:, :], AF.Copy, [bpB], [bvb])
                            yield
                            if BC < 3:
                                continue
                            h0 = psT_half()
                            for j in range(4):
                                TR(psT[:, h0 + j * 128:h0 + (j + 1) * 128], cT[:, j, lc:lc + 128], identb[:], [bcT, bidb], [bpsT])
                            xs3 = psT[:, h0:h0 + 512].rearrange("p (r q) -> p r q", q=64)

                            def b8(row):
                                return sm[:, row, :].unsqueeze(2).broadcast_to([128, 8, 64])
                            TT(xdd[:].rearrange("p (r q) -> p r q", q=64), xs3, b8(8), ALU.mult, [bpsT, bsm], [bxdd])
                            if main:
                                TT(xdt[:].rearrange("p (r q) -> p r q", q=64), xs3, b8(1), ALU.mult, [bpsT, bsm], [bxdt])
                                dskb = small[:, 2, g * 8:(g + 1) * 8].unsqueeze(2).broadcast_to([128, 8, 64])
                                TT(xsk[:].rearrange("p (r q) -> p r q", q=64), xs3, dskb, ALU.mult, [bpsT, bsmall], [bxsk])
                            if BC < 4:
                                continue
                            h1 = psT_half()
                            TR(psT[:, h1:h1 + 128], cT[:, 4, lc:lc + 128], identb[:], [bcT, bidb], [bpsT])
                            for hf in range(2):
                                TR(psT[:, h1 + 128 + hf * 128:h1 + 256 + hf * 128], qkT[:, 2 + hf, lc:lc + 128], identb[:],
                                   [bqkT, bidb], [bpsT])
                            ACT(Btok[:], psT[:, h1:h1 + 128], AF.Copy, [bpsT], [bBtok, bpsT])
                            TS(kd[:], psT[:, h1 + 128:h1 + 384], C[:, oK + g:oK + g + 1], None, ALU.mult, None, [bpsT, bC], [bkd])
                            yield
                            if main:
                                MM(psM[:, 128:256], cT[:, 4, lc:lc + 128], cT[:, 5, lc:lc + 128], True, True, [bcT], [bpsM])
                                TT(qkm[:], psM[:, 128:256], cst("causal"), ALU.mult, [bpsM, bC], [bqkm])
                                for hh in range(2):
                                    TT(Ld[:], cst("strict").unsqueeze(1).broadcast_to([128, 4, 128]),
                                       sm[:, 2, hh * 4:hh * 4 + 4].unsqueeze(2).broadcast_to([128, 4, 128]), ALU.mult,
                                       [bC, bsm], [bLd])
                                    for r in range(4):
                                        MM(p2[:, r * 128:(r + 1) * 128], Ld[:, r, :], cst("causal"), True, True, [bLd, bC], [bp2])
                                    ACT(segT[:], p2[:, :], AF.Exp, [bp2], [bsegT])
                                    TT(MT[:, hh * 4:hh * 4 + 4, :], segT[:].rearrange("p (r l) -> p r l", l=128),
                                       qkm[:].unsqueeze(1).broadcast_to([128, 4, 128]), ALU.mult, [bsegT, bqkm], [bMT])
                                    yield
                                for r in range(8):
                                    MM(p3[:, r * 64:(r + 1) * 64], MT[:, r, :], xdt[:, r * 64:(r + 1) * 64], True, True,
                                       [bMT, bxdt], [bp3])
                                MM(p5[:, :], cT[:, 5, lc:lc + 128], STb[:], True, True, [bcT, bSTb], [bp5])
                                for kc in range(16):
                                    MM(pB[:, :], hT[:, kc, tk:tk + 128], Wt[:, kc, 512:1024], kc == 0, kc == 15, [bhT, bWt], [bpB])
                                ACT(zs[:], pB[:, :], AF.Silu, [bpB], [bzs])
                                yield
                                TT(y1[:].rearrange("p (r q) -> p r q", q=64), p5[:, :].rearrange("p (r q) -> p r q", q=64),
                                   b8(6), ALU.mult, [bp5, bsm], [by1])
                                TT(y1[:], y1[:], p3[:, :], ALU.add, [by1, bp3], [by1])
                                TT(y1[:], y1[:], xsk[:], ALU.add, [by1, bxsk], [by1])
                                TT(y1[:], y1[:], zs[:], ALU.mult, [by1, bzs], [by1])
                                rms_to(ybar[:], bybar, y1[:], by1, normw[:], bnw)
                                h0 = psT_half()
                                for j in range(4):
                                    TR(psT[:, h0 + j * 128:h0 + (j + 1) * 128], ybar[:, j * 128:(j + 1) * 128], identb[:],
                                       [bybar, bidb], [bpsT])
                                ACT(stage[:], psT[:, h0:h0 + 512].rearrange("p (a b) -> p a b", b=128), AF.Copy, [bpsT], [bstage])
                                LD(YTd[g * 4:g * 4 + 4, :, tk:tk + 128].rearrange("j p t -> p j t"), stage[:], [], bstage,
                                   R=[bstage])
                                yield
                            if BC < 5:
                                continue
                            MM(p5[:, :], Btok[:], xdd[:], True, True, [bBtok, bxdd], [bp5])
                            TT(tmpS[:].rearrange("p (r q) -> p r q", q=64), ST[:].rearrange("p (r q) -> p r q", q=64),
                               b8(7), ALU.mult, [bST, bsm], [btmpS])
                            TT(ST[:], tmpS[:], p5[:, :], ALU.add, [btmpS, bp5], [bST])
                            ACT(STb[:], ST[:], AF.Copy, [bST], [bSTb])
                            yield
                            if main:
                                for hf in range(2):
                                    MM(psM[:, 128:256], qkT[:, 2 + hf, lc:lc + 128], qkT[:, hf, lc:lc + 128], hf == 0, hf == 1,
                                       [bqkT], [bpsM])
                                TT(MTr[:], psM[:, 128:256], cg[:, 0:128], ALU.mult, [bpsM, bcg], [bMTr])
                                TT(qs[:], qkT[:, 0:2, lc:lc + 128], cg[:, 128:256].unsqueeze(1).broadcast_to([128, 2, 128]),
                                   ALU.mult, [bqkT, bcg], [bqs])
                                MM(p3[:, :], MTr[:], vb[:], True, False, [bMTr, bvb], [bp3])
                                for hf in range(2):
                                    MM(p3[:, :], qs[:, hf, :], RSTb[:, hf, :], False, hf == 1, [bqs, bRSTb], [bp3])
                                for kc in range(16):
                                    MM(pB[:, :], hT[:, kc, tk:tk + 128], Wt[:, kc, 1024:1536], kc == 0, kc == 15, [bhT, bWt], [bpB])
                                ACT(zs[:], pB[:, :], AF.Silu, [bpB], [bzs])
                                yield
                                rms_to(ybar[:], bybar, p3[:, :], bp3, zs[:], bzs)
                                h0 = psT_half()
                                for j in range(4):
                                    TR(psT[:, h0 + j * 128:h0 + (j + 1) * 128], ybar[:, j * 128:(j + 1) * 128], identb[:],
                                       [bybar, bidb], [bpsT])
                                ACT(stage[:], psT[:, h0:h0 + 512].rearrange("p (a b) -> p a b", b=128), AF.Copy, [bpsT], [bstage])
                                LD(OTd[g * 4:g * 4 + 4, :, tk:tk + 128].rearrange("j p t -> p j t"), stage[:], [], bstage,
                                   R=[bstage])
                                yield
                            if BC < 6:
                                continue
                            for hf in range(2):
                                pq, bpq = (p5, bp5) if hf == 0 else (p2, bp2)
                                MM(pq[:, :], kd[:, hf * 128:(hf + 1) * 128], vb[:], True, True, [bkd, bvb], [bpq])
                                STT(RST[:, hf, :], RST[:, hf, :], gQ, pq[:, :], ALU.mult, ALU.add, [bRST, bpq], [bRST])
                            ACT(RSTb[:], RST[:], AF.Copy, [bRST], [bRSTb])
                            yield
                        b_done.add(idx)
                        yield
                    if samp:
                        while g not in as_done:
                            BLK[0] = True
                            yield
                        yield from sample_B(g)
                        bs_done.add(g)
                    if not main:
                        TS(ST[:], ST[:], flag[:, 0:1], None, ALU.mult, None, [bST, bflag], [bST])
                        TS(RST[:], RST[:], flag[:, 0:1], None, ALU.mult, None, [bRST, bflag], [bRST])
                        LD(STs_d[g], ST[:], [], bST, R=[bST])
                        LD(STr_d[g].rearrange("p (a b) -> p a b", b=512), RST[:], [], bRST, R=[bRST])
                    else:
                        stf, bstf = R0buf[0]
                        for j in range(4):
                            TR(p2[:, j * 128:(j + 1) * 128], ST[:, j * 128:(j + 1) * 128], identf, [bST, bC], [bp2])
                        ACT(stf[:, :, 0:128], p2[:, :].rearrange("p (a b) -> p a b", b=128), AF.Copy, [bp2], [bstf])
                        LD(ssmp[g * 512:(g + 1) * 512, :].rearrange("(j p) n -> p j n", p=128), stf[:, :, 0:128], [], bstf,
                           R=[bstf])
                        stf, bstf = R0buf[1]
                        for hf in range(2):
                            for j in range(4):
                                TR(p2[:, j * 128:(j + 1) * 128], RST[:, hf, j * 128:(j + 1) * 128], identf, [bRST, bC], [bp2])
                            ACT(stf[:, :, hf * 128:(hf + 1) * 128], p2[:, :].rearrange("p (a b) -> p a b", b=128), AF.Copy,
                                [bp2], [bstf])
                        LD(retp[g * 512:(g + 1) * 512, :].rearrange("(j p) k -> p j k", p=128), stf[:], [], bstf, R=[bstf])
                    yield

            def sample_B(g):
                seqm = cst("seqmask", 64)
                S = slice(0, 64)
                decay_scalars(g, hTs, bhTs, 0, S, cst("incl_s", 64), cst("same_s", 64))
                CP(segT[S, :].rearrange("p (r q) -> p r q", q=64), sm[S, 2, :].unsqueeze(2).broadcast_to([64, 8, 64]),
                   [bsm], [bsegT])
                for j in range(4):
                    MM(psM[:, 32 + j * 16:32 + (j + 1) * 16], segT[S, j * 128:(j + 1) * 128], seqm, True, True,
                       [bsegT, bC], [bpsM])
                ACT(cdcol[:], psM[:, 32:96].rearrange("p (j s) -> p j s", s=16), AF.Exp, [bpsM], [bcdcol])
                yield

                def b8s(row):
                    return sm[S, row, :].unsqueeze(2).broadcast_to([64, 8, 64])
                h0 = psT_half()
                for j in range(4):
                    TR(psT[S, h0 + j * 128:h0 + (j + 1) * 128], cTs[:, j, :], identb[:], [bcTs, bidb], [bpsT])
                xs3 = psT[S, h0:h0 + 512].rearrange("p (r q) -> p r q", q=64)
                TT(xdd[S, :].rearrange("p (r q) -> p r q", q=64), xs3, b8s(8), ALU.mult, [bpsT, bsm], [bxdd])
                TT(xdt[S, :].rearrange("p (r q) -> p r q", q=64), xs3, b8s(1), ALU.mult, [bpsT, bsm], [bxdt])
                dskb = small[S, 2, g * 8:(g + 1) * 8].unsqueeze(2).broadcast_to([64, 8, 64])
                TT(xsk[S, :].rearrange("p (r q) -> p r q", q=64), xs3, dskb, ALU.mult, [bpsT, bsmall], [bxsk])
                h1 = psT_half()
                TR(psT[S, h1:h1 + 128], cTs[:, 4, :], identb[:], [bcTs, bidb], [bpsT])
                for hf in range(2):
                    TR(psT[S, h1 + 128 + hf * 128:h1 + 256 + hf * 128], qkTs[:, 2 + hf, :], identb[:], [bqkTs, bidb], [bpsT])
                ACT(Btok[S, :], psT[S, h1:h1 + 128], AF.Copy, [bpsT], [bBtok, bpsT])
                oK = CONST_LAYOUT["kds_s"][0]
                TS(kd[S, :], psT[S, h1 + 128:h1 + 384], C[S, oK + g:oK + g + 1], None, ALU.mult, None, [bpsT, bC], [bkd])
                yield
                MM(psM[S, 128:192], cTs[:, 4, :], cTs[:, 5, :], True, True, [bcTs], [bpsM])
                TT(qkm[S, 0:64], psM[S, 128:192], cst("causal_s", 64), ALU.mult, [bpsM, bC], [bqkm])
                for hh in range(2):
                    TT(Ld[S, :, 0:64], cst("strict_s", 64).unsqueeze(1).broadcast_to([64, 4, 64]),
                       sm[S, 2, hh * 4:hh * 4 + 4].unsqueeze(2).broadcast_to([64, 4, 64]), ALU.mult, [bC, bsm], [bLd])
                    for r in range(4):
                        MM(p2[S, r * 64:(r + 1) * 64], Ld[S, r, 0:64], cst("incl_s", 64), True, True, [bLd, bC], [bp2])
                    ACT(segT[S, 0:256], p2[S, 0:256], AF.Exp, [bp2], [bsegT])
                    TT(MT[S, hh * 4:hh * 4 + 4, 0:64], segT[S, 0:256].rearrange("p (r l) -> p r l", l=64),
                       qkm[S, 0:64].unsqueeze(1).broadcast_to([64, 4, 64]), ALU.mult, [bsegT, bqkm], [bMT])
                for r in range(8):
                    MM(p3[S, r * 64:(r + 1) * 64], MT[S, r, 0:64], xdt[S, r * 64:(r + 1) * 64], True, True, [bMT, bxdt], [bp3])
                yield
                for kc in range(16):
                    MM(pB[S, :], hTs[:, kc, 0:64], Wt[:, kc, 0:512], kc == 0, kc == 15, [bhTs, bWt], [bpB])
                ACT(vb[S, :], pB[S, :], AF.Copy, [bpB], [bvb])
                for hf in range(2):
                    MM(psM[S, 128:192], qkTs[:, 2 + hf, :], qkTs[:, hf, :], hf == 0, hf == 1, [bqkTs], [bpsM])
                TT(MTr[S, 0:64], psM[S, 128:192], cg[S, 256:320], ALU.mult, [bpsM, bcg], [bMTr])
                qs_s = qs[:, :, 0:64]
                TT(qs_s, qkTs[:, 0:2, :], cg[:, 320:384].unsqueeze(1).broadcast_to([128, 2, 64]), ALU.mult, [bqkTs, bcg], [bqs])
                MM(p2[S, :], MTr[S, 0:64], vb[S, :], True, True, [bMTr, bvb], [bp2])
                yield
                g4 = float(_gammas()[g] ** 4)

                def ssd_seqs():
                    for s in range(16):
                        S0n, bS0n = S0buf[s % 2]
                        LD(S0n[:], sssm_d[s, g * 512:(g + 1) * 512, :].rearrange("(j p) n -> p j n", p=128), [bS0n], bS0n)
                        P.op("dve", lambda e, S0n=S0n: e.tensor_copy(out=S0b[:], in_=S0n[:]), [bS0n], [bS0b])
                        h = psT_half()
                        for j in range(4):
                            TR(psT[:, h + j * 128:h + (j + 1) * 128], S0b[:, j, :], identb[:], [bS0b, bidb], [bpsT])
                        ACT(STb[:], psT[:, h:h + 512], AF.Copy, [bpsT], [bSTb])
                        for j in range(4):
                            MM(p5[:, j * 64 + s:(j + 1) * 64:16], STb[:, j * 128:(j + 1) * 128], cTs[:, 5, s:64:16], True, True,
                               [bSTb, bcTs], [bp5])
                        TS(Bm[S, :], Btok[S, :], seqm[:, s:s + 1], None, ALU.mult, None, [bBtok, bC], [bBm])
                        for j in range(4):
                            MM(pB[:, j * 128:(j + 1) * 128], xdd[S, j * 128:(j + 1) * 128], Bm[S, :], True, True,
                               [bxdd, bBm], [bpB])
                        TT(S0n[:], S0n[:], cdcol[:, :, s].unsqueeze(2).broadcast_to([128, 4, 128]), ALU.mult,
                           [bS0n, bcdcol], [bS0n])
                        TT(S0n[:], S0n[:], pB[:, :].rearrange("p (j n) -> p j n", n=128), ALU.add, [bS0n, bpB], [bS0n])
                        LD(ssms[s, g * 512:(g + 1) * 512, :].rearrange("(j p) n -> p j n", p=128), S0n[:], [], bS0o[s % 2],
                           R=[bS0n], eng="pool")
                        yield

                def ret_seqs():
                    for s in range(16):
                        R0n, bR0n = R0buf[s % 2]
                        LD(R0n[:], sret_d[s, g * 512:(g + 1) * 512, :].rearrange("(j p) k -> p j k", p=128), [bR0n], bR0n)
                        P.op("dve", lambda e, R0n=R0n: e.tensor_copy(out=R0b[:], in_=R0n[:]), [bR0n], [bR0b])
                        for hf in range(2):
                            h = psT_half()
                            for j in range(4):
                                TR(psT[:, h + j * 128:h + (j + 1) * 128], R0b[:, j, hf * 128:(hf + 1) * 128], identb[:],
                                   [bR0b, bidb], [bpsT])
                            ACT(RSTb[:, hf, :], psT[:, h:h + 512], AF.Copy, [bpsT], [bRSTb])
                        for j in range(4):
                            for hf in range(2):
                                MM(p5[:, 256 + j * 64 + s:256 + (j + 1) * 64:16], RSTb[:, hf, j * 128:(j + 1) * 128],
                                   qs_s[:, hf, s:64:16], hf == 0, hf == 1, [bRSTb, bqs], [bp5])
                        TS(kdm[S, :], kd[S, :], seqm[:, s:s + 1], None, ALU.mult, None, [bkd, bC], [bkdm])
                        for jj in range(2):
                            for j2 in range(2):
                                j = jj * 2 + j2
                                MM(p3[:, j2 * 256:(j2 + 1) * 256] if False else pB[:, j2 * 256:(j2 + 1) * 256],
                                   vb[S, j * 128:(j + 1) * 128], kdm[S, :], True, True, [bvb, bkdm], [bpB])
                            STT(R0n[:, jj * 2:jj * 2 + 2, :], R0n[:, jj * 2:jj * 2 + 2, :], g4,
                                pB[:, :].rearrange("p (j k) -> p j k", k=256), ALU.mult, ALU.add, [bR0n, bpB], [bR0n])
                        LD(rets[s, g * 512:(g + 1) * 512, :].rearrange("(j p) k -> p j k", p=128), R0n[:], [], bR0o[s % 2],
                           R=[bR0n], eng="pool")
                        yield

                subs = [ssd_seqs(), ret_seqs()]
                while subs:
                    for sg_ in list(subs):
                        try:
                            next(sg_)
                        except StopIteration:
                            subs.remove(sg_)
                    yield
                ACT(yinT, p5[:, 0:256].rearrange("p (j t) -> p j t", t=64), AF.Copy, [bp5], [byinT])
                for j in range(4):
                    TR(pB[S, j * 128:(j + 1) * 128], yinT[:, j, :], identf, [byinT, bC], [bpB])
                TT(y1[S, :].rearrange("p (r q) -> p r q", q=64), pB[S, :].rearrange("p (r q) -> p r q", q=64), b8s(6),
                   ALU.mult, [bpB, bsm], [by1])
                TT(y1[S, :], y1[S, :], p3[S, :], ALU.add, [by1, bp3], [by1])
                TT(y1[S, :], y1[S, :], xsk[S, :], ALU.add, [by1, bxsk], [by1])
                for kc in range(16):
                    MM(pB[S, :], hTs[:, kc, 0:64], Wt[:, kc, 512:1024], kc == 0, kc == 15, [bhTs, bWt], [bpB])
                ACT(zs[S, :], pB[S, :], AF.Silu, [bpB], [bzs])
                TT(y1[S, :], y1[S, :], zs[S, :], ALU.mult, [by1, bzs], [by1])
                rms_to(ybar[S, :], bybar, y1[S, :], by1, normw[S, :], bnw, S)
                h0 = psT_half()
                for j in range(4):
                    TR(psT[:, h0 + j * 64:h0 + (j + 1) * 64], ybar[S, j * 128:(j + 1) * 128], identb[0:64, 0:64],
                       [bybar, bidb], [bpsT])
                stage_s = stage[:, :, 0:64]
                ACT(stage_s, psT[:, h0:h0 + 256].rearrange("p (a b) -> p a b", b=64), AF.Copy, [bpsT], [bstage])
                LD(YTd[g * 4:g * 4 + 4, :, NM:NM + 64].rearrange("j p t -> p j t"), stage_s, [], bstage, R=[bstage])
                yield
                ACT(yinT, p5[:, 256:512].rearrange("p (j t) -> p j t", t=64), AF.Copy, [bp5], [byinT])
                for j in range(4):
                    TR(pB[S, j * 128:(j + 1) * 128], yinT[:, j, :], identf, [byinT, bC], [bpB])
                ACT(y1[S, :], pB[S, :], AF.Copy, [bpB], [by1])
                TT(y1[S, :], y1[S, :], p2[S, :], ALU.add, [by1, bp2], [by1])
                for kc in range(16):
                    MM(pB[S, :], hTs[:, kc, 0:64], Wt[:, kc, 1024:1536], kc == 0, kc == 15, [bhTs, bWt], [bpB])
                ACT(zs[S, :], pB[S, :], AF.Silu, [bpB], [bzs])
                rms_to(ybar[S, :], bybar, y1[S, :], by1, zs[S, :], bzs, S)
                h0 = psT_half()
                for j in range(4):
                    TR(psT[:, h0 + j * 64:h0 + (j + 1) * 64], ybar[S, j * 128:(j + 1) * 128], identb[0:64, 0:64],
                       [bybar, bidb], [bpsT])
                ACT(stage_s, psT[:, h0:h0 + 256].rearrange("p (a b) -> p a b", b=64), AF.Copy, [bpsT], [bstage])
                LD(OTd[g * 4:g * 4 + 4, :, NM:NM + 64].rearrange("j p t -> p j t"), stage_s, [], bstage, R=[bstage])
                yield

            KCUT = os.environ.get('K_CUT', '')
            if KCUT == 'A':
                b_done.update(range(100)); bs_done.update(range(8))
                interleave([gen_A()])
            elif KCUT == 'B':
                a_done.update(range(100)); as_done.update(range(8))
                interleave([gen_B()])
            else:
                interleave([gen_A(), gen_B()])

        def phase2(t0, segs, chunks):
            n = sum(sg_[3] for sg_ in segs)
            NW = 576 if n > 512 else 512
            with contextlib.ExitStack() as s2:
                mergedT, bmg = sbuf(s2, "mergedT", [128, 16, NW], BF16)
                with contextlib.ExitStack() as s2a:
                    YT, bYT = sbuf(s2a, "YT", [128, 32, NW], BF16)
                    OT, bOT = sbuf(s2a, "OT", [128, 32, NW], BF16)
                    Wsb = [sbuf(s2a, f"Wssm_j{i}", [128, 32, 128], BF16) for i in range(2)]
                    Wrb = [sbuf(s2a, f"Wret_j{i}", [128, 32, 128], BF16) for i in range(2)]
                    Wg1b = [sbuf(s2a, f"Wgs_j{i}", [128, 16, 128], BF16) for i in range(2)]
                    Wg2b = [sbuf(s2a, f"Wgr_j{i}", [128, 16, 128], BF16) for i in range(2)]
                    sg, bsg = sbuf(s2a, "sg", [128, NW])
                    m1, bm1 = sbuf(s2a, "m1", [128, NW])
                    m2, bm2 = sbuf(s2a, "m2", [128, NW])
                    LD(YT[:, :, 0:n], YTd[:, :, t0:t0 + n].rearrange("k p t -> p k t"), [bYT], bYT)
                    LD(OT[:, :, 0:n], OTd[:, :, t0:t0 + n].rearrange("k p t -> p k t"), [bOT], bOT)
                    for j in range(16):
                        (Ws, bWs), (Wr, bWr), (Wg1, bWg1), (Wg2, bWg2) = Wsb[j % 2], Wrb[j % 2], Wg1b[j % 2], Wg2b[j % 2]
                        LDC(Ws[:], wssm[:, j * 128:(j + 1) * 128].rearrange("(kb p) n -> p kb n", p=128), [bWs], bWs)
                        LDC(Wr[:], wret[:, j * 128:(j + 1) * 128].rearrange("(kb p) n -> p kb n", p=128), [bWr], bWr)
                        LDC(Wg1[:], w_in[:, OGS + j * 128:OGS + (j + 1) * 128].rearrange("(kb p) n -> p kb n", p=128),
                            [bWg1], bWg1)
                        LDC(Wg2[:], w_in[:, OGR + j * 128:OGR + (j + 1) * 128].rearrange("(kb p) n -> p kb n", p=128),
                            [bWg2], bWg2)
                        for bi_, (Wp, bWp, Xp, bXp, Wg, bWg, pa, pb, mo, bmo) in enumerate((
                                (Ws, bWs, YT, bYT, Wg1, bWg1, ps[0], ps[1], m1, bm1),
                                (Wr, bWr, OT, bOT, Wg2, bWg2, ps[2], ps[3], m2, bm2))):
                            c0 = 0
                            for si, (hT, bhT, hoff, nc_) in enumerate(segs):
                                if si == 0:
                                    pav, bpa, pbv, bpb = pa[0][:, 0:nc_], pa[1], pb[0][:, 0:nc_], pb[1]
                                else:
                                    o_ = bi_ * 128
                                    pav, bpa = ps[4][0][:, o_:o_ + nc_], ps[4][1]
                                    pbv, bpb = ps[4][0][:, o_ + 64:o_ + 64 + nc_], ps[4][1]
                                for kb in range(32):
                                    MM(pav, Wp[:, kb, :], Xp[:, kb, c0:c0 + nc_], kb == 0, kb == 31, [bWp, bXp], [bpa])
                                for kc in range(16):
                                    MM(pbv, Wg[:, kc, :], hT[:, kc, hoff:hoff + nc_], kc == 0, kc == 15, [bWg, bhT], [bpb])
                                ACT(sg[:, c0:c0 + nc_], pbv, AF.Sigmoid, [bpb], [bsg])
                                TT(mo[:, c0:c0 + nc_], sg[:, c0:c0 + nc_], pav, ALU.mult, [bsg, bpa], [bmo])
                                c0 += nc_
                        TT(mergedT[:, j, 0:n], m1[:, 0:n], m2[:, 0:n], ALU.add, [bm1, bm2], [bmg])
                P.barrier()
                with contextlib.ExitStack() as s2b:
                    Wo, bWo = sbuf(s2b, "Wo", [128, 16, D], BF16)
                    xtb = [sbuf(s2b, f"p2_xt{i}", [128, D]) for i in range(2)]
                    yob = [sbuf(s2b, f"p2_yo{i}", [128, D]) for i in range(2)]
                    wpost, bwpost = sbuf(s2b, "p2_wpost", [128, D])
                    ss4, bss4 = sbuf(s2b, "ss4", [128, 8])
                    junk, bjunk = sbuf(s2b, "junk", [128, 512])
                    LD(wpost[:], wpost_d[:, :], [bwpost], bwpost)
                    for nb in range(4):
                        LDC(Wo[:, :, nb * 512:(nb + 1) * 512],
                            wout[:, nb * 512:(nb + 1) * 512].rearrange("(kb p) n -> p kb n", p=128), [bWo], bWo)
                    for ci, (c0, rows, xsrc, xoff, ydst, yoff) in enumerate(chunks):
                        (xt, bxt), (yo, byo) = xtb[ci % 2], yob[ci % 2]
                        LD(xt[0:rows, :], xsrc[xoff:xoff + rows, :], [bxt], bxt)
                        ZERO(ss4[:, 0:4], bss4)
                        for nb in range(4):
                            pt, bpt = ps[nb]
                            for j in range(16):
                                MM(pt[0:rows, :], mergedT[:, j, c0:c0 + rows], Wo[:, j, nb * 512:(nb + 1) * 512], j == 0,
                                   j == 15, [bmg, bWo], [bpt])
                            ACT(junk[0:rows, :], pt[0:rows, :], AF.Square, [bpt], [bjunk, bss4], accum_out=ss4[0:rows, nb:nb + 1])
                        P.op("dve", lambda e, rows=rows: e.reduce_sum(out=ss4[0:rows, 4:5], in_=ss4[0:rows, 0:4], axis=AX.X),
                             [bss4], [bss4])
                        TS(ss4[0:rows, 5:6], ss4[0:rows, 4:5], 1.0 / D, EPS, ALU.mult, ALU.add, [bss4], [bss4])
                        ACT(ss4[0:rows, 5:6], ss4[0:rows, 5:6], AF.Sqrt, [bss4], [bss4])
                        P.op("dve", lambda e, rows=rows: e.reciprocal(out=ss4[0:rows, 5:6], in_=ss4[0:rows, 5:6]), [bss4], [bss4])
                        for nb in range(4):
                            pt, bpt = ps[nb]
                            STT(yo[0:rows, nb * 512:(nb + 1) * 512], pt[0:rows, :], ss4[0:rows, 5:6],
                                wpost[0:rows, nb * 512:(nb + 1) * 512], ALU.mult, ALU.mult, [bpt, bss4, bwpost], [byo])
                        TT(yo[0:rows, :], yo[0:rows, :], xt[0:rows, :], ALU.add, [byo, bxt], [byo])
                        LD(ydst[yoff:yoff + rows, :], yo[0:rows, :], [], byo, R=[byo])
                P.barrier()

        KSTOP = int(os.environ.get("K_STOP", "9"))
        with contextlib.ExitStack() as sA:
            hTp, bhTp = sbuf(sA, "hTp", [128, 16, NP_], BF16)
            with contextlib.ExitStack() as s0:
                phase0(s0, xp, NP_, hTp, bhTp)
            P.barrier()
            if KSTOP >= 2:
                with contextlib.ExitStack() as s1:
                    phase1(s1, hTp, bhTp, NP_, cosP_d, sinP_d, "pre")
                P.barrier()
        if KSTOP >= 3:
            with contextlib.ExitStack() as s0:
                phase0(s0, xm, NM, hTm, bhTm)
                phase0(s0, xs_, NS, hTs, bhTs)
            P.barrier()
        if KSTOP >= 4:
            with contextlib.ExitStack() as s1:
                phase1(s1, hTm, bhTm, NM, cosM_d, sinM_d, "main")
            for cb in range(48):
                LDNC(convp[:, cb * 128:(cb + 1) * 128].rearrange("t p -> p t"), convout[:, cb, :], [], bco, R=[bco])
            P.barrier()
        if KSTOP >= 5:
            phase2(0, [(hTm, bhTm, 0, 512)], [(c * 128, 128, xm, c * 128, ym, c * 128) for c in range(4)])
            ch2 = [(c * 128, 128, xm, 512 + c * 128, ym, 512 + c * 128) for c in range(4)]
            if WITH_SAMPLE:
                phase2(512, [(hTm, bhTm, 512, 512), (hTs, bhTs, 0, NS)], ch2 + [(512, NS, xs_, 0, ys, 0)])
            else:
                phase2(512, [(hTm, bhTm, 512, 512)], ch2)
        P.barrier()
        blk = st0.enter_context(nc.Block())
        P.emit(blk)
    return nc


def kernel(x_prompt, x_sample, cache_conv, state_ssm, state_ret, w_pre, w_in, conv_w, conv_b, dt_bias, a_log,
           d_skip, ssm_norm_w, w_proj_ssm, w_proj_ret, w_out, w_post):
    f = lambda a: np.ascontiguousarray(np.asarray(a, dtype=np.float32))
    x_prompt, x_sample = f(x_prompt), f(x_sample)
    consts, cgrp = make_consts()
    nc = build_program()
    bc = lambda v: np.ascontiguousarray(np.broadcast_to(f(v).reshape(1, -1), (128, f(v).size)))
    shared = {
        "consts": consts, "cgrp": cgrp, "w_in": f(w_in)[0], "wssm": f(w_proj_ssm)[0], "wret": f(w_proj_ret)[0], "wout": f(w_out)[0],
        "wpre_b": bc(w_pre), "wpost_b": bc(w_post), "normw_b": bc(ssm_norm_w),
        "convw": np.ascontiguousarray(f(conv_w)[0].reshape(4, 48, 128).transpose(2, 1, 0)),
        "convb": np.ascontiguousarray(f(conv_b)[0].reshape(48, 128).T),
        "dtb_b": bc(dt_bias), "alog_b": bc(a_log), "dsk_b": bc(d_skip),
    }
    cS, sS = rope_tables(16384 + (np.arange(64) // 16))
    in_maps = []
    for c in range(8):
        b, hf = c // 2, c % 2
        cM, sM = rope_tables(hf * 1024 + np.arange(1024))
        cP, sP = rope_tables(np.arange(1024))
        sl = slice(16 * c, 16 * c + 16)
        m = dict(shared)
        m.update({
            "xm": np.ascontiguousarray(x_prompt[b, hf * 1024:(hf + 1) * 1024]),
            "xp": np.ascontiguousarray(x_prompt[b, 0:1024]),
            "xs": np.ascontiguousarray(x_sample[sl].transpose(1, 0, 2).reshape(64, D)),
            "flag": np.full((128, 1), float(hf), np.float32),
            "cosM": cM, "sinM": sM, "cosP": cP, "sinP": sP, "cosS": cS, "sinS": sS,
            "cconv": np.ascontiguousarray(f(cache_conv)[0, sl].transpose(1, 0, 2).reshape(48, 6144)),
            "sssm": np.ascontiguousarray(f(state_ssm)[0, sl].reshape(16, 4096, 128)),
            "sret": np.ascontiguousarray(f(state_ret)[0, sl].reshape(16, 4096, 256)),
        })
        in_maps.append(m)
    res = run_bass_kernel_spmd(nc, in_maps, core_ids=list(range(8)))
    R = res.results
    y_prompt = np.zeros((4, 2048, D), np.float32)
    y_sample = np.zeros((128, 4, D), np.float32)
    conv_p = np.zeros((1, 4, 3, 6144), np.float32)
    ssm_p = np.zeros((1, 4, 64, 64, 128), np.float32)
    ret_p = np.zeros((1, 4, 8, 512, 256), np.float32)
    conv_s = np.zeros((1, 128, 3, 6144), np.float32)
    ssm_s = np.zeros((1, 128, 64, 64, 128), np.float32)
    ret_s = np.zeros((1, 128, 8, 512, 256), np.float32)
    for c in range(8):
        b, hf = c // 2, c % 2
        r = R[c]
        y_prompt[b, hf * 1024:(hf + 1) * 1024] = r["ym"]
        y_sample[16 * c:16 * c + 16] = r["ys"].reshape(4, 16, D).transpose(1, 0, 2)
        if hf == 1:
            conv_p[0, b] = r["convp"]
            ssm_p[0, b] = r["ssmp"].reshape(64, 64, 128)
            ret_p[0, b] = r["retp"].reshape(8, 512, 256)
        conv_s[0, 16 * c:16 * c + 16] = r["convs"]
        ssm_s[0, 16 * c:16 * c + 16] = r["ssms"].reshape(16, 64, 64, 128)
        ret_s[0, 16 * c:16 * c + 16] = r["rets"].reshape(16, 8, 512, 256)
    return (y_prompt, y_sample, conv_p, ssm_p, ret_p, conv_s, ssm_s, ret_s)
```
